# Optimizing a Trainium2 kernel written in Bass

```python
import jax, jax.numpy as jnp
from jax import lax
import numpy as np

D_MODEL = 1024
BATCH = 4
SEQ = 4096
DEPTH = 1

PLE_DIM = 256
EPS = 1e-6
A_HEADS = 4
A_DK = 128
A_DV = 128
A_CONV = 4
A_CHUNK = 64
A_WIDTH = A_HEADS * A_DV
A_CONV_CH = 2 * A_HEADS * A_DK + A_WIDTH
B_HEADS = 8
B_KV_HEADS = 2
B_HD = 64
B_WIDTH = B_HEADS * B_HD
IDX_HEADS = 8
IDX_DIM = 128
TOPK_MAX = 256
Q_BLOCK = 128

MIX_WIDTH = A_WIDTH + B_WIDTH
IN_SIZES = (
    A_HEADS * A_DK,
    A_HEADS * A_DK,
    A_WIDTH,
    A_WIDTH,
    A_HEADS,
    A_HEADS,
    B_WIDTH,
    B_KV_HEADS * B_HD,
    B_KV_HEADS * B_HD,
    B_WIDTH,
    IDX_HEADS * IDX_DIM,
    IDX_DIM,
    IDX_HEADS,
)
IN_WIDTH = sum(IN_SIZES)
IN_OFFSETS = tuple(int(s) for s in np.cumsum(IN_SIZES)[:-1])

kernel_name = "hybrid_gdn_dsa_parallel_heads"


def rms_norm(x, gain):
    xf = x.astype(jnp.float32)
    y = xf * lax.rsqrt(jnp.mean(xf * xf, axis=-1, keepdims=True) + EPS)
    return (y * gain.astype(jnp.float32)).astype(x.dtype)


def l2_norm(x):
    return x * lax.rsqrt(jnp.sum(x * x, axis=-1, keepdims=True) + EPS)


def causal_dwconv(x, w):
    c = x.shape[-1]
    return lax.conv_general_dilated(
        x, w[:, None, :].astype(x.dtype), window_strides=(1,), padding=[(A_CONV - 1, 0)],
        dimension_numbers=('NWC', 'WIO', 'NWC'), feature_group_count=c)


def gated_delta_rule_chunked(q, k, v, g, beta):
    bn, seq_len, nh, dk = q.shape
    dv = v.shape[-1]
    c = A_CHUNK
    nc = seq_len // c

    def chunks(t):
        t = jnp.moveaxis(t, 2, 1)
        return t.reshape(bn, nh, nc, c, *t.shape[3:])

    q, k, v, g, beta = chunks(q), chunks(k), chunks(v), chunks(g), chunks(beta)
    g_cum = jnp.cumsum(g, axis=-1)
    tril = jnp.tril(jnp.ones((c, c), dtype=bool))
    strict = jnp.tril(jnp.ones((c, c), dtype=bool), -1)
    diff = g_cum[..., :, None] - g_cum[..., None, :]
    decay = jnp.where(tril, jnp.exp(jnp.where(tril, diff, 0.0)), 0.0)
    k_beta = k * beta[..., None]
    v_beta = v * beta[..., None]
    eye = jnp.eye(c, dtype=jnp.float32)
    kkt = jnp.einsum('bhncd,bhnsd->bhncs', k_beta, k) * decay
    a_mat = eye + jnp.where(strict, kkt, 0.0)
    t_mat = lax.linalg.triangular_solve(a_mat, jnp.broadcast_to(eye, a_mat.shape),
                                        left_side=True, lower=True, unit_diagonal=True)
    value = jnp.einsum('bhncs,bhnse->bhnce', t_mat, v_beta)
    k_cumdecay = jnp.einsum('bhncs,bhnsd->bhncd', t_mat, k_beta * jnp.exp(g_cum)[..., None])
    attn_intra = jnp.einsum('bhncd,bhnsd->bhncs', q, k) * decay
    q_decay = q * jnp.exp(g_cum)[..., None]
    k_decay = k * jnp.exp(g_cum[..., -1:] - g_cum)[..., None]
    g_last = jnp.exp(g_cum[..., -1])

    def step(state, inp):
        qd, kd, val, kcd, att, gl = inp
        v_new = val - jnp.einsum('bhcd,bhde->bhce', kcd, state)
        o = jnp.einsum('bhcd,bhde->bhce', qd, state) + jnp.einsum('bhcs,bhse->bhce', att, v_new)
        state = state * gl[..., None, None] + jnp.einsum('bhcd,bhce->bhde', kd, v_new)
        return state, o

    xs = tuple(jnp.moveaxis(t, 2, 0) for t in (q_decay, k_decay, value, k_cumdecay, attn_intra, g_last))
    s0 = jnp.zeros((bn, nh, dk, dv), jnp.float32)
    _, o = lax.scan(step, s0, xs)
    return o.transpose(1, 0, 3, 2, 4).reshape(bn, seq_len, nh, dv)


def dsa_sparse_attention(q, k, v, iq, ik, iw):
    bn, seq_len = q.shape[:2]
    n_sel = min(TOPK_MAX, seq_len // 4)
    n_blocks = seq_len // Q_BLOCK
    rep = B_HEADS // B_KV_HEADS
    qg = q.reshape(bn, seq_len, B_KV_HEADS, rep, B_HD)
    key_pos = jnp.arange(seq_len)
    scale = B_HD ** -0.5
    gather = jax.vmap(lambda tb, ib: tb[ib])

    def block(i):
        start = i * Q_BLOCK
        q_b = lax.dynamic_slice_in_dim(qg, start, Q_BLOCK, axis=1)
        iq_b = lax.dynamic_slice_in_dim(iq, start, Q_BLOCK, axis=1)
        iw_b = lax.dynamic_slice_in_dim(iw, start, Q_BLOCK, axis=1)
        q_pos = start + jnp.arange(Q_BLOCK)
        logits = jnp.einsum('bqhd,bsd->bqhs', iq_b, ik)
        score = jnp.einsum('bqh,bqhs->bqs', iw_b.astype(jnp.float32),
                           jax.nn.relu(logits.astype(jnp.float32)))
        causal = key_pos[None, :] <= q_pos[:, None]
        score = jnp.where(causal[None], score, -jnp.inf)
        _, idx = lax.top_k(score, n_sel)
        k_sel = gather(k, idx)
        v_sel = gather(v, idx)
        s = jnp.einsum('bqgrd,bqkgd->bqgrk', q_b, k_sel).astype(jnp.float32) * scale
        valid = idx <= q_pos[None, :, None]
        s = jnp.where(valid[:, :, None, None, :], s, -jnp.inf)
        prob = jax.nn.softmax(s, axis=-1).astype(v.dtype)
        o = jnp.einsum('bqgrk,bqkgd->bqgrd', prob, v_sel)
        return o.reshape(bn, Q_BLOCK, B_WIDTH)

    out = lax.map(block, jnp.arange(n_blocks))
    return out.transpose(1, 0, 2, 3).reshape(bn, seq_len, B_WIDTH)


def setup_inputs(seed: int = 0) -> dict:
    key = jax.random.key(seed)
    ks = jax.random.split(key, 16)
    f32 = jnp.float32
    x = jax.random.normal(ks[0], (BATCH, SEQ, D_MODEL), f32)
    p = jax.random.normal(ks[1], (DEPTH, BATCH, SEQ, PLE_DIM), f32)
    attn_norm_w = 1.0 + 0.02 * jax.random.normal(ks[2], (DEPTH, D_MODEL), f32)
    w_in = jax.random.normal(ks[3], (DEPTH, D_MODEL, IN_WIDTH), f32) * D_MODEL ** -0.5
    conv_w = jax.random.normal(ks[4], (DEPTH, A_CONV, A_CONV_CH), f32) * A_CONV ** -0.5
    a_log = jnp.log(jax.random.uniform(ks[5], (DEPTH, A_HEADS), f32, minval=1.0, maxval=16.0))
    dt = jnp.exp(jax.random.uniform(ks[6], (DEPTH, A_HEADS), f32,
                                    minval=float(np.log(1e-3)), maxval=float(np.log(1e-1))))
    dt_bias = dt + jnp.log(-jnp.expm1(-dt))
    a_out_norm_w = 1.0 + 0.02 * jax.random.normal(ks[7], (DEPTH, A_DV), f32)
    b_q_norm_w = 1.0 + 0.02 * jax.random.normal(ks[8], (DEPTH, B_HD), f32)
    b_k_norm_w = 1.0 + 0.02 * jax.random.normal(ks[9], (DEPTH, B_HD), f32)
    w_out = jax.random.normal(ks[10], (DEPTH, MIX_WIDTH, D_MODEL), f32) * MIX_WIDTH ** -0.5
    w_ple = jax.random.normal(ks[11], (DEPTH, PLE_DIM, D_MODEL), f32) * PLE_DIM ** -0.5
    ple_gate_norm_w = 1.0 + 0.02 * jax.random.normal(ks[12], (DEPTH, D_MODEL), f32)
    w_ple_gate = jax.random.normal(ks[13], (DEPTH, D_MODEL, D_MODEL), f32) * D_MODEL ** -0.5
    b_ple_gate = 0.01 * jax.random.normal(ks[14], (DEPTH, D_MODEL), f32)
    return {"x": x, "p": p, "attn_norm_w": attn_norm_w, "w_in": w_in, "conv_w": conv_w,
            "a_log": a_log, "dt_bias": dt_bias, "a_out_norm_w": a_out_norm_w,
            "b_q_norm_w": b_q_norm_w, "b_k_norm_w": b_k_norm_w, "w_out": w_out,
            "w_ple": w_ple, "ple_gate_norm_w": ple_gate_norm_w, "w_ple_gate": w_ple_gate,
            "b_ple_gate": b_ple_gate}


def reference(x, p, attn_norm_w, w_in, conv_w, a_log, dt_bias, a_out_norm_w, b_q_norm_w,
              b_k_norm_w, w_out, w_ple, ple_gate_norm_w, w_ple_gate, b_ple_gate):
    bn, seq_len, _ = x.shape
    f32 = jnp.float32
    for i in range(DEPTH):
        h = rms_norm(x, attn_norm_w[i])
        proj = h @ w_in[i]
        (a_q, a_k, a_v, a_z, a_b, a_a, b_q, b_k, b_v, b_z,
         i_q, i_k, i_w) = jnp.split(proj, IN_OFFSETS, axis=-1)

        qkv = jax.nn.silu(causal_dwconv(jnp.concatenate([a_q, a_k, a_v], axis=-1), conv_w[i]))
        aq, ak, av = jnp.split(qkv, (A_HEADS * A_DK, 2 * A_HEADS * A_DK), axis=-1)
        aq = l2_norm(aq.astype(f32).reshape(bn, seq_len, A_HEADS, A_DK)) * (A_DK ** -0.5)
        ak = l2_norm(ak.astype(f32).reshape(bn, seq_len, A_HEADS, A_DK))
        av = av.astype(f32).reshape(bn, seq_len, A_HEADS, A_DV)
        beta = jax.nn.sigmoid(a_b.astype(f32))
        g = -jnp.exp(a_log[i].astype(f32)) * jax.nn.softplus(a_a.astype(f32) + dt_bias[i].astype(f32))
        o_a = gated_delta_rule_chunked(aq, ak, av, g, beta)
        o_a = rms_norm(o_a, a_out_norm_w[i]).reshape(bn, seq_len, A_WIDTH).astype(x.dtype)
        o_a = o_a * jax.nn.silu(a_z)

        bq = rms_norm(b_q.reshape(bn, seq_len, B_HEADS, B_HD), b_q_norm_w[i])
        bk = rms_norm(b_k.reshape(bn, seq_len, B_KV_HEADS, B_HD), b_k_norm_w[i])
        bv = b_v.reshape(bn, seq_len, B_KV_HEADS, B_HD)
        iq = i_q.reshape(bn, seq_len, IDX_HEADS, IDX_DIM)
        iw = i_w * (IDX_HEADS ** -0.5 * IDX_DIM ** -0.5)
        o_b = dsa_sparse_attention(bq, bk, bv, iq, i_k, iw) * jax.nn.silu(b_z)

        x = x + jnp.concatenate([o_a, o_b], axis=-1) @ w_out[i]

        gate = jax.nn.sigmoid(rms_norm(x, ple_gate_norm_w[i]) @ w_ple_gate[i] + b_ple_gate[i])
        x = x + (p[i] @ w_ple[i]) * gate
    return x
```

```python
import numpy as np
import ml_dtypes
from contextlib import ExitStack
import concourse.bass as bass
import concourse.mybir as mybir
from concourse.bass_utils import run_bass_kernel_spmd

F32 = mybir.dt.float32
BF16 = mybir.dt.bfloat16
AF = mybir.ActivationFunctionType
ALU = mybir.AluOpType
AX = mybir.AxisListType

D_MODEL = 1024
SEQ = 4096
NT = SEQ // 128
NOWN = 16
EPS = 1e-6
IN_WIDTH = 4496
NEGM = -30000.0
BIGNEG = -1.0e30

ENGS = ['pe', 'dve', 'act', 'pool', 'sp']
SAME_ENGINE_SYNC = ('dve', 'act', 'pool')


class Buf:
    __slots__ = ('name', 'last_w', 'readers', 'excl')

    def __init__(self, name=''):
        self.name = name
        self.excl = False
        self.last_w = []
        self.readers = {}


class TB:
    def __init__(self, t, name=''):
        self.t = t
        self.b = Buf(name)
        self.subs = {}

    def sub(self, key):
        if key not in self.subs:
            self.subs[key] = Buf()
        return self.subs[key]

    def __getitem__(self, idx):
        return self.t[idx]


class K:
    def __init__(self, nc, n_dma_sems=(('sp', 24), ('act', 8), ('pool', 8))):
        self.nc = nc
        self.eng = dict(pe=nc.tensor, dve=nc.vector, act=nc.scalar, pool=nc.gpsimd, sp=nc.sync)
        self.sem = {e: nc.alloc_semaphore(name='c_' + e) for e in ENGS}
        self.cnt = {e: 0 for e in ENGS}
        self.known = {e: {} for e in ENGS}
        self.dsem = {}
        self.dnext = {}
        for e, n in n_dma_sems:
            self.dsem[e] = [[nc.alloc_semaphore(name='d_%s_%d' % (e, i)), 0] for i in range(n)]
            self.dnext[e] = 0
        self.ninst = 0

    def _wait(self, e, tok):
        if tok is None:
            return
        kind, key, val = tok
        if kind == 'c' and key == e and e not in SAME_ENGINE_SYNC:
            return
        kk = (kind, key) if kind == 'c' else (kind, id(key))
        if self.known[e].get(kk, 0) >= val:
            return
        semh = self.sem[key] if kind == 'c' else key[0]
        self.eng[e].wait_ge(semh, val)
        self.known[e][kk] = val

    def _deps(self, e, reads, writes, is_dma=False):
        for b in reads:
            for t in b.last_w:
                self._wait(e, t)
        for b in writes:
            if not (is_dma and not b.readers and all(t[0] == 'd' for t in b.last_w)):
                for t in b.last_w:
                    self._wait(e, t)
            for t in list(b.readers.values()):
                self._wait(e, t)

    def _commit(self, tok, reads, writes):
        for b in writes:
            if tok[0] == 'd' and not b.readers and b.last_w and all(t[0] == 'd' for t in b.last_w):
                b.last_w = b.last_w + [tok]
            else:
                b.last_w = [tok]
            b.readers = {}
        kind, key, val = tok
        rk = (kind, key if kind == 'c' else id(key))
        for b in reads:
            if b in writes:
                continue
            b.readers[rk] = tok

    def op(self, e, fn, reads=(), writes=()):
        if any(b.excl for b in reads):
            writes = list(writes) + [b for b in reads if b.excl and b not in writes]
            reads = [b for b in reads if not b.excl]
        self._deps(e, reads, writes)
        inst = fn()
        self.cnt[e] += 1
        inst.then_inc(self.sem[e], 1)
        self._commit(('c', e, self.cnt[e]), reads, writes)
        self.ninst += 1
        return inst

    def dma(self, e, out, in_, reads=(), writes=(), **kw):
        self._deps(e, reads, writes, is_dma=True)
        pool = self.dsem[e]
        i = self.dnext[e]
        self.dnext[e] = (i + 1) % len(pool)
        ent = pool[i]
        if ent[1] > 0:
            self._wait(e, ('d', ent, ent[1]))
        inst = self.eng[e].dma_start(out=out, in_=in_, **kw)
        ent[1] += 16
        inst.then_inc(ent[0], 16)
        self._commit(('d', ent, ent[1]), reads, writes)
        self.ninst += 1
        return inst

    def barrier(self):
        for e in ENGS:
            for f in ENGS:
                if f != e and self.cnt[f] > 0:
                    self._wait(e, ('c', f, self.cnt[f]))
            for q, pool in self.dsem.items():
                for ent in pool:
                    if ent[1] > 0:
                        self._wait(e, ('d', ent, ent[1]))

    def finish(self, bufs, e='sp'):
        for b in bufs:
            for t in b.last_w:
                self._wait(e, t)
            for t in list(b.readers.values()):
                self._wait(e, t)


def run_pipelined(gens, depth=2):
    active = []
    it = iter(gens)
    more = True
    while True:
        if more and len(active) < depth:
            try:
                active.append(next(it))
            except StopIteration:
                more = False
        if not active:
            break
        for g_ in list(active):
            try:
                next(g_)
            except StopIteration:
                active.remove(g_)


def build(debug=(), stop=None):
    nc = bass.Bass("TRN2", target_bir_lowering=False)
    k = K(nc)

    def din(name, shape, dt=F32):
        return nc.dram_tensor(name, list(shape), dt, kind="ExternalInput").ap()

    xb_d = din("xb", [SEQ, D_MODEL])
    xo_d = din("xo", [NOWN * 128, D_MODEL])
    po_d = din("po", [NOWN * 128, 256])
    win_d = din("w_in", [D_MODEL, IN_WIDTH])
    wout_d = din("w_out", [1024, 1024])
    wple_d = din("w_ple", [256, 1024])
    wg_d = din("w_gate", [1024, 1024])
    n1_d = din("n1", [128, 8])
    n2_d = din("n2", [128, 8])
    cw_d = din("cw", [128, 48])
    alog_d = din("alog", [1, 4])
    dtb_d = din("dtb", [1, 4])
    aon_d = din("aon", [1, 128])
    gq_d = din("gq", [1, 64])
    gk_d = din("gk", [1, 64])
    bgate_d = din("bgate", [1, 1024])
    cst_d = din("cst", [128, 9 * 128])
    cm_d = din("cm", [128, 256])
    sel_d = din("sel", [128, 2])
    y_d = nc.dram_tensor("y", [NOWN * 128, D_MODEL], F32, kind="ExternalOutput").ap()
    qkvc_d = nc.dram_tensor("qkvc", [12, 128, SEQ], F32, kind="Internal").ap()
    bqkvc = [[Buf() for _ in range(8)] for _ in range(12)]
    sza_d = nc.dram_tensor("sza_s", [NOWN, 128, 512], BF16, kind="Internal").ap()
    szb_d = nc.dram_tensor("szb_s", [NOWN, 128, 512], BF16, kind="Internal").ap()
    bqT_d = nc.dram_tensor("bqT_s", [64, 8, NOWN * 128], BF16, kind="Internal").ap()
    iqT_d = nc.dram_tensor("iqT_s", [128, 8, NOWN * 128], BF16, kind="Internal").ap()
    ikT_d = nc.dram_tensor("ikT_s", [128, SEQ], BF16, kind="Internal").ap()
    bkT_d = nc.dram_tensor("bkT_s", [64, 2, SEQ], BF16, kind="Internal").ap()
    Vp_d = nc.dram_tensor("Vp_s", [128, NT, 130], BF16, kind="Internal").ap()
    score_d = nc.dram_tensor("score_s", [NOWN, 128, SEQ], F32, kind="Internal").ap()
    dbg_out = {}

    def done():
        k.barrier()
        return nc, dbg_out, k

    def dbg(name, shape):
        if name in debug:
            dbg_out[name] = nc.dram_tensor("dbg_" + name, list(shape), F32, kind="ExternalOutput").ap()
            return dbg_out[name]
        return None

    def P(name, shape, dt=F32):
        return TB(nc.alloc_sbuf_tensor("s_" + name, list(shape), dt), name)

    def act(out, in_, func, reads, writes, **kw):
        return k.op('act', lambda: nc.scalar.activation(out=out, in_=in_, func=func, **kw), reads, writes)

    def tt(e, out, in0, in1, op, reads, writes):
        eng = nc.vector if e == 'dve' else nc.gpsimd
        return k.op(e, lambda: eng.tensor_tensor(out=out, in0=in0, in1=in1, op=op), reads, writes)

    def ts(e, out, in0, s1, op0, reads, writes, s2=None, op1=None, **kw):
        eng = nc.vector if e == 'dve' else nc.gpsimd
        if op1 is None:
            return k.op(e, lambda: eng.tensor_scalar(out=out, in0=in0, scalar1=s1, scalar2=None, op0=op0, **kw), reads, writes)
        return k.op(e, lambda: eng.tensor_scalar(out=out, in0=in0, scalar1=s1, scalar2=s2, op0=op0, op1=op1, **kw), reads, writes)

    def stt(out, in0, scalar, in1, op0, op1, reads, writes):
        return k.op('dve', lambda: nc.vector.scalar_tensor_tensor(out=out, in0=in0, scalar=scalar, in1=in1, op0=op0, op1=op1), reads, writes)

    def mm(out, lhsT, rhs, reads, writes, start=True, stop=True, sgc=False):
        if sgc:
            return k.op('pe', lambda: nc.tensor.matmul(out, lhsT=lhsT, rhs=rhs, start=start, stop=stop,
                                                        skip_group_check=True), reads, writes)
        return k.op('pe', lambda: nc.tensor.matmul(out, lhsT=lhsT, rhs=rhs, start=start, stop=stop), reads, writes)

    def tr(out, in_, ident, reads, writes):
        return k.op('pe', lambda: nc.tensor.transpose(out=out, in_=in_, identity=ident), reads, writes)

    def cp(e, out, in_, reads, writes):
        if e == 'act':
            return k.op('act', lambda: nc.scalar.copy(out=out, in_=in_), reads, writes)
        eng = nc.vector if e == 'dve' else nc.gpsimd
        return k.op(e, lambda: eng.tensor_copy(out=out, in_=in_), reads, writes)

    PS = [TB(nc.alloc_psum_tensor("ps%d" % i, [128, 512], F32), "ps%d" % i) for i in range(8)]
    for p_ in PS:
        p_.b.excl = True

    def psb16(i):
        return PS[i].t[:].bitcast(BF16)

    cst = P("cst", [128, 9 * 128])
    k.dma('sp', cst[:], cst_d[:, :], writes=[cst.b])
    ident = cst[:, 0:128]
    tribd = cst[:, 128:256]
    blk = cst[:, 256:384]
    l0 = cst[:, 384:512]
    l1 = cst[:, 512:640]
    negns = cst[:, 640:768]
    negst = cst[:, 768:896]
    ones = cst[:, 896:1024]
    negones = cst[:, 1024:1152]
    identb = P("identb", [128, 128], BF16)
    cp('dve', identb[:], ident, [cst.b], [identb.b])
    n1t = P("n1t", [128, 8]); k.dma('sp', n1t[:], n1_d[:, :], writes=[n1t.b])
    n2t = P("n2t", [128, 8]); k.dma('sp', n2t[:], n2_d[:, :], writes=[n2t.b])
    cwt = P("cwt", [128, 48]); k.dma('sp', cwt[:], cw_d[:, :], writes=[cwt.b])
    alogB = P("alogB", [128, 4]); k.dma('sp', alogB[:], alog_d[0:1, :].partition_broadcast(128), writes=[alogB.b])
    dtbB = P("dtbB", [128, 4]); k.dma('sp', dtbB[:], dtb_d[0:1, :].partition_broadcast(128), writes=[dtbB.b])
    aonB = P("aonB", [128, 128]); k.dma('sp', aonB[:], aon_d[0:1, :].partition_broadcast(128), writes=[aonB.b])
    gqB = P("gqB", [128, 64]); k.dma('sp', gqB[:], gq_d[0:1, :].partition_broadcast(128), writes=[gqB.b])
    gkB = P("gkB", [128, 64]); k.dma('sp', gkB[:], gk_d[0:1, :].partition_broadcast(128), writes=[gkB.b])
    cmt = P("cmt", [128, 256]); k.dma('sp', cmt[:], cm_d[:, :], writes=[cmt.b])
    selt = P("selt", [128, 2]); k.dma('sp', selt[:], sel_d[:, :], writes=[selt.b])

    iw = P("iw", [128, NOWN, 8])
    BA = P("BA", [128, NT, 8])
    beta = P("beta", [128, 128])
    gg = P("gg", [128, 128])
    eg = P("eg", [128, 128])
    ekd = P("ekd", [128, 128])
    bkg = P("bkg", [128, 128])
    eglb = [P("eglb0", [128, 128]), P("eglb1", [128, 128])]
    sm = P("sm", [128, 64])

    wst = [None, None]
    wbf = [None, None]
    wctr = [0]

    def alloc_w(Rf, tag):
        w_ = Rf("wst" + tag, [128, 8, 512])
        wst[0] = wst[1] = w_
        wbf[0] = Rf("wbf0" + tag, [128, 8, 512], BF16)
        wbf[1] = Rf("wbf1" + tag, [128, 8, 512], BF16)

    def load_w(w_d, ranges, gain, kc=8):
        i = wctr[0] % 2
        wctr[0] += 1
        st, wb = wst[i], wbf[i]
        src = w_d.rearrange("(c p) n -> p c n", p=128)
        off = 0
        for (a, b_) in ranges:
            k.dma('sp', st[:, 0:kc, off:off + (b_ - a)], src[:, :, a:b_], writes=[st.b])
            off += b_ - a
        if gain is not None:
            for c_ in range(kc):
                ts('dve', wb[:, c_, 0:off], st[:, c_, 0:off], gain[:, c_:c_ + 1], ALU.mult, [st.b, gain.b], [wb.b])
        else:
            cp('dve', wb[:, 0:kc, 0:off], st[:, 0:kc, 0:off], [st.b], [wb.b])
        return wb, off

    def rstd_from_ssq(dst, src, n, scale, reads, writes):
        act(dst, src, AF.Ln, reads, writes, scale=scale, bias=EPS)
        act(dst, dst, AF.Exp, writes, writes, scale=-0.5)

    sO1 = ExitStack()
    hTo = TB(sO1.enter_context(nc.sbuf_tensor("hTo", [128, 8, NOWN * 128], BF16, side="right")), "hTo")
    with ExitStack() as sAB:
        def R(name, shape, dt=F32):
            return TB(sAB.enter_context(nc.sbuf_tensor("r_" + name, list(shape), dt, side="right")), name)

        hT = R("hT", [128, 8, SEQ], BF16)
        alloc_w(R, "ab")
        stg_ik = [R("stg_ik0", [128, 512], BF16), R("stg_ik1", [128, 512], BF16)]
        stg_bk = [R("stg_bk0", [64, 2, 128], BF16), R("stg_bk1", [64, 2, 128], BF16)]
        stg_v = [R("stg_v0", [128, 2, 65], BF16), R("stg_v1", [128, 2, 65], BF16)]
        xin = [R("xin0", [128, 1024]), R("xin1", [128, 1024])]
        junk = R("junk", [128, 1024])
        xn = [R("xn0", [128, 1024], BF16), R("xn1", [128, 1024], BF16)]
        ssq = R("ssq", [128, 4])

        def phase_a(x_d, ntiles, dst, grp):
            def tile_gen(t):
                xt = xin[t % 2]
                xnt = xn[t % 2]
                so = 2 * (t % 2)
                sb = ssq.sub(t % 2)
                k.dma('sp', xt[:], x_d[t * 128:(t + 1) * 128, :], writes=[xt.b])
                act(junk[:], xt[:], AF.Square, [xt.b], [junk.b, sb], accum_out=ssq[:, so:so + 1])
                yield
                rstd_from_ssq(ssq[:, so + 1:so + 2], ssq[:, so:so + 1], 1, 1.0 / D_MODEL, [sb], [sb])
                yield
                ts('dve', xnt[:], xt[:], ssq[:, so + 1:so + 2], ALU.mult, [xt.b, sb], [xnt.b])
                yield
                pb = PS[t % 2]
                for fc in range(8):
                    tr(psb16(t % 2)[:, fc * 128:(fc + 1) * 128], xnt[:, fc * 128:(fc + 1) * 128], identb[:],
                       [xnt.b, identb.b], [pb.b])
                yield
                cp('act' if t % 2 == 0 else 'dve', dst[:, :, t * 128:(t + 1) * 128],
                   psb16(t % 2).rearrange("p (c t) -> p c t", c=8), [pb.b], [dst.sub(t // grp)])
                yield
            run_pipelined((tile_gen(t) for t in range(ntiles)), depth=2)

        phase_a(xb_d, NT, hT, 4)
        hsel = [R("hsel0", [128, 8, 128], BF16), R("hsel1", [128, 8, 128], BF16)]
        for j in range(NOWN):
            tmp_ = hsel[j % 2]
            ts('dve', tmp_[:], hT[:, :, (2 * j) * 128:(2 * j + 1) * 128], selt[:, 0:1], ALU.mult,
               [hT.sub((2 * j) // 4), selt.b], [tmp_.b])
            stt(hTo[:, :, j * 128:(j + 1) * 128], hT[:, :, (2 * j + 1) * 128:(2 * j + 2) * 128], selt[:, 1:2], tmp_[:],
                ALU.mult, ALU.add, [hT.sub((2 * j + 1) // 4), selt.b, tmp_.b], [hTo.sub(j)])
        if stop == 'A':
            return done()

        d = dbg("hT", [128, 8 * 512])
        if d is not None:
            tmp = R("dbg_hT", [128, 8, 512])
            cp('dve', tmp[:], hT[:, :, 0:512], [hT.sub(0)], [tmp.b])
            k.dma('sp', d[:, :], tmp[:].rearrange("p c t -> p (c t)"), reads=[tmp.b], writes=[Buf()])

        pre = [R("pre%d" % i, [128, 528]) for i in range(3)]
        cacc = [R("cacc%d" % i, [128, 512]) for i in range(3)]
        cout = [R("cout%d" % i, [128, 512]) for i in range(3)]

        def b1_gen(it, wb, cc, ch, g):
            pf = pre[it % 3]
            pfn = pre[(it + 1) % 3]
            pb = PS[2 + (it % 3)]
            ca = cacc[it % 3]
            co = cout[it % 3]
            for fc in range(8):
                mm(pb[:, :], wb[:, fc, cc * 128:(cc + 1) * 128], hT[:, fc, g * 512:(g + 1) * 512],
                   [wb.b, hT.sub(g)], [pb.b], start=(fc == 0), stop=(fc == 7))
            if g == 0:
                k.op('dve', lambda pf=pf: nc.vector.memset(pf[:, 0:8], 0.0), [], [pf.b])
            yield
            cp('act', pf[:, 8:520], pb[:, :], [pb.b], [pf.b])
            act(ca[:], pb[:, :], AF.Identity, [pb.b, cwt.b], [ca.b], scale=cwt[:, ch * 4 + 3:ch * 4 + 4])
            if g < 7:
                cp('pool', pfn[:, 5:8], pf[:, 517:520], [pf.b], [pfn.b])
            yield
            rd = [pf.b, cwt.b]
            for j in range(3):
                stt(ca[:], pf[:, 5 + j:5 + j + 512], cwt[:, ch * 4 + j:ch * 4 + j + 1], ca[:],
                    ALU.mult, ALU.add, rd + [ca.b], [ca.b])
            yield
            act(co[:], ca[:], AF.Silu, [ca.b], [co.b])
            k.dma('act', qkvc_d[ch, :, g * 512:(g + 1) * 512], co[:], reads=[co.b], writes=[bqkvc[ch][g]])
            yield

        def b1_all():
            it = 0
            for wg in range(3):
                wb, _ = load_w(win_d, [(wg * 512, (wg + 1) * 512)], n1t)
                for cc in range(4):
                    for g in range(8):
                        yield b1_gen(it, wb, cc, wg * 4 + cc, g)
                        it += 1
        run_pipelined(b1_all(), depth=3)

        if stop == 'B1':
            return done()
        wb, _ = load_w(win_d, [(4360, 4488)], n1t)
        def b2_gen(g, wb):
            pb = PS[2 + (g % 2)]
            for fc in range(8):
                mm(pb[:, :], wb[:, fc, 0:128], hT[:, fc, g * 512:(g + 1) * 512], [wb.b, hT.sub(g)], [pb.b],
                   start=(fc == 0), stop=(fc == 7))
            yield
            sg = stg_ik[g % 2]
            cp('act', sg[:], pb[:, :], [pb.b], [sg.b])
            k.dma('act', ikT_d[:, g * 512:(g + 1) * 512], sg[:], reads=[sg.b], writes=[Buf()])
            yield
        run_pipelined((b2_gen(g, wb) for g in range(8)), depth=2)

        wb, ncol = load_w(win_d, [(2048, 2056), (2568, 2824)], n1t)
        knt = [R("knt0", [128, 128], BF16), R("knt1", [128, 128], BF16)]
        for sv in stg_v:
            k.op('pool', lambda sv=sv: nc.gpsimd.memset(sv[:, :, 64:65], 1.0), [], [sv.b])
        def b3_gen(t, wb):
            pb = PS[2 + (t % 2)]
            for fc in range(8):
                mm(pb[:, 0:264], hT[:, fc, t * 128:(t + 1) * 128], wb[:, fc, 0:264], [wb.b, hT.sub(t // 4)], [pb.b],
                   start=(fc == 0), stop=(fc == 7))
            yield
            cp('act', BA[:, t, :], pb[:, 0:8], [pb.b], [BA.sub(t)])
            smb = sm.sub(('b3', t % 2))
            so = (t % 2) * 8
            for g in range(2):
                act(junk[:, 0:64], pb[:, 8 + g * 64:8 + (g + 1) * 64], AF.Square, [pb.b], [junk.b, smb],
                    accum_out=sm[:, so + g:so + g + 1])
            sv = stg_v[t % 2]
            cp('act', sv[:, :, 0:64], pb[:, 136:264].rearrange("p (g d) -> p g d", g=2), [pb.b], [sv.b])
            k.dma('act', Vp_d[:, t, :], sv[:].rearrange("p g d -> p (g d)"), reads=[sv.b], writes=[Buf()])
            yield
            rstd_from_ssq(sm[:, so + 2:so + 4], sm[:, so:so + 2], 2, 1.0 / 64, [smb], [smb])
            yield
            kt = knt[t % 2]
            for g in range(2):
                stt(kt[:, g * 64:(g + 1) * 64], pb[:, 8 + g * 64:8 + (g + 1) * 64], sm[:, so + 2 + g:so + 3 + g], gkB[:],
                    ALU.mult, ALU.mult, [pb.b, smb, gkB.b], [kt.b])
            yield
            pt = PS[4 + (t % 2)]
            for g in range(2):
                tr(psb16(4 + (t % 2))[0:64, g * 128:(g + 1) * 128], kt[:, g * 64:(g + 1) * 64], identb[:],
                   [kt.b, identb.b], [pt.b])
            yield
            sk = stg_bk[t % 2]
            cp('dve', sk[:], psb16(4 + (t % 2))[0:64, 0:256].rearrange("p (g t) -> p g t", g=2), [pt.b], [sk.b])
            k.dma('pool', bkT_d[:, :, t * 128:(t + 1) * 128], sk[:], reads=[sk.b], writes=[Buf()])
            yield
        run_pipelined((b3_gen(t, wb) for t in range(NT)), depth=2)

        if stop == 'B3':
            return done()
        BAv = BA[:].rearrange("p t (a h) -> p t a h", a=2)
        bv3 = beta[:].rearrange("p (t h) -> p t h", h=4)
        g3 = gg[:].rearrange("p (t h) -> p t h", h=4)
        tA = R("tA", [128, 128]); tA3 = tA[:].rearrange("p (t h) -> p t h", h=4)
        tB_ = R("tB", [128, 128]); tB3 = tB_[:].rearrange("p (t h) -> p t h", h=4)
        nA = R("nA", [128, 4])
        act(beta[:].rearrange("p (t h) -> p t h", h=4), BAv[:, :, 0, :], AF.Sigmoid, [BA.sub(t_) for t_ in range(NT)], [beta.b])
        tt('dve', tA3, BAv[:, :, 1, :], dtbB[:].unsqueeze(1).to_broadcast([128, NT, 4]), ALU.add, [BA.sub(t_) for t_ in range(NT)] + [dtbB.b], [tA.b])
        act(tB_[:], tA[:], AF.Abs, [tA.b], [tB_.b])
        act(tB_[:], tB_[:], AF.Exp, [tB_.b], [tB_.b], scale=-1.0)
        act(tB_[:], tB_[:], AF.Ln, [tB_.b], [tB_.b], bias=1.0)
        stt(tA[:], tA[:], 0.0, tB_[:], ALU.max, ALU.add, [tA.b, tB_.b], [tA.b])
        act(nA[:], alogB[:], AF.Exp, [alogB.b], [nA.b])
        ts('dve', nA[:], nA[:], -1.0, ALU.mult, [nA.b], [nA.b])
        tt('dve', g3, tA3, nA[:].unsqueeze(1).to_broadcast([128, NT, 4]), ALU.mult, [tA.b, nA.b], [gg.b])
        pb = PS[6]
        mm(pb[:, 0:128], tribd, gg[:], [cst.b, gg.b], [pb.b])
        mm(pb[:, 128:256], blk, gg[:], [cst.b, gg.b], [pb.b])
        mm(pb[:, 256:384], l0, gg[:], [cst.b, gg.b], [pb.b])
        mm(pb[:, 384:512], l1, gg[:], [cst.b, gg.b], [pb.b])
        act(eg[:], pb[:, 0:128], AF.Exp, [pb.b], [eg.b])
        cp('dve', tA[:], pb[:, 0:128], [pb.b], [tA.b])
        tt('dve', tB_[:], pb[:, 128:256], tA[:], ALU.subtract, [pb.b, tA.b], [tB_.b])
        act(ekd[:], tB_[:], AF.Exp, [tB_.b], [ekd.b])
        act(eglb[0][:], pb[:, 256:384], AF.Exp, [pb.b], [eglb[0].b])
        act(eglb[1][:], pb[:, 384:512], AF.Exp, [pb.b], [eglb[1].b])
        tt('dve', bkg[:], beta[:], eg[:], ALU.mult, [beta.b, eg.b], [bkg.b])

        for name, tb_ in (("beta", beta), ("gg", gg), ("eg", eg), ("ekd", ekd), ("eglb1", eglb[1])):
            d = dbg(name, [128, 128])
            if d is not None:
                k.dma('sp', d[:, :], tb_[:], reads=[tb_.b], writes=[Buf()])

        k.barrier()
    if stop == 'AB':
        return done()

    with ExitStack() as sB5:
        def R(name, shape, dt=F32):
            return TB(sB5.enter_context(nc.sbuf_tensor("r_" + name, list(shape), dt, side="right")), name)

        alloc_w(R, "b5")
        stg_z = [R("stg_z0", [128, 512], BF16), R("stg_z1", [128, 512], BF16)]
        stg_q = [R("stg_q0", [64, 8, 128], BF16), R("stg_q1", [64, 8, 128], BF16)]
        sqq = R("sqq", [128, 512])
        qn1 = R("qn1", [128, 512])
        qnb = [R("qnb0", [128, 512], BF16), R("qnb1", [128, 512], BF16)]
        stq = R("stq", [128, 16])
        sqq2 = [sqq, R("sqq1", [128, 512])]
        qn12 = [qn1, R("qn11", [128, 512])]

        def z_gen(j, wb, dst_d):
            pb = PS[j % 2]
            for fc in range(8):
                mm(pb[:, :], hTo[:, fc, j * 128:(j + 1) * 128], wb[:, fc, 0:512], [wb.b, hTo.sub(j)], [pb.b],
                   start=(fc == 0), stop=(fc == 7))
            yield
            sg = stg_z[j % 2]
            act(sg[:], pb[:, :], AF.Silu, [pb.b], [sg.b])
            k.dma('act', dst_d[j, :, :], sg[:], reads=[sg.b], writes=[Buf()])
            yield
        for (c0, dst_d) in ((1536, sza_d), (2824, szb_d)):
            wb, _ = load_w(win_d, [(c0, c0 + 512)], n1t)
            run_pipelined((z_gen(j, wb, dst_d) for j in range(NOWN)), depth=2)

        def bq_gen(j, wb):
            pb = PS[2 + (j % 2)]
            sq_s = sqq2[j % 2]; qn_s = qn12[j % 2]
            so = 16 * 0
            stb = stq.sub(j % 2)
            c0 = (j % 2) * 8
            for fc in range(8):
                mm(pb[:, :], hTo[:, fc, j * 128:(j + 1) * 128], wb[:, fc, 0:512], [wb.b, hTo.sub(j)], [pb.b],
                   start=(fc == 0), stop=(fc == 7))
            yield
            act(sq_s[:], pb[:, :], AF.Square, [pb.b], [sq_s.b])
            yield
            k.op('dve', lambda: nc.vector.tensor_reduce(out=stq2[:, c0:c0 + 8], in_=sq_s[:].rearrange("p (h d) -> p h d", h=8),
                                                        axis=AX.X, op=ALU.add), [sq_s.b], [stb])
            yield
            rstd_from_ssq(stq2[:, 16 + c0:16 + c0 + 8], stq2[:, c0:c0 + 8], 8, 1.0 / 64, [stb], [stb])
            yield
            tt('dve', qn_s[:].rearrange("p (h d) -> p h d", h=8), pb[:, :].rearrange("p (h d) -> p h d", h=8),
               stq2[:, 16 + c0:16 + c0 + 8].unsqueeze(2).to_broadcast([128, 8, 64]), ALU.mult, [pb.b, stb], [qn_s.b])
            qb_ = qnb[j % 2]
            tt('dve', qb_[:].rearrange("p (h d) -> p h d", h=8), qn_s[:].rearrange("p (h d) -> p h d", h=8),
               gqB[:].unsqueeze(1).to_broadcast([128, 8, 64]), ALU.mult, [qn_s.b, gqB.b], [qb_.b])
            yield
            pt = PS[4 + (j % 2)]
            for h in range(8):
                tr(psb16(4 + (j % 2))[0:64, h * 128:(h + 1) * 128], qb_[:, h * 64:(h + 1) * 64], identb[:], [qb_.b, identb.b], [pt.b])
            yield
            sq_ = stg_q[j % 2]
            cp('act', sq_[:], psb16(4 + (j % 2))[0:64, :].rearrange("p (h t) -> p h t", h=8), [pt.b], [sq_.b])
            k.dma('act', bqT_d[:, :, j * 128:(j + 1) * 128], sq_[:], reads=[sq_.b], writes=[Buf()])
            yield
        stq2 = R("stq2", [128, 32])
        wb, _ = load_w(win_d, [(2056, 2568)], n1t)
        run_pipelined((bq_gen(j, wb) for j in range(NOWN)), depth=2)
        wb, _ = load_w(win_d, [(4488, 4496)], n1t)
        for j in range(NOWN):
            pb = PS[j % 2]
            for fc in range(8):
                mm(pb[:, 0:8], hTo[:, fc, j * 128:(j + 1) * 128], wb[:, fc, 0:8], [wb.b, hTo.sub(j)], [pb.b],
                   start=(fc == 0), stop=(fc == 7))
            ts('dve', iw[:, j, :], pb[:, 0:8], float(8 ** -0.5 * 128 ** -0.5), ALU.mult, [pb.b], [iw.sub(j)])

        def iq_gen(it, wb, wg, hh, g):
            pb = PS[2 + (it % 2)]
            for fc in range(8):
                mm(pb[:, :], wb[:, fc, hh * 128:(hh + 1) * 128], hTo[:, fc, g * 512:(g + 1) * 512],
                   [wb.b] + [hTo.sub(4 * g + i) for i in range(4)], [pb.b], start=(fc == 0), stop=(fc == 7))
            yield
            sg = stg_z[it % 2]
            cp('act', sg[:], pb[:, :], [pb.b], [sg.b])
            k.dma('act', iqT_d[:, wg * 4 + hh, g * 512:(g + 1) * 512], sg[:], reads=[sg.b], writes=[Buf()])
            yield

        def iq_all():
            it = 0
            for wg in range(2):
                wb, _ = load_w(win_d, [(3336 + wg * 512, 3336 + (wg + 1) * 512)], n1t)
                for hh in range(4):
                    for g in range(4):
                        yield iq_gen(it, wb, wg, hh, g)
                        it += 1
        run_pipelined(iq_all(), depth=2)
        k.barrier()
    sO1.close()
    MIXT = P("MIXT", [128, 8, NOWN * 128], BF16)
    if stop == 'B5':
        return done()

    with ExitStack() as sC:
        def R(name, shape, dt=F32):
            return TB(sC.enter_context(nc.sbuf_tensor("r_" + name, list(shape), dt, side="right")), name)

        def RN_(name, shape, n, dt=F32):
            return [R(name + str(i), shape, dt) for i in range(n)]

        Xin = RN_("Xin", [128, 12, 128], 2)
        SQ = R("SQ", [128, 1024])
        RNt = R("RN", [128, 1024])
        QKn3 = RN_("QKn", [128, 8, 128], 3)
        KD3 = RN_("KD", [128, 4, 128], 3)
        ATT3 = RN_("ATT", [128, 4, 128], 3)
        KBG2 = RN_("KBG", [128, 4, 128], 2); VB2 = RN_("VB", [128, 4, 128], 2)
        TG = R("TG", [128, 4, 128])
        DT = R("DT", [128, 512]); DS = R("DS", [128, 512])
        KKs = R("KKs", [128, 512]); KQs = R("KQs", [128, 512])
        Am2 = RN_("Am", [128, 4, 128], 2); Um2 = RN_("Um", [128, 4, 128], 2)
        Pa2 = RN_("Pa", [128, 4, 128], 2); Qa2 = RN_("Qa", [128, 4, 128], 2)
        Rm2 = RN_("Rm", [128, 4, 128], 2)
        VAL2 = RN_("VAL", [128, 4, 128], 2); KCDT2 = RN_("KCDT", [128, 4, 128], 2)
        VN = R("VN", [128, 4, 128]); AVs = R("AVs", [128, 4, 128])
        Ot = R("Ot", [128, 4, 128]); ON = R("ON", [128, 4, 128]); OS = R("OS", [128, 512])
        MIXb = R("MIXb", [128, 512], BF16)
        szat = [R("szat0", [128, 512], BF16), R("szat1", [128, 512], BF16)]
        S = R("S", [128, 4, 128])
        st4 = R("st4", [128, 8])
        k.op('pool', lambda: nc.gpsimd.memset(S[:], 0.0), [], [S.b])
        ident4 = ident.unsqueeze(1).to_broadcast([128, 4, 128])
        b6, b7 = PS[6], PS[7]

        def v4(ps):
            return ps[:, :].rearrange("p (h c) -> p h c", h=4)

        def f2(tb_):
            return tb_[:].rearrange("p h c -> p (h c)")

        def bc4(t_, sc):
            return t_[:, sc].unsqueeze(2).to_broadcast([128, 4, 128])

        def gen_p1(n):
            X = Xin[n % 2]
            QKn = QKn3[n % 3]; KD = KD3[n % 3]; ATT = ATT3[n % 3]
            KBG = KBG2[n % 2]; VB = VB2[n % 2]; Am = Am2[n % 2]; Um = Um2[n % 2]; Rm = Rm2[n % 2]
            for c3 in range(3):
                k.dma('sp', X[:, c3 * 4:(c3 + 1) * 4, :],
                      qkvc_d[c3 * 4:(c3 + 1) * 4, :, n * 128:(n + 1) * 128].rearrange("c p t -> p c t"),
                      reads=[bqkvc[c3 * 4 + i][n // 4] for i in range(4)], writes=[X.b])
            sc = slice(n * 4, (n + 1) * 4)
            act(SQ[:], X[:, 0:8, :].rearrange("p c t -> p (c t)"), AF.Square, [X.b], [SQ.b])
            mm(b6[:, :], ones, SQ[:, 0:512], [cst.b, SQ.b], [b6.b])
            mm(b7[:, :], ones, SQ[:, 512:1024], [cst.b, SQ.b], [b7.b])
            yield
            act(RNt[:, 0:512], b6[:, :], AF.Ln, [b6.b], [RNt.b], bias=EPS)
            act(RNt[:, 512:1024], b7[:, :], AF.Ln, [b7.b], [RNt.b], bias=EPS)
            act(RNt[:], RNt[:], AF.Exp, [RNt.b], [RNt.b], scale=-0.5)
            yield
            stt(QKn[:, 0:4, :].rearrange("p c t -> p (c t)"), X[:, 0:4, :].rearrange("p c t -> p (c t)"), 128.0 ** -0.5,
                RNt[:, 0:512], ALU.mult, ALU.mult, [X.b, RNt.b], [QKn.b])
            tt('dve', QKn[:, 4:8, :].rearrange("p c t -> p (c t)"), X[:, 4:8, :].rearrange("p c t -> p (c t)"),
               RNt[:, 512:1024], ALU.mult, [X.b, RNt.b], [QKn.b])
            for h in range(4):
                tr(b6[:, h * 128:(h + 1) * 128], QKn[:, 4 + h, :], ident, [QKn.b, cst.b], [b6.b])
            for h in range(4):
                tr(b7[:, h * 128:(h + 1) * 128], X[:, 8 + h, :], ident, [X.b, cst.b], [b7.b])
            tt('dve', TG[:], tribd.unsqueeze(1).to_broadcast([128, 4, 128]), bc4(gg, sc), ALU.mult, [cst.b, gg.b], [TG.b])
            yield
            tt('dve', KBG[:], v4(b6), bc4(bkg, sc), ALU.mult, [b6.b, bkg.b], [KBG.b])
            tt('dve', KD[:], v4(b6), bc4(ekd, sc), ALU.mult, [b6.b, ekd.b], [KD.b])
            tt('dve', VB[:], v4(b7), bc4(beta, sc), ALU.mult, [b7.b, beta.b], [VB.b])
            for h in range(4):
                mm(b6[:, h * 128:(h + 1) * 128], QKn[:, 4 + h, :], QKn[:, 4 + h, :], [QKn.b], [b6.b])
            for h in range(4):
                mm(b7[:, h * 128:(h + 1) * 128], QKn[:, 4 + h, :], QKn[:, h, :], [QKn.b], [b7.b])
            yield
            cp('act', KKs[:], b6[:, :], [b6.b], [KKs.b])
            cp('dve', KQs[:], b7[:, :], [b7.b], [KQs.b])
            for h in range(4):
                o = b6[:, h * 128:(h + 1) * 128]
                mm(o, ones, TG[:, h, :], [cst.b, TG.b], [b6.b], start=True, stop=False)
                mm(o, TG[:, h, :], negones, [cst.b, TG.b], [b6.b], start=False, stop=False)
                mm(o, ident, negns, [cst.b], [b6.b], start=False, stop=True)
            for h in range(4):
                o = b7[:, h * 128:(h + 1) * 128]
                mm(o, TG[:, h, :], ones, [cst.b, TG.b], [b7.b], start=True, stop=False)
                mm(o, negones, TG[:, h, :], [cst.b, TG.b], [b7.b], start=False, stop=False)
                mm(o, ident, negst, [cst.b], [b7.b], start=False, stop=True)
            yield
            act(DT[:], b6[:, :], AF.Exp, [b6.b], [DT.b])
            act(DS[:], b7[:, :], AF.Exp, [b7.b], [DS.b])
            yield
            tt('dve', f2(Am), KKs[:], DS[:], ALU.mult, [KKs.b, DS.b], [Am.b])
            tt('dve', Am[:], Am[:], bc4(beta, sc), ALU.mult, [Am.b, beta.b], [Am.b])
            tt('dve', f2(ATT), KQs[:], DT[:], ALU.mult, [KQs.b, DT.b], [ATT.b])
            for h in range(4):
                tr(b6[:, h * 128:(h + 1) * 128], Am[:, h, :], ident, [Am.b, cst.b], [b6.b])
            yield
            cp('act', f2(Um), b6[:, :], [b6.b], [Um.b])
            stt(Rm[:], v4(b6), -1.0, ident4, ALU.mult, ALU.add, [b6.b, cst.b], [Rm.b])
            if n == 0:
                for name, tb_ in (("ATT0", ATT), ("Am0", Am)):
                    d = dbg(name, [128, 512])
                    if d is not None:
                        k.dma('sp', d[:, :], f2(tb_), reads=[tb_.b], writes=[Buf()])
            yield

        def gen_p2(n):
            KBG = KBG2[n % 2]; VB = VB2[n % 2]; Am = Am2[n % 2]; Um = Um2[n % 2]; Rm = Rm2[n % 2]
            Pa = Pa2[n % 2]; Qa = Qa2[n % 2]; VAL = VAL2[n % 2]; KCDT = KCDT2[n % 2]
            bA, bB, bC = PS[3], PS[4], PS[5]
            Pc, Qc = Um, Am
            Pn, Qn = Pa, Qa
            for stg in range(1, 7):
                if stg >= 2:
                    for h in range(4):
                        mm(bC[:, h * 128:(h + 1) * 128], Qc[:, h, :], Rm[:, h, :], [Qc.b, Rm.b], [bC.b])
                if stg <= 4:
                    for h in range(4):
                        mm(bA[:, h * 128:(h + 1) * 128], Qc[:, h, :], Pc[:, h, :], [Qc.b, Pc.b], [bA.b])
                if stg <= 5:
                    for h in range(4):
                        mm(bB[:, h * 128:(h + 1) * 128], Pc[:, h, :], Qc[:, h, :], [Qc.b, Pc.b], [bB.b])
                yield
                if stg >= 2:
                    tt('dve', f2(Rm), f2(Rm), bC[:, :], ALU.add, [Rm.b, bC.b], [Rm.b])
                if stg <= 4:
                    cp('act', f2(Pn), bA[:, :], [bA.b], [Pn.b])
                if stg <= 5:
                    cp('act' if stg > 4 else 'dve', f2(Qn), bB[:, :], [bB.b], [Qn.b])
                if stg == 1:
                    Pc, Qc, Pn, Qn = Pa, Qa, Um, Am
                else:
                    Pc, Qc, Pn, Qn = Pn, Qn, Pc, Qc
                yield
            for h in range(4):
                mm(bA[:, h * 128:(h + 1) * 128], Rm[:, h, :], VB[:, h, :], [Rm.b, VB.b], [bA.b])
            for h in range(4):
                mm(bB[:, h * 128:(h + 1) * 128], KBG[:, h, :], Rm[:, h, :], [Rm.b, KBG.b], [bB.b])
            yield
            cp('act', f2(VAL), bA[:, :], [bA.b], [VAL.b])
            cp('dve', f2(KCDT), bB[:, :], [bB.b], [KCDT.b])
            if n == 0:
                for name, tb_ in (("T0", Rm), ("VAL0", VAL)):
                    d = dbg(name, [128, 512])
                    if d is not None:
                        k.dma('sp', d[:, :], f2(tb_), reads=[tb_.b], writes=[Buf()])
            yield

        def gen_rec(n):
            QKn = QKn3[n % 3]; KD = KD3[n % 3]; ATT = ATT3[n % 3]; VAL = VAL2[n % 2]; KCDT = KCDT2[n % 2]
            sc = slice(n * 4, (n + 1) * 4)
            bKS, bQS, bAV = PS[0], PS[1], PS[2]
            bSU = PS[0]
            for j in range(2):
                pr = slice(64 * j, 64 * j + 64)
                for h in range(4):
                    mm(bKS[pr, h * 128:(h + 1) * 128], KCDT[:, h, pr], S[:, h, :], [KCDT.b, S.b], [bKS.b])
                for h in range(4):
                    mm(bQS[pr, h * 128:(h + 1) * 128], QKn[:, h, pr], S[:, h, :], [QKn.b, S.b], [bQS.b])
                yield
                tt('dve', VN[pr].rearrange("p h c -> p (h c)"), VAL[pr].rearrange("p h c -> p (h c)"), bKS[pr, :], ALU.subtract,
                   [VAL.b, bKS.b], [VN.b])
                yield
                for h in range(4):
                    mm(bSU[:, h * 128:(h + 1) * 128], KD[pr, h, :], VN[pr, h, :], [KD.b, VN.b], [bSU.b])
                for h in range(4):
                    mm(bAV[pr, h * 128:(h + 1) * 128], ATT[pr, h, pr], VN[pr, h, :], [ATT.b, VN.b], [bAV.b])
                tt('dve', S[:], S[:], eglb[j][:, sc].unsqueeze(2).to_broadcast([128, 4, 128]), ALU.mult, [S.b, eglb[j].b], [S.b])
                yield
                tt('dve', f2(S), f2(S), bSU[:, :], ALU.add, [S.b, bSU.b], [S.b])
                cp('act', AVs[pr].rearrange("p h c -> p (h c)"), bAV[pr, :], [bAV.b], [AVs.b])
                tt('dve', Ot[pr], bQS[pr, :].rearrange("p (h c) -> p h c", h=4), eg[pr, sc].unsqueeze(2).to_broadcast([64, 4, 128]),
                   ALU.mult, [bQS.b, eg.b], [Ot.b])
                yield
                tt('dve', Ot[pr], Ot[pr], AVs[pr], ALU.add, [Ot.b, AVs.b], [Ot.b])
            act(ON[:], Ot[:], AF.Square, [Ot.b], [ON.b])
            yield
            k.op('dve', lambda: nc.vector.tensor_reduce(out=st4[:, 0:4], in_=ON[:], axis=AX.X, op=ALU.add), [ON.b], [st4.b])
            rstd_from_ssq(st4[:, 4:8], st4[:, 0:4], 4, 1.0 / 128, [st4.b], [st4.b])
            yield
            tt('dve', ON[:], Ot[:], st4[:, 4:8].unsqueeze(2).to_broadcast([128, 4, 128]), ALU.mult, [Ot.b, st4.b], [ON.b])
            tt('dve', ON[:], ON[:], aonB[:].unsqueeze(1).to_broadcast([128, 4, 128]), ALU.mult, [ON.b, aonB.b], [ON.b])
            if n == 0:
                for name, tb_ in (("O0", Ot), ("ON0", ON)):
                    d = dbg(name, [128, 512])
                    if d is not None:
                        k.dma('sp', d[:, :], f2(tb_), reads=[tb_.b], writes=[Buf()])
            jo = n // 2
            if n % 2 == 0:
                ts('dve', OS[:], f2(ON), selt[:, 0:1], ALU.mult, [ON.b, selt.b], [OS.b])
            else:
                stt(OS[:], f2(ON), selt[:, 1:2], OS[:], ALU.mult, ALU.add, [ON.b, selt.b, OS.b], [OS.b])
                sz = szat[jo % 2]
                k.dma('sp', sz[:], sza_d[jo, :, :], writes=[sz.b])
                tt('dve', MIXb[:], OS[:], sz[:], ALU.mult, [OS.b, sz.b], [MIXb.b])
                yield
                for c in range(4):
                    tr(psb16(2)[:, c * 128:(c + 1) * 128], MIXb[:, c * 128:(c + 1) * 128], identb[:], [MIXb.b, identb.b], [PS[2].b])
                yield
                cp('act', MIXT[:, 0:4, jo * 128:(jo + 1) * 128], psb16(2)[:, 0:512].rearrange("p (c t) -> p c t", c=4),
                   [PS[2].b], [MIXT.sub(('a', jo))])
            yield

        def stepg(gen_):
            try:
                next(gen_)
                return True
            except StopIteration:
                return False

        for s_ in range(NT + 2):
            gens = []
            if 0 <= s_ - 2 < NT:
                gens.append(("rec", gen_rec(s_ - 2)))
            if 0 <= s_ - 1 < NT:
                gens.append(("p2", gen_p2(s_ - 1)))
            if s_ < NT:
                gens.append(("p1", gen_p1(s_)))
            NR = 14
            quota = {"rec": 11, "p2": 14, "p1": 8}
            r_ = 0
            alive = True
            while alive:
                alive = False
                for nm_, g_ in gens:
                    q_ = quota[nm_]
                    n_now = ((r_ + 1) * q_) // NR - (r_ * q_) // NR if r_ < NR else 1
                    for _ in range(n_now):
                        alive = stepg(g_) or alive
                    if n_now == 0 and r_ < NR:
                        alive = True
                r_ += 1
        k.barrier()

    if stop == 'C':
        return done()
    with ExitStack() as sD:
        def R(name, shape, dt=F32):
            return TB(sD.enter_context(nc.sbuf_tensor("r_" + name, list(shape), dt, side="right")), name)

        ikT = R("ikT", [128, SEQ], BF16)
        bkT = R("bkT", [64, 2, SEQ], BF16)
        Vp = R("Vp", [128, NT, 2, 65], BF16)
        k.dma('sp', ikT[:], ikT_d[:, :], writes=[ikT.b])
        k.dma('sp', bkT[:], bkT_d[:, :, :], writes=[bkT.b])
        k.dma('sp', Vp[:].rearrange("p t g d -> p t (g d)"), Vp_d[:, :, :], writes=[Vp.b])
        iqj = [R("iqj0", [128, 8, 128], BF16), R("iqj1", [128, 8, 128], BF16)]
        score = [R("score%d" % i, [128, SEQ]) for i in range(2)]
        cjunk = R("cjunk", [128, SEQ], BF16)
        midT = R("midT", [128, 1]); cntD = R("cntD", [128, 1]); sgA = R("sgA", [128, 1]); tcm = R("tcm", [128, 2])
        rl = [R("rl%d" % i, [128, 512]) for i in range(4)]
        m8 = R("m8", [128, 8])
        bis = R("bis", [128, 8])
        Hh = R("Hh", [128, 32])
        pw2 = R("pw2", [128, 32])
        for i_ in range(32):
            k.op('pool', lambda i_=i_: nc.gpsimd.memset(pw2[:, i_:i_ + 1], float(2.0 ** -(i_ + 1))), [], [pw2.b])
        bigI = R("bigI", [128, 128], BF16)
        ts('dve', bigI[:], ident, 30000.0, ALU.mult, [cst.b], [bigI.b])
        tau = [R("tau0", [128, 1]), R("tau1", [128, 1])]
        msel = R("msel", [128, SEQ], BF16)
        MTs = [R("MT0", [128, NT, 128], BF16), R("MT1", [128, NT, 128], BF16)]
        Pb = [R("Pb%d" % i, [128, 4, 128], BF16) for i in range(4)]
        ob = R("ob", [128, 8, 65])
        rden = R("rden", [128, 8])
        obn = R("obn", [128, 8, 64])
        obg = R("obg", [128, 512], BF16)
        KBIS = 20

        bscore_d = [Buf() for _ in range(NOWN)]

        def gen_front(j):
            NK = 256 * (j + 1)
            qs = slice(j * 128, (j + 1) * 128)
            iqT = iqj[j % 2]
            sc_ = score[j % 2]
            groups = [(a, min(a + 512, NK)) for a in range(0, NK, 512)]
            it = 0
            for h in range(8):
                for gi, (a, b_) in enumerate(groups):
                    pb = PS[it % 4]
                    r = rl[it % 4]
                    sb_ = sc_.sub(gi)
                    mm(pb[:, 0:b_ - a], iqT[:, h, :], ikT[:, a:b_], [iqT.b, ikT.b], [pb.b])
                    act(r[:, 0:b_ - a], pb[:, 0:b_ - a], AF.Relu, [pb.b], [r.b])
                    if h == 0:
                        ts('dve', sc_[:, a:b_], r[:, 0:b_ - a], iw[:, j, 0:1], ALU.mult, [r.b, iw.sub(j)], [sb_, sc_.b])
                    else:
                        stt(sc_[:, a:b_], r[:, 0:b_ - a], iw[:, j, h:h + 1], sc_[:, a:b_], ALU.mult, ALU.add,
                            [r.b, iw.sub(j), sb_], [sb_])
                    it += 1
                    yield
            allg = [sc_.sub(gi) for gi in range(len(groups))]
            tt('dve', sc_[:, NK - 256:NK], sc_[:, NK - 256:NK], cmt[:], ALU.add, allg + [cmt.b], allg + [sc_.b])
            k.dma('sp', score_d[j, :, 0:NK], sc_[:, 0:NK], reads=[sc_.b], writes=[bscore_d[j]])
            yield

        cjunk2 = cjunk
        bisA = R("bisA", [128, 8])
        HhA = R("HhA", [128, 32])
        negHA = R("negHA", [128, 32])
        bqj4 = [R("bqj4_%d" % i, [64, 8, 128], BF16) for i in range(4)]
        szbj4 = [R("szbj4_%d" % i, [128, 512], BF16) for i in range(4)]

        def chain_init(j, on_act):
            NK = 256 * (j + 1)
            sc_ = score[j % 2]
            if j == 0:
                k.op('dve', lambda: nc.vector.memset(tau[0][:], -1.0e29), [], [tau[0].b])
                return
            H_ = HhA if on_act else Hh
            bb = bisA if on_act else bis
            k.op('dve', lambda: nc.vector.max(out=m8[:], in_=sc_[:, 0:NK]), [sc_.b], [m8.b])
            k.op('dve', lambda: nc.vector.tensor_reduce(out=bb[:, 5:6], in_=sc_[:, 0:256], axis=AX.X, op=ALU.min),
                 [sc_.b], [bb.b])
            tt('dve', bb[:, 6:7], m8[:, 0:1], bb[:, 5:6], ALU.subtract, [m8.b, bb.b], [bb.b])
            tt('dve', H_[:], pw2[:], bb[:, 6:7].to_broadcast([128, 32]), ALU.mult, [pw2.b, bb.b], [H_.b])
            if on_act:
                ts('dve', negHA[:], HhA[:], -1.0, ALU.mult, [HhA.b], [negHA.b])
                ts('dve', bisA[:, 0:1], bisA[:, 5:6], -1.0, ALU.mult, [bisA.b, HhA.b], [bisA.b], s2=HhA[:, 0:1], op1=ALU.subtract)
            else:
                tt('dve', bis[:, 2:3], bis[:, 5:6], Hh[:, 0:1], ALU.add, [bis.b, Hh.b], [bis.b])

        def gen_chain_dve(j):
            NK = 256 * (j + 1)
            sc_ = score[j % 2]
            ta = tau[j % 2]
            if j == 0:
                return
            c = max(64, (int(0.46 * NK) // 64) * 64)
            nA = NK - c
            thr = 255.5 - nA / 2.0
            cp('dve', midT[:], bis[:, 2:3], [bis.b], [midT.b])
            for it_ in range(KBIS):
                act(cjunk[:, 0:nA], sc_[:, c:NK], AF.Sign, [sc_.b, midT.b], [cjunk.b, sgA.b], scale=-1.0, bias=midT[:, 0:1],
                    accum_out=sgA[:, 0:1])
                k.op('dve', lambda: nc.vector.tensor_scalar(out=msel[:, 0:c], in0=sc_[:, 0:c], scalar1=midT[:, 0:1],
                                                            scalar2=0.0, op0=ALU.is_ge, op1=ALU.add,
                                                            accum_out=cntD[:, 0:1]), [sc_.b, midT.b], [msel.b, cntD.b])
                ts('dve', tcm[:, 0:1], sgA[:, 0:1], -0.5, ALU.mult, [sgA.b, cntD.b], [tcm.b], s2=cntD[:, 0:1], op1=ALU.add)
                ts('dve', tcm[:, 1:2], tcm[:, 0:1], float(thr), ALU.is_ge, [tcm.b, Hh.b], [tcm.b], s2=Hh[:, it_:it_ + 1], op1=ALU.mult)
                nxt = it_ + 1 if it_ < KBIS - 1 else it_
                dst = midT if it_ < KBIS - 1 else ta
                ts('dve', dst[:, 0:1], midT[:, 0:1], tcm[:, 1:2], ALU.add, [midT.b, tcm.b, Hh.b], [dst.b], s2=Hh[:, nxt:nxt + 1],
                   op1=ALU.subtract)
                yield

        def gen_chain_act(j):
            NK = 256 * (j + 1)
            sc_ = score[j % 2]
            ta = tau[j % 2]
            for it_ in range(KBIS):
                act(cjunk2[:, 0:NK], sc_[:, 0:NK], AF.Sign, [sc_.b, bisA.b], [cjunk2.b, bisA.b], bias=bisA[:, 0:1],
                    accum_out=bisA[:, 1:2])
                act(bisA[:, 2:3], bisA[:, 1:2], AF.Sign, [bisA.b], [bisA.b], bias=float(NK - 511.5))
                act(bisA[:, 0:1], bisA[:, 2:3], AF.Identity, [bisA.b, negHA.b], [bisA.b], scale=negHA[:, it_ + 1:it_ + 2],
                    bias=bisA[:, 0:1])
                yield
            ts('dve', ta[:], bisA[:, 0:1], -1.0, ALU.mult, [bisA.b, HhA.b], [ta.b], s2=HhA[:, KBIS:KBIS + 1], op1=ALU.subtract)

        def finish_select(j, part=0):
            NK = 256 * (j + 1)
            nkb = 2 * (j + 1)
            sc_ = score[j % 2]
            ta = tau[j % 2]
            MT = MTs[j % 2]
            if part in (0, 1):
                ts('dve', msel[:, 0:NK], sc_[:, 0:NK], ta[:, 0:1], ALU.is_ge, [sc_.b, ta.b], [msel.b], s2=-1.0, op1=ALU.add)
            if part == 1:
                return
            for kb0 in range(0, nkb, 8):
                nb = min(8, nkb - kb0)
                bi = 2 + ((kb0 // 8) % 2)
                for i in range(nb):
                    kb = kb0 + i
                    tr(psb16(bi)[:, i * 128:(i + 1) * 128], msel[:, kb * 128:(kb + 1) * 128], identb[:], [msel.b, identb.b], [PS[bi].b])
                cp('act', MT[:, kb0:kb0 + nb, :], psb16(bi)[:, 0:nb * 128].rearrange("p (k t) -> p k t", k=nb), [PS[bi].b], [MT.b])

        def gen_attn(j):
            nkb = 2 * (j + 1)
            qs = slice(j * 128, (j + 1) * 128)
            bqT = bqj4[j % 4]
            MT = MTs[j % 2]
            units = [(kb, g) for kb in range(nkb) for g in range(2)]

            PSX = [PS[4], PS[5], PS[0], PS[1]]
            LA = 3

            def st_part(u):
                kb, g = units[u]
                pst = PSX[u % 4]
                Pm = Pb[u % 4]
                mm(pst[:, :], bkT[0:64, g, kb * 128:(kb + 1) * 128], bqT[0:64, 4 * g:4 * g + 4, :],
                   [bkT.b, bqT.b], [pst.b], start=True, stop=False)
                for r_ in range(4):
                    mm(pst[:, r_ * 128:(r_ + 1) * 128], bigI[:], MT[:, kb, :], [bigI.b, MT.b], [pst.b], start=False, stop=(r_ == 3))
                act(Pm[:].rearrange("p h t -> p (h t)"), pst[:, :], AF.Exp, [pst.b], [Pm.b], scale=0.125)

            def pv_part(u):
                kb, g = units[u]
                Pm = Pb[u % 4]
                pso = PS[6 + g]
                for r_ in range(4):
                    mm(pso[:, r_ * 65:(r_ + 1) * 65], Pm[:, r_, :], Vp[:, kb, g, :], [Pm.b, Vp.b], [pso.b],
                       start=(kb == 0 and r_ == 0), stop=(kb == nkb - 1), sgc=True)

            for u0 in range(min(LA, len(units))):
                st_part(u0)
            for u in range(len(units)):
                if u + LA < len(units):
                    st_part(u + LA)
                pv_part(u)
                yield
            szb = szbj4[j % 4]
            for g in range(2):
                cp('act', ob[:, 4 * g:4 * g + 4, :].rearrange("p h d -> p (h d)"), PS[6 + g][:, 0:260], [PS[6 + g].b], [ob.b])
            k.op('dve', lambda: nc.vector.reciprocal(out=rden[:], in_=ob[:, :, 64]), [ob.b], [rden.b])
            tt('dve', obn[:], ob[:, :, 0:64], rden[:].unsqueeze(2).to_broadcast([128, 8, 64]), ALU.mult, [ob.b, rden.b], [obn.b])
            tt('dve', obg[:], obn[:].rearrange("p h d -> p (h d)"), szb[:], ALU.mult, [obn.b, szb.b], [obg.b])
            for c in range(4):
                tr(psb16(3)[:, c * 128:(c + 1) * 128], obg[:, c * 128:(c + 1) * 128], identb[:], [obg.b, identb.b], [PS[3].b])
            cp('act', MIXT[:, 4:8, qs], psb16(3)[:, 0:512].rearrange("p (c t) -> p c t", c=4), [PS[3].b], [MIXT.sub(('b', j))])
            yield

        def chain2(*gens):
            for g_ in gens:
                for _ in g_:
                    yield

        def step(gen_, n=1):
            for _ in range(n):
                try:
                    next(gen_)
                except StopIteration:
                    return False
            return True

        def load_iq(j):
            k.dma('sp', iqj[j % 2][:], iqT_d[:, :, j * 128:(j + 1) * 128], writes=[iqj[j % 2].b])

        load_iq(0)
        for j in range(NOWN):
            if j + 1 < NOWN:
                load_iq(j + 1)
            for _ in gen_front(j):
                pass
        def load_blk(j):
            NK_ = 256 * (j + 1)
            k.dma('sp', score[j % 2][:, 0:NK_], score_d[j, :, 0:NK_], reads=[bscore_d[j]], writes=[score[j % 2].b])
            k.dma('sp', bqj4[j % 4][:], bqT_d[:, :, j * 128:(j + 1) * 128], writes=[bqj4[j % 4].b])
            k.dma('sp', szbj4[j % 4][:], szb_d[j, :, :], writes=[szbj4[j % 4].b])

        load_blk(0)
        for j in range(NOWN + 1):
            attn_seq = gen_attn(j - 1) if j >= 1 else iter(())
            if j < NOWN:
                if j + 1 < NOWN:
                    load_blk(j + 1)
                chain_init(j, False)
                ch_ = gen_chain_dve(j)
                n_attn = (4 * j + 2) if j >= 1 else 0
                r_attn = max(1, -(-n_attn // KBIS))
                alive = True
                it_r = 0
                while alive:
                    alive = step(ch_)
                    if it_r < KBIS:
                        n_now = ((it_r + 1) * n_attn) // (KBIS + 3) - (it_r * n_attn) // (KBIS + 3)
                    else:
                        n_now = r_attn
                    it_r += 1
                    if it_r > KBIS:
                        break
                    if n_now > 0:
                        alive = step(attn_seq, n_now) or alive
                    elif j >= 1 and it_r < KBIS:
                        alive = True
                finish_select(j, part=1)
                for _ in attn_seq:
                    pass
                finish_select(j, part=2)
            else:
                for _ in attn_seq:
                    pass
        k.barrier()

    if stop == 'D':
        return done()
    bout = []
    with ExitStack() as sE:
        def R(name, shape, dt=F32):
            return TB(sE.enter_context(nc.sbuf_tensor("r_" + name, list(shape), dt, side="right")), name)

        wo = R("wo", [128, 8, 1024], BF16)
        wgt = R("wgt", [128, 8, 1024], BF16)
        wpl = R("wpl", [128, 2, 1024], BF16)
        bgB = R("bgB", [128, 1024])
        k.dma('sp', bgB[:], bgate_d[0:1, :].partition_broadcast(128), writes=[bgB.b])
        wstE = [R("wstE0", [128, 8, 512]), R("wstE1", [128, 8, 512])]
        specs = []
        for half in range(2):
            specs += [(wout_d, half, None, wo, 8), (wg_d, half, n2t, wgt, 8), (wple_d, half, None, wpl, 2)]
        for i_, (w_d_, half, gain_, dst_, kc_) in enumerate(specs):
            st_ = wstE[i_ % 2]
            cs = slice(half * 512, (half + 1) * 512)
            k.dma('sp', st_[:, 0:kc_, :], w_d_.rearrange("(c p) n -> p c n", p=128)[:, :, cs], writes=[st_.b])
            if gain_ is not None:
                for c_ in range(kc_):
                    if c_ % 2 == 0:
                        ts('dve', dst_[:, c_, cs], st_[:, c_, :], gain_[:, c_:c_ + 1], ALU.mult, [st_.b, gain_.b], [dst_.sub((half, c_))])
                    else:
                        act(dst_[:, c_, cs], st_[:, c_, :], AF.Identity, [st_.b, gain_.b], [dst_.sub((half, c_))], scale=gain_[:, c_:c_ + 1])
            else:
                hk = kc_ // 2
                cp('dve', dst_[:, 0:hk, cs], st_[:, 0:hk, :], [st_.b], [dst_.sub((half, 0))])
                cp('act', dst_[:, hk:kc_, cs], st_[:, hk:kc_, :], [st_.b], [dst_.sub((half, 1))])

        def wall(tb_):
            return [tb_.b] + list(tb_.subs.values())
        xo = [R("xo0", [128, 1024]), R("xo1", [128, 1024])]
        pin = [R("pin0", [128, 256]), R("pin1", [128, 256])]
        pbf2 = [R("pbf0", [128, 256], BF16), R("pbf1", [128, 256], BF16)]
        pT2 = [R("pT0", [128, 2, 128], BF16), R("pT1", [128, 2, 128], BF16)]
        x1 = [R("x10", [128, 1024]), R("x11", [128, 1024])]
        junk2 = R("junk2", [128, 1024])
        xn22 = [R("xn20", [128, 1024], BF16), R("xn21", [128, 1024], BF16)]
        h2T2 = [R("h2T0", [128, 8, 128], BF16), R("h2T1", [128, 8, 128], BF16)]
        gt2 = [R("gt0", [128, 1024]), R("gt1", [128, 1024])]
        yt = [R("yt0", [128, 1024]), R("yt1", [128, 1024])]
        st2 = R("st2", [128, 4])

        def e_gen(j):
            qs = slice(j * 128, (j + 1) * 128)
            xt = xo[j % 2]; pt_ = pin[j % 2]; xx = x1[j % 2]; pbf = pbf2[j % 2]; pT = pT2[j % 2]
            xn2 = xn22[j % 2]; h2T = h2T2[j % 2]; gt = gt2[j % 2]; yy = yt[j % 2]
            so = 2 * (j % 2); sb = st2.sub(j % 2)
            k.dma('sp', xt[:], xo_d[qs, :], writes=[xt.b])
            k.dma('sp', pt_[:], po_d[qs, :], writes=[pt_.b])
            for hf in range(2):
                pb = PS[hf]
                for kc in range(8):
                    mm(pb[:, :], MIXT[:, kc, qs], wo[:, kc, hf * 512:(hf + 1) * 512],
                       [MIXT.sub(('a', j)), MIXT.sub(('b', j))] + wall(wo), [pb.b], start=(kc == 0), stop=(kc == 7))
            cp('dve', pbf[:], pt_[:], [pt_.b], [pbf.b])
            yield
            for hf in range(2):
                tt('dve', xx[:, hf * 512:(hf + 1) * 512], PS[hf][:, :], xt[:, hf * 512:(hf + 1) * 512], ALU.add, [PS[hf].b, xt.b], [xx.b])
            for c in range(2):
                tr(psb16(3)[:, c * 128:(c + 1) * 128], pbf[:, c * 128:(c + 1) * 128], identb[:], [pbf.b, identb.b], [PS[3].b])
            yield
            act(junk2[:], xx[:], AF.Square, [xx.b], [junk2.b, sb], accum_out=st2[:, so:so + 1])
            cp('act', pT[:], psb16(3)[:, 0:256].rearrange("p (c t) -> p c t", c=2), [PS[3].b], [pT.b])
            yield
            rstd_from_ssq(st2[:, so + 1:so + 2], st2[:, so:so + 1], 1, 1.0 / D_MODEL, [sb], [sb])
            yield
            ts('dve', xn2[:], xx[:], st2[:, so + 1:so + 2], ALU.mult, [xx.b, sb], [xn2.b])
            yield
            for fc in range(8):
                tr(psb16(2)[:, fc * 128:(fc + 1) * 128], xn2[:, fc * 128:(fc + 1) * 128], identb[:], [xn2.b, identb.b], [PS[2].b])
            yield
            cp('act', h2T[:], psb16(2).rearrange("p (c t) -> p c t", c=8), [PS[2].b], [h2T.b])
            yield
            for hf in range(2):
                pg = PS[4 + hf]
                for fc in range(8):
                    mm(pg[:, :], h2T[:, fc, :], wgt[:, fc, hf * 512:(hf + 1) * 512], [h2T.b] + wall(wgt), [pg.b],
                       start=(fc == 0), stop=(fc == 7))
            yield
            for hf in range(2):
                tt('dve', gt[:, hf * 512:(hf + 1) * 512], PS[4 + hf][:, :], bgB[:, hf * 512:(hf + 1) * 512], ALU.add, [PS[4 + hf].b, bgB.b], [gt.b])
            yield
            act(gt[:], gt[:], AF.Sigmoid, [gt.b], [gt.b])
            for hf in range(2):
                pp_ = PS[6 + hf]
                for c in range(2):
                    mm(pp_[:, :], pT[:, c, :], wpl[:, c, hf * 512:(hf + 1) * 512], [pT.b] + wall(wpl), [pp_.b],
                       start=(c == 0), stop=(c == 1))
            yield
            for hf in range(2):
                tt('dve', yy[:, hf * 512:(hf + 1) * 512], PS[6 + hf][:, :], gt[:, hf * 512:(hf + 1) * 512], ALU.mult, [PS[6 + hf].b, gt.b], [yy.b])
            yield
            tt('pool', yy[:], yy[:], xx[:], ALU.add, [yy.b, xx.b], [yy.b])
            bo = Buf()
            k.dma('pool', y_d[qs, :], yy[:], reads=[yy.b], writes=[bo])
            bout.append(bo)
            yield
        run_pipelined((e_gen(j) for j in range(NOWN)), depth=2)
        k.barrier()
    k.finish(bout)
    return nc, dbg_out, k


def _consts():
    p = np.arange(128)
    same = (p[:, None] // 64) == (p[None, :] // 64)
    ident = np.eye(128, dtype=np.float32)
    tribd = (same & (p[:, None] <= p[None, :])).astype(np.float32)
    blk = same.astype(np.float32)
    l0 = np.zeros((128, 128), np.float32); l0[0:64, :] = 1.0
    l1 = np.zeros((128, 128), np.float32); l1[64:128, :] = 1.0
    negns = np.where(same & (p[:, None] <= p[None, :]), 0.0, NEGM).astype(np.float32)
    negst = np.where(same & (p[None, :] < p[:, None]), 0.0, NEGM).astype(np.float32)
    ones = np.ones((128, 128), np.float32)
    return np.concatenate([ident, tribd, blk, l0, l1, negns, negst, ones, -ones], axis=1)


def make_in_maps(inputs):
    f = lambda a: np.ascontiguousarray(np.asarray(a, dtype=np.float32))
    x = f(inputs["x"]); p = f(inputs["p"])
    cst = _consts()
    tril = np.where(np.arange(128)[None, :] <= np.arange(128)[:, None], 0.0, BIGNEG).astype(np.float32)
    shared = {
        "w_in": f(inputs["w_in"][0]), "w_out": f(inputs["w_out"][0]), "w_ple": f(inputs["w_ple"][0]),
        "w_gate": f(inputs["w_ple_gate"][0]),
        "n1": f(inputs["attn_norm_w"][0].reshape(8, 128).T), "n2": f(inputs["ple_gate_norm_w"][0].reshape(8, 128).T),
        "cw": f(inputs["conv_w"][0].reshape(4, 12, 128).transpose(2, 1, 0).reshape(128, 48)),
        "alog": f(inputs["a_log"][0].reshape(1, 4)), "dtb": f(inputs["dt_bias"][0].reshape(1, 4)),
        "aon": f(inputs["a_out_norm_w"][0].reshape(1, 128)), "gq": f(inputs["b_q_norm_w"][0].reshape(1, 64)),
        "gk": f(inputs["b_k_norm_w"][0].reshape(1, 64)), "bgate": f(inputs["b_ple_gate"][0].reshape(1, 1024)),
        "cst": cst,
    }
    maps = []
    for c in range(8):
        b, half = c // 2, c % 2
        xb = x[b]
        own = xb.reshape(NT, 128, D_MODEL)[half::2].reshape(NOWN * 128, D_MODEL)
        po = p[0, b].reshape(NT, 128, 256)[half::2].reshape(NOWN * 128, 256)
        if half == 0:
            cm = np.concatenate([tril, np.full((128, 128), BIGNEG, np.float32)], axis=1)
        else:
            cm = np.concatenate([np.zeros((128, 128), np.float32), tril], axis=1)
        sel = np.zeros((128, 2), np.float32); sel[:, half] = 1.0
        m = dict(shared)
        m.update({"xb": np.ascontiguousarray(xb), "xo": np.ascontiguousarray(own), "po": np.ascontiguousarray(po),
                  "cm": np.ascontiguousarray(cm), "sel": sel})
        maps.append(m)
    return maps


_CACHE = {}


def kernel(**inputs):
    if "nc" not in _CACHE:
        _CACHE["nc"] = build()[0]
    nc = _CACHE["nc"]
    maps = make_in_maps(inputs)
    res = run_bass_kernel_spmd(nc, maps, core_ids=list(range(8)))
    out = np.empty((4, NT, 128, D_MODEL), np.float32)
    for c in range(8):
        b, half = c // 2, c % 2
        out[b, half::2] = np.asarray(res.results[c]["y"], dtype=np.float32).reshape(NOWN, 128, D_MODEL)
    return out.reshape(4, SEQ, D_MODEL)
```

```python
import numpy as np
import ml_dtypes
from contextlib import ExitStack
import concourse.bass as bass
import concourse.mybir as mybir
from concourse.bass_utils import run_bass_kernel_spmd

F32 = mybir.dt.float32
BF16 = mybir.dt.bfloat16
AF = mybir.ActivationFunctionType
ALU = mybir.AluOpType
AX = mybir.AxisListType

D_MODEL = 1024
SEQ = 4096
NT = SEQ // 128
NOWN = 16
EPS = 1e-6
IN_WIDTH = 4496
NEGM = -30000.0
BIGNEG = -1.0e30

ENGS = ['pe', 'dve', 'act', 'pool', 'sp']
SAME_ENGINE_SYNC = ('dve', 'act', 'pool')


class Buf:
    __slots__ = ('name', 'last_w', 'readers', 'excl')

    def __init__(self, name=''):
        self.name = name
        self.excl = False
        self.last_w = []
        self.readers = {}


class TB:
    def __init__(self, t, name=''):
        self.t = t
        self.b = Buf(name)
        self.subs = {}

    def sub(self, key):
        if key not in self.subs:
            self.subs[key] = Buf()
        return self.subs[key]

    def __getitem__(self, idx):
        return self.t[idx]


class K:
    def __init__(self, nc, n_dma_sems=(('sp', 24), ('act', 8), ('pool', 8))):
        self.nc = nc
        self.eng = dict(pe=nc.tensor, dve=nc.vector, act=nc.scalar, pool=nc.gpsimd, sp=nc.sync)
        self.sem = {e: nc.alloc_semaphore(name='c_' + e) for e in ENGS}
        self.cnt = {e: 0 for e in ENGS}
        self.known = {e: {} for e in ENGS}
        self.dsem = {}
        self.dnext = {}
        for e, n in n_dma_sems:
            self.dsem[e] = [[nc.alloc_semaphore(name='d_%s_%d' % (e, i)), 0] for i in range(n)]
            self.dnext[e] = 0
        self.ninst = 0

    def _wait(self, e, tok):
        if tok is None:
            return
        kind, key, val = tok
        if kind == 'c' and key == e and e not in SAME_ENGINE_SYNC:
            return
        kk = (kind, key) if kind == 'c' else (kind, id(key))
        if self.known[e].get(kk, 0) >= val:
            return
        semh = self.sem[key] if kind == 'c' else key[0]
        self.eng[e].wait_ge(semh, val)
        self.known[e][kk] = val

    def _deps(self, e, reads, writes, is_dma=False):
        for b in reads:
            for t in b.last_w:
                self._wait(e, t)
        for b in writes:
            if not (is_dma and not b.readers and all(t[0] == 'd' for t in b.last_w)):
                for t in b.last_w:
                    self._wait(e, t)
            for t in list(b.readers.values()):
                self._wait(e, t)

    def _commit(self, tok, reads, writes):
        for b in writes:
            if tok[0] == 'd' and not b.readers and b.last_w and all(t[0] == 'd' for t in b.last_w):
                b.last_w = b.last_w + [tok]
            else:
                b.last_w = [tok]
            b.readers = {}
        kind, key, val = tok
        rk = (kind, key if kind == 'c' else id(key))
        for b in reads:
            if b in writes:
                continue
            b.readers[rk] = tok

    def op(self, e, fn, reads=(), writes=()):
        if any(b.excl for b in reads):
            writes = list(writes) + [b for b in reads if b.excl and b not in writes]
            reads = [b for b in reads if not b.excl]
        self._deps(e, reads, writes)
        inst = fn()
        self.cnt[e] += 1
        inst.then_inc(self.sem[e], 1)
        self._commit(('c', e, self.cnt[e]), reads, writes)
        self.ninst += 1
        return inst

    def dma(self, e, out, in_, reads=(), writes=(), **kw):
        self._deps(e, reads, writes, is_dma=True)
        pool = self.dsem[e]
        i = self.dnext[e]
        self.dnext[e] = (i + 1) % len(pool)
        ent = pool[i]
        if ent[1] > 0:
            self._wait(e, ('d', ent, ent[1]))
        inst = self.eng[e].dma_start(out=out, in_=in_, **kw)
        ent[1] += 16
        inst.then_inc(ent[0], 16)
        self._commit(('d', ent, ent[1]), reads, writes)
        self.ninst += 1
        return inst

    def barrier(self):
        for e in ENGS:
            for f in ENGS:
                if f != e and self.cnt[f] > 0:
                    self._wait(e, ('c', f, self.cnt[f]))
            for q, pool in self.dsem.items():
                for ent in pool:
                    if ent[1] > 0:
                        self._wait(e, ('d', ent, ent[1]))

    def finish(self, bufs, e='sp'):
        for b in bufs:
            for t in b.last_w:
                self._wait(e, t)
            for t in list(b.readers.values()):
                self._wait(e, t)


def run_pipelined(gens, depth=2):
    active = []
    it = iter(gens)
    more = True
    while True:
        if more and len(active) < depth:
            try:
                active.append(next(it))
            except StopIteration:
                more = False
        if not active:
            break
        for g_ in list(active):
            try:
                next(g_)
            except StopIteration:
                active.remove(g_)


def build(debug=(), stop=None):
    nc = bass.Bass("TRN2", target_bir_lowering=False)
    k = K(nc)

    def din(name, shape, dt=F32):
        return nc.dram_tensor(name, list(shape), dt, kind="ExternalInput").ap()

    xb_d = din("xb", [SEQ, D_MODEL])
    xo_d = din("xo", [NOWN * 128, D_MODEL])
    po_d = din("po", [NOWN * 128, 256])
    win_d = din("w_in", [D_MODEL, IN_WIDTH])
    wout_d = din("w_out", [1024, 1024])
    wple_d = din("w_ple", [256, 1024])
    wg_d = din("w_gate", [1024, 1024])
    n1_d = din("n1", [128, 8])
    n2_d = din("n2", [128, 8])
    cw_d = din("cw", [128, 48])
    alog_d = din("alog", [1, 4])
    dtb_d = din("dtb", [1, 4])
    aon_d = din("aon", [1, 128])
    gq_d = din("gq", [1, 64])
    gk_d = din("gk", [1, 64])
    bgate_d = din("bgate", [1, 1024])
    cst_d = din("cst", [128, 9 * 128])
    cm_d = din("cm", [128, 256])
    sel_d = din("sel", [128, 2])
    y_d = nc.dram_tensor("y", [NOWN * 128, D_MODEL], F32, kind="ExternalOutput").ap()
    qkvc_d = nc.dram_tensor("qkvc", [12, 128, SEQ], F32, kind="Internal").ap()
    bqkvc = [[Buf() for _ in range(8)] for _ in range(12)]
    sza_d = nc.dram_tensor("sza_s", [NOWN, 128, 512], BF16, kind="Internal").ap()
    szb_d = nc.dram_tensor("szb_s", [NOWN, 128, 512], BF16, kind="Internal").ap()
    bqT_d = nc.dram_tensor("bqT_s", [64, 8, NOWN * 128], BF16, kind="Internal").ap()
    iqT_d = nc.dram_tensor("iqT_s", [128, 8, NOWN * 128], BF16, kind="Internal").ap()
    ikT_d = nc.dram_tensor("ikT_s", [128, SEQ], BF16, kind="Internal").ap()
    bkT_d = nc.dram_tensor("bkT_s", [64, 2, SEQ], BF16, kind="Internal").ap()
    Vp_d = nc.dram_tensor("Vp_s", [128, NT, 130], BF16, kind="Internal").ap()
    score_d = nc.dram_tensor("score_s", [NOWN, 128, SEQ], F32, kind="Internal").ap()
    dbg_out = {}

    def done():
        k.barrier()
        return nc, dbg_out, k

    def dbg(name, shape):
        if name in debug:
            dbg_out[name] = nc.dram_tensor("dbg_" + name, list(shape), F32, kind="ExternalOutput").ap()
            return dbg_out[name]
        return None

    def P(name, shape, dt=F32):
        return TB(nc.alloc_sbuf_tensor("s_" + name, list(shape), dt), name)

    def act(out, in_, func, reads, writes, **kw):
        return k.op('act', lambda: nc.scalar.activation(out=out, in_=in_, func=func, **kw), reads, writes)

    def tt(e, out, in0, in1, op, reads, writes):
        eng = nc.vector if e == 'dve' else nc.gpsimd
        return k.op(e, lambda: eng.tensor_tensor(out=out, in0=in0, in1=in1, op=op), reads, writes)

    def ts(e, out, in0, s1, op0, reads, writes, s2=None, op1=None, **kw):
        eng = nc.vector if e == 'dve' else nc.gpsimd
        if op1 is None:
            return k.op(e, lambda: eng.tensor_scalar(out=out, in0=in0, scalar1=s1, scalar2=None, op0=op0, **kw), reads, writes)
        return k.op(e, lambda: eng.tensor_scalar(out=out, in0=in0, scalar1=s1, scalar2=s2, op0=op0, op1=op1, **kw), reads, writes)

    def stt(out, in0, scalar, in1, op0, op1, reads, writes):
        return k.op('dve', lambda: nc.vector.scalar_tensor_tensor(out=out, in0=in0, scalar=scalar, in1=in1, op0=op0, op1=op1), reads, writes)

    def mm(out, lhsT, rhs, reads, writes, start=True, stop=True, sgc=False):
        if sgc:
            return k.op('pe', lambda: nc.tensor.matmul(out, lhsT=lhsT, rhs=rhs, start=start, stop=stop,
                                                        skip_group_check=True), reads, writes)
        return k.op('pe', lambda: nc.tensor.matmul(out, lhsT=lhsT, rhs=rhs, start=start, stop=stop), reads, writes)

    def tr(out, in_, ident, reads, writes):
        return k.op('pe', lambda: nc.tensor.transpose(out=out, in_=in_, identity=ident), reads, writes)

    def cp(e, out, in_, reads, writes):
        if e == 'act':
            return k.op('act', lambda: nc.scalar.copy(out=out, in_=in_), reads, writes)
        eng = nc.vector if e == 'dve' else nc.gpsimd
        return k.op(e, lambda: eng.tensor_copy(out=out, in_=in_), reads, writes)

    PS = [TB(nc.alloc_psum_tensor("ps%d" % i, [128, 512], F32), "ps%d" % i) for i in range(8)]
    for p_ in PS:
        p_.b.excl = True

    def psb16(i):
        return PS[i].t[:].bitcast(BF16)

    cst = P("cst", [128, 9 * 128])
    k.dma('sp', cst[:], cst_d[:, :], writes=[cst.b])
    ident = cst[:, 0:128]
    tribd = cst[:, 128:256]
    blk = cst[:, 256:384]
    l0 = cst[:, 384:512]
    l1 = cst[:, 512:640]
    negns = cst[:, 640:768]
    negst = cst[:, 768:896]
    ones = cst[:, 896:1024]
    negones = cst[:, 1024:1152]
    identb = P("identb", [128, 128], BF16)
    cp('dve', identb[:], ident, [cst.b], [identb.b])
    n1t = P("n1t", [128, 8]); k.dma('sp', n1t[:], n1_d[:, :], writes=[n1t.b])
    n2t = P("n2t", [128, 8]); k.dma('sp', n2t[:], n2_d[:, :], writes=[n2t.b])
    cwt = P("cwt", [128, 48]); k.dma('sp', cwt[:], cw_d[:, :], writes=[cwt.b])
    alogB = P("alogB", [128, 4]); k.dma('sp', alogB[:], alog_d[0:1, :].partition_broadcast(128), writes=[alogB.b])
    dtbB = P("dtbB", [128, 4]); k.dma('sp', dtbB[:], dtb_d[0:1, :].partition_broadcast(128), writes=[dtbB.b])
    aonB = P("aonB", [128, 128]); k.dma('sp', aonB[:], aon_d[0:1, :].partition_broadcast(128), writes=[aonB.b])
    gqB = P("gqB", [128, 64]); k.dma('sp', gqB[:], gq_d[0:1, :].partition_broadcast(128), writes=[gqB.b])
    gkB = P("gkB", [128, 64]); k.dma('sp', gkB[:], gk_d[0:1, :].partition_broadcast(128), writes=[gkB.b])
    cmt = P("cmt", [128, 256]); k.dma('sp', cmt[:], cm_d[:, :], writes=[cmt.b])
    selt = P("selt", [128, 2]); k.dma('sp', selt[:], sel_d[:, :], writes=[selt.b])

    iw = P("iw", [128, NOWN, 8])
    BA = P("BA", [128, NT, 8])
    beta = P("beta", [128, 128])
    gg = P("gg", [128, 128])
    eg = P("eg", [128, 128])
    ekd = P("ekd", [128, 128])
    bkg = P("bkg", [128, 128])
    eglb = [P("eglb0", [128, 128]), P("eglb1", [128, 128])]
    sm = P("sm", [128, 64])

    wst = [None, None]
    wbf = [None, None]
    wctr = [0]

    def alloc_w(Rf, tag):
        w_ = Rf("wst" + tag, [128, 8, 512])
        wst[0] = wst[1] = w_
        wbf[0] = Rf("wbf0" + tag, [128, 8, 512], BF16)
        wbf[1] = Rf("wbf1" + tag, [128, 8, 512], BF16)

    def load_w(w_d, ranges, gain, kc=8):
        i = wctr[0] % 2
        wctr[0] += 1
        st, wb = wst[i], wbf[i]
        src = w_d.rearrange("(c p) n -> p c n", p=128)
        off = 0
        for (a, b_) in ranges:
            k.dma('sp', st[:, 0:kc, off:off + (b_ - a)], src[:, :, a:b_], writes=[st.b])
            off += b_ - a
        if gain is not None:
            for c_ in range(kc):
                ts('dve', wb[:, c_, 0:off], st[:, c_, 0:off], gain[:, c_:c_ + 1], ALU.mult, [st.b, gain.b], [wb.b])
        else:
            cp('dve', wb[:, 0:kc, 0:off], st[:, 0:kc, 0:off], [st.b], [wb.b])
        return wb, off

    def rstd_from_ssq(dst, src, n, scale, reads, writes):
        act(dst, src, AF.Ln, reads, writes, scale=scale, bias=EPS)
        act(dst, dst, AF.Exp, writes, writes, scale=-0.5)

    sO1 = ExitStack()
    hTo = TB(sO1.enter_context(nc.sbuf_tensor("hTo", [128, 8, NOWN * 128], BF16, side="right")), "hTo")
    with ExitStack() as sAB:
        def R(name, shape, dt=F32):
            return TB(sAB.enter_context(nc.sbuf_tensor("r_" + name, list(shape), dt, side="right")), name)

        hT = R("hT", [128, 8, SEQ], BF16)
        alloc_w(R, "ab")
        stg_ik = [R("stg_ik0", [128, 512], BF16), R("stg_ik1", [128, 512], BF16)]
        stg_bk = [R("stg_bk0", [64, 2, 128], BF16), R("stg_bk1", [64, 2, 128], BF16)]
        stg_v = [R("stg_v0", [128, 2, 65], BF16), R("stg_v1", [128, 2, 65], BF16)]
        xin = [R("xin0", [128, 1024]), R("xin1", [128, 1024])]
        junk = R("junk", [128, 1024])
        xn = [R("xn0", [128, 1024], BF16), R("xn1", [128, 1024], BF16)]
        ssq = R("ssq", [128, 4])

        def phase_a(x_d, ntiles, dst, grp):
            def tile_gen(t):
                xt = xin[t % 2]
                xnt = xn[t % 2]
                so = 2 * (t % 2)
                sb = ssq.sub(t % 2)
                k.dma('sp', xt[:], x_d[t * 128:(t + 1) * 128, :], writes=[xt.b])
                act(junk[:], xt[:], AF.Square, [xt.b], [junk.b, sb], accum_out=ssq[:, so:so + 1])
                yield
                rstd_from_ssq(ssq[:, so + 1:so + 2], ssq[:, so:so + 1], 1, 1.0 / D_MODEL, [sb], [sb])
                yield
                ts('dve', xnt[:], xt[:], ssq[:, so + 1:so + 2], ALU.mult, [xt.b, sb], [xnt.b])
                yield
                pb = PS[t % 2]
                for fc in range(8):
                    tr(psb16(t % 2)[:, fc * 128:(fc + 1) * 128], xnt[:, fc * 128:(fc + 1) * 128], identb[:],
                       [xnt.b, identb.b], [pb.b])
                yield
                cp('act' if t % 2 == 0 else 'dve', dst[:, :, t * 128:(t + 1) * 128],
                   psb16(t % 2).rearrange("p (c t) -> p c t", c=8), [pb.b], [dst.sub(t // grp)])
                yield
            run_pipelined((tile_gen(t) for t in range(ntiles)), depth=2)

        phase_a(xb_d, NT, hT, 4)
        hsel = [R("hsel0", [128, 8, 128], BF16), R("hsel1", [128, 8, 128], BF16)]
        for j in range(NOWN):
            tmp_ = hsel[j % 2]
            ts('dve', tmp_[:], hT[:, :, (2 * j) * 128:(2 * j + 1) * 128], selt[:, 0:1], ALU.mult,
               [hT.sub((2 * j) // 4), selt.b], [tmp_.b])
            stt(hTo[:, :, j * 128:(j + 1) * 128], hT[:, :, (2 * j + 1) * 128:(2 * j + 2) * 128], selt[:, 1:2], tmp_[:],
                ALU.mult, ALU.add, [hT.sub((2 * j + 1) // 4), selt.b, tmp_.b], [hTo.sub(j)])
        if stop == 'A':
            return done()

        d = dbg("hT", [128, 8 * 512])
        if d is not None:
            tmp = R("dbg_hT", [128, 8, 512])
            cp('dve', tmp[:], hT[:, :, 0:512], [hT.sub(0)], [tmp.b])
            k.dma('sp', d[:, :], tmp[:].rearrange("p c t -> p (c t)"), reads=[tmp.b], writes=[Buf()])

        pre = [R("pre%d" % i, [128, 528]) for i in range(3)]
        cacc = [R("cacc%d" % i, [128, 512]) for i in range(3)]
        cout = [R("cout%d" % i, [128, 512]) for i in range(3)]

        def b1_gen(it, wb, cc, ch, g):
            pf = pre[it % 3]
            pfn = pre[(it + 1) % 3]
            pb = PS[2 + (it % 3)]
            ca = cacc[it % 3]
            co = cout[it % 3]
            for fc in range(8):
                mm(pb[:, :], wb[:, fc, cc * 128:(cc + 1) * 128], hT[:, fc, g * 512:(g + 1) * 512],
                   [wb.b, hT.sub(g)], [pb.b], start=(fc == 0), stop=(fc == 7))
            if g == 0:
                k.op('dve', lambda pf=pf: nc.vector.memset(pf[:, 0:8], 0.0), [], [pf.b])
            yield
            cp('act', pf[:, 8:520], pb[:, :], [pb.b], [pf.b])
            act(ca[:], pb[:, :], AF.Identity, [pb.b, cwt.b], [ca.b], scale=cwt[:, ch * 4 + 3:ch * 4 + 4])
            if g < 7:
                cp('pool', pfn[:, 5:8], pf[:, 517:520], [pf.b], [pfn.b])
            yield
            rd = [pf.b, cwt.b]
            for j in range(3):
                stt(ca[:], pf[:, 5 + j:5 + j + 512], cwt[:, ch * 4 + j:ch * 4 + j + 1], ca[:],
                    ALU.mult, ALU.add, rd + [ca.b], [ca.b])
            yield
            act(co[:], ca[:], AF.Silu, [ca.b], [co.b])
            k.dma('act', qkvc_d[ch, :, g * 512:(g + 1) * 512], co[:], reads=[co.b], writes=[bqkvc[ch][g]])
            yield

        def b1_all():
            it = 0
            for wg in range(3):
                wb, _ = load_w(win_d, [(wg * 512, (wg + 1) * 512)], n1t)
                for cc in range(4):
                    for g in range(8):
                        yield b1_gen(it, wb, cc, wg * 4 + cc, g)
                        it += 1
        run_pipelined(b1_all(), depth=3)

        if stop == 'B1':
            return done()
        wb, _ = load_w(win_d, [(4360, 4488)], n1t)
        def b2_gen(g, wb):
            pb = PS[2 + (g % 2)]
            for fc in range(8):
                mm(pb[:, :], wb[:, fc, 0:128], hT[:, fc, g * 512:(g + 1) * 512], [wb.b, hT.sub(g)], [pb.b],
                   start=(fc == 0), stop=(fc == 7))
            yield
            sg = stg_ik[g % 2]
            cp('act', sg[:], pb[:, :], [pb.b], [sg.b])
            k.dma('act', ikT_d[:, g * 512:(g + 1) * 512], sg[:], reads=[sg.b], writes=[Buf()])
            yield
        run_pipelined((b2_gen(g, wb) for g in range(8)), depth=2)

        wb, ncol = load_w(win_d, [(2048, 2056), (2568, 2824)], n1t)
        knt = [R("knt0", [128, 128], BF16), R("knt1", [128, 128], BF16)]
        for sv in stg_v:
            k.op('pool', lambda sv=sv: nc.gpsimd.memset(sv[:, :, 64:65], 1.0), [], [sv.b])
        def b3_gen(t, wb):
            pb = PS[2 + (t % 2)]
            for fc in range(8):
                mm(pb[:, 0:264], hT[:, fc, t * 128:(t + 1) * 128], wb[:, fc, 0:264], [wb.b, hT.sub(t // 4)], [pb.b],
                   start=(fc == 0), stop=(fc == 7))
            yield
            cp('act', BA[:, t, :], pb[:, 0:8], [pb.b], [BA.sub(t)])
            smb = sm.sub(('b3', t % 2))
            so = (t % 2) * 8
            for g in range(2):
                act(junk[:, 0:64], pb[:, 8 + g * 64:8 + (g + 1) * 64], AF.Square, [pb.b], [junk.b, smb],
                    accum_out=sm[:, so + g:so + g + 1])
            sv = stg_v[t % 2]
            cp('act', sv[:, :, 0:64], pb[:, 136:264].rearrange("p (g d) -> p g d", g=2), [pb.b], [sv.b])
            k.dma('act', Vp_d[:, t, :], sv[:].rearrange("p g d -> p (g d)"), reads=[sv.b], writes=[Buf()])
            yield
            rstd_from_ssq(sm[:, so + 2:so + 4], sm[:, so:so + 2], 2, 1.0 / 64, [smb], [smb])
            yield
            kt = knt[t % 2]
            for g in range(2):
                stt(kt[:, g * 64:(g + 1) * 64], pb[:, 8 + g * 64:8 + (g + 1) * 64], sm[:, so + 2 + g:so + 3 + g], gkB[:],
                    ALU.mult, ALU.mult, [pb.b, smb, gkB.b], [kt.b])
            yield
            pt = PS[4 + (t % 2)]
            for g in range(2):
                tr(psb16(4 + (t % 2))[0:64, g * 128:(g + 1) * 128], kt[:, g * 64:(g + 1) * 64], identb[:],
                   [kt.b, identb.b], [pt.b])
            yield
            sk = stg_bk[t % 2]
            cp('dve', sk[:], psb16(4 + (t % 2))[0:64, 0:256].rearrange("p (g t) -> p g t", g=2), [pt.b], [sk.b])
            k.dma('pool', bkT_d[:, :, t * 128:(t + 1) * 128], sk[:], reads=[sk.b], writes=[Buf()])
            yield
        run_pipelined((b3_gen(t, wb) for t in range(NT)), depth=2)

        if stop == 'B3':
            return done()
        BAv = BA[:].rearrange("p t (a h) -> p t a h", a=2)
        bv3 = beta[:].rearrange("p (t h) -> p t h", h=4)
        g3 = gg[:].rearrange("p (t h) -> p t h", h=4)
        tA = R("tA", [128, 128]); tA3 = tA[:].rearrange("p (t h) -> p t h", h=4)
        tB_ = R("tB", [128, 128]); tB3 = tB_[:].rearrange("p (t h) -> p t h", h=4)
        nA = R("nA", [128, 4])
        act(beta[:].rearrange("p (t h) -> p t h", h=4), BAv[:, :, 0, :], AF.Sigmoid, [BA.sub(t_) for t_ in range(NT)], [beta.b])
        tt('dve', tA3, BAv[:, :, 1, :], dtbB[:].unsqueeze(1).to_broadcast([128, NT, 4]), ALU.add, [BA.sub(t_) for t_ in range(NT)] + [dtbB.b], [tA.b])
        act(tB_[:], tA[:], AF.Abs, [tA.b], [tB_.b])
        act(tB_[:], tB_[:], AF.Exp, [tB_.b], [tB_.b], scale=-1.0)
        act(tB_[:], tB_[:], AF.Ln, [tB_.b], [tB_.b], bias=1.0)
        stt(tA[:], tA[:], 0.0, tB_[:], ALU.max, ALU.add, [tA.b, tB_.b], [tA.b])
        act(nA[:], alogB[:], AF.Exp, [alogB.b], [nA.b])
        ts('dve', nA[:], nA[:], -1.0, ALU.mult, [nA.b], [nA.b])
        tt('dve', g3, tA3, nA[:].unsqueeze(1).to_broadcast([128, NT, 4]), ALU.mult, [tA.b, nA.b], [gg.b])
        pb = PS[6]
        mm(pb[:, 0:128], tribd, gg[:], [cst.b, gg.b], [pb.b])
        mm(pb[:, 128:256], blk, gg[:], [cst.b, gg.b], [pb.b])
        mm(pb[:, 256:384], l0, gg[:], [cst.b, gg.b], [pb.b])
        mm(pb[:, 384:512], l1, gg[:], [cst.b, gg.b], [pb.b])
        act(eg[:], pb[:, 0:128], AF.Exp, [pb.b], [eg.b])
        cp('dve', tA[:], pb[:, 0:128], [pb.b], [tA.b])
        tt('dve', tB_[:], pb[:, 128:256], tA[:], ALU.subtract, [pb.b, tA.b], [tB_.b])
        act(ekd[:], tB_[:], AF.Exp, [tB_.b], [ekd.b])
        act(eglb[0][:], pb[:, 256:384], AF.Exp, [pb.b], [eglb[0].b])
        act(eglb[1][:], pb[:, 384:512], AF.Exp, [pb.b], [eglb[1].b])
        tt('dve', bkg[:], beta[:], eg[:], ALU.mult, [beta.b, eg.b], [bkg.b])

        for name, tb_ in (("beta", beta), ("gg", gg), ("eg", eg), ("ekd", ekd), ("eglb1", eglb[1])):
            d = dbg(name, [128, 128])
            if d is not None:
                k.dma('sp', d[:, :], tb_[:], reads=[tb_.b], writes=[Buf()])

        k.barrier()
    if stop == 'AB':
        return done()

    with ExitStack() as sB5:
        def R(name, shape, dt=F32):
            return TB(sB5.enter_context(nc.sbuf_tensor("r_" + name, list(shape), dt, side="right")), name)

        alloc_w(R, "b5")
        stg_z = [R("stg_z0", [128, 512], BF16), R("stg_z1", [128, 512], BF16)]
        stg_q = [R("stg_q0", [64, 8, 128], BF16), R("stg_q1", [64, 8, 128], BF16)]
        sqq = R("sqq", [128, 512])
        qn1 = R("qn1", [128, 512])
        qnb = [R("qnb0", [128, 512], BF16), R("qnb1", [128, 512], BF16)]
        stq = R("stq", [128, 16])
        sqq2 = [sqq, R("sqq1", [128, 512])]
        qn12 = [qn1, R("qn11", [128, 512])]

        def z_gen(j, wb, dst_d):
            pb = PS[j % 2]
            for fc in range(8):
                mm(pb[:, :], hTo[:, fc, j * 128:(j + 1) * 128], wb[:, fc, 0:512], [wb.b, hTo.sub(j)], [pb.b],
                   start=(fc == 0), stop=(fc == 7))
            yield
            sg = stg_z[j % 2]
            act(sg[:], pb[:, :], AF.Silu, [pb.b], [sg.b])
            k.dma('act', dst_d[j, :, :], sg[:], reads=[sg.b], writes=[Buf()])
            yield
        for (c0, dst_d) in ((1536, sza_d), (2824, szb_d)):
            wb, _ = load_w(win_d, [(c0, c0 + 512)], n1t)
            run_pipelined((z_gen(j, wb, dst_d) for j in range(NOWN)), depth=2)

        def bq_gen(j, wb):
            pb = PS[2 + (j % 2)]
            sq_s = sqq2[j % 2]; qn_s = qn12[j % 2]
            so = 16 * 0
            stb = stq.sub(j % 2)
            c0 = (j % 2) * 8
            for fc in range(8):
                mm(pb[:, :], hTo[:, fc, j * 128:(j + 1) * 128], wb[:, fc, 0:512], [wb.b, hTo.sub(j)], [pb.b],
                   start=(fc == 0), stop=(fc == 7))
            yield
            act(sq_s[:], pb[:, :], AF.Square, [pb.b], [sq_s.b])
            yield
            k.op('dve', lambda: nc.vector.tensor_reduce(out=stq2[:, c0:c0 + 8], in_=sq_s[:].rearrange("p (h d) -> p h d", h=8),
                                                        axis=AX.X, op=ALU.add), [sq_s.b], [stb])
            yield
            rstd_from_ssq(stq2[:, 16 + c0:16 + c0 + 8], stq2[:, c0:c0 + 8], 8, 1.0 / 64, [stb], [stb])
            yield
            tt('dve', qn_s[:].rearrange("p (h d) -> p h d", h=8), pb[:, :].rearrange("p (h d) -> p h d", h=8),
               stq2[:, 16 + c0:16 + c0 + 8].unsqueeze(2).to_broadcast([128, 8, 64]), ALU.mult, [pb.b, stb], [qn_s.b])
            qb_ = qnb[j % 2]
            tt('dve', qb_[:].rearrange("p (h d) -> p h d", h=8), qn_s[:].rearrange("p (h d) -> p h d", h=8),
               gqB[:].unsqueeze(1).to_broadcast([128, 8, 64]), ALU.mult, [qn_s.b, gqB.b], [qb_.b])
            yield
            pt = PS[4 + (j % 2)]
            for h in range(8):
                tr(psb16(4 + (j % 2))[0:64, h * 128:(h + 1) * 128], qb_[:, h * 64:(h + 1) * 64], identb[:], [qb_.b, identb.b], [pt.b])
            yield
            sq_ = stg_q[j % 2]
            cp('act', sq_[:], psb16(4 + (j % 2))[0:64, :].rearrange("p (h t) -> p h t", h=8), [pt.b], [sq_.b])
            k.dma('act', bqT_d[:, :, j * 128:(j + 1) * 128], sq_[:], reads=[sq_.b], writes=[Buf()])
            yield
        stq2 = R("stq2", [128, 32])
        wb, _ = load_w(win_d, [(2056, 2568)], n1t)
        run_pipelined((bq_gen(j, wb) for j in range(NOWN)), depth=2)
        wb, _ = load_w(win_d, [(4488, 4496)], n1t)
        for j in range(NOWN):
            pb = PS[j % 2]
            for fc in range(8):
                mm(pb[:, 0:8], hTo[:, fc, j * 128:(j + 1) * 128], wb[:, fc, 0:8], [wb.b, hTo.sub(j)], [pb.b],
                   start=(fc == 0), stop=(fc == 7))
            ts('dve', iw[:, j, :], pb[:, 0:8], float(8 ** -0.5 * 128 ** -0.5), ALU.mult, [pb.b], [iw.sub(j)])

        def iq_gen(it, wb, wg, hh, g):
            pb = PS[2 + (it % 2)]
            for fc in range(8):
                mm(pb[:, :], wb[:, fc, hh * 128:(hh + 1) * 128], hTo[:, fc, g * 512:(g + 1) * 512],
                   [wb.b] + [hTo.sub(4 * g + i) for i in range(4)], [pb.b], start=(fc == 0), stop=(fc == 7))
            yield
            sg = stg_z[it % 2]
            cp('act', sg[:], pb[:, :], [pb.b], [sg.b])
            k.dma('act', iqT_d[:, wg * 4 + hh, g * 512:(g + 1) * 512], sg[:], reads=[sg.b], writes=[Buf()])
            yield

        def iq_all():
            it = 0
            for wg in range(2):
                wb, _ = load_w(win_d, [(3336 + wg * 512, 3336 + (wg + 1) * 512)], n1t)
                for hh in range(4):
                    for g in range(4):
                        yield iq_gen(it, wb, wg, hh, g)
                        it += 1
        run_pipelined(iq_all(), depth=2)
        k.barrier()
    sO1.close()
    MIXT = P("MIXT", [128, 8, NOWN * 128], BF16)
    if stop == 'B5':
        return done()

    with ExitStack() as sC:
        def R(name, shape, dt=F32):
            return TB(sC.enter_context(nc.sbuf_tensor("r_" + name, list(shape), dt, side="right")), name)

        def RN_(name, shape, n, dt=F32):
            return [R(name + str(i), shape, dt) for i in range(n)]

        Xin = RN_("Xin", [128, 12, 128], 2)
        SQ = R("SQ", [128, 1024])
        RNt = R("RN", [128, 1024])
        QKn3 = RN_("QKn", [128, 8, 128], 3)
        KD3 = RN_("KD", [128, 4, 128], 3)
        ATT3 = RN_("ATT", [128, 4, 128], 3)
        KBG2 = RN_("KBG", [128, 4, 128], 2); VB2 = RN_("VB", [128, 4, 128], 2)
        TG = R("TG", [128, 4, 128])
        DT = R("DT", [128, 512]); DS = R("DS", [128, 512])
        KKs = R("KKs", [128, 512]); KQs = R("KQs", [128, 512])
        Am2 = RN_("Am", [128, 4, 128], 2); Um2 = RN_("Um", [128, 4, 128], 2)
        Pa2 = RN_("Pa", [128, 4, 128], 2); Qa2 = RN_("Qa", [128, 4, 128], 2)
        Rm2 = RN_("Rm", [128, 4, 128], 2)
        VAL2 = RN_("VAL", [128, 4, 128], 2); KCDT2 = RN_("KCDT", [128, 4, 128], 2)
        VN = R("VN", [128, 4, 128]); AVs = R("AVs", [128, 4, 128])
        Ot = R("Ot", [128, 4, 128]); ON = R("ON", [128, 4, 128]); OS = R("OS", [128, 512])
        MIXb = R("MIXb", [128, 512], BF16)
        szat = [R("szat0", [128, 512], BF16), R("szat1", [128, 512], BF16)]
        S = R("S", [128, 4, 128])
        st4 = R("st4", [128, 8])
        k.op('pool', lambda: nc.gpsimd.memset(S[:], 0.0), [], [S.b])
        ident4 = ident.unsqueeze(1).to_broadcast([128, 4, 128])
        b6, b7 = PS[6], PS[7]

        def v4(ps):
            return ps[:, :].rearrange("p (h c) -> p h c", h=4)

        def f2(tb_):
            return tb_[:].rearrange("p h c -> p (h c)")

        def bc4(t_, sc):
            return t_[:, sc].unsqueeze(2).to_broadcast([128, 4, 128])

        def gen_p1(n):
            X = Xin[n % 2]
            QKn = QKn3[n % 3]; KD = KD3[n % 3]; ATT = ATT3[n % 3]
            KBG = KBG2[n % 2]; VB = VB2[n % 2]; Am = Am2[n % 2]; Um = Um2[n % 2]; Rm = Rm2[n % 2]
            for c3 in range(3):
                k.dma('sp', X[:, c3 * 4:(c3 + 1) * 4, :],
                      qkvc_d[c3 * 4:(c3 + 1) * 4, :, n * 128:(n + 1) * 128].rearrange("c p t -> p c t"),
                      reads=[bqkvc[c3 * 4 + i][n // 4] for i in range(4)], writes=[X.b])
            sc = slice(n * 4, (n + 1) * 4)
            act(SQ[:], X[:, 0:8, :].rearrange("p c t -> p (c t)"), AF.Square, [X.b], [SQ.b])
            mm(b6[:, :], ones, SQ[:, 0:512], [cst.b, SQ.b], [b6.b])
            mm(b7[:, :], ones, SQ[:, 512:1024], [cst.b, SQ.b], [b7.b])
            yield
            act(RNt[:, 0:512], b6[:, :], AF.Ln, [b6.b], [RNt.b], bias=EPS)
            act(RNt[:, 512:1024], b7[:, :], AF.Ln, [b7.b], [RNt.b], bias=EPS)
            act(RNt[:], RNt[:], AF.Exp, [RNt.b], [RNt.b], scale=-0.5)
            yield
            stt(QKn[:, 0:4, :].rearrange("p c t -> p (c t)"), X[:, 0:4, :].rearrange("p c t -> p (c t)"), 128.0 ** -0.5,
                RNt[:, 0:512], ALU.mult, ALU.mult, [X.b, RNt.b], [QKn.b])
            tt('dve', QKn[:, 4:8, :].rearrange("p c t -> p (c t)"), X[:, 4:8, :].rearrange("p c t -> p (c t)"),
               RNt[:, 512:1024], ALU.mult, [X.b, RNt.b], [QKn.b])
            for h in range(4):
                tr(b6[:, h * 128:(h + 1) * 128], QKn[:, 4 + h, :], ident, [QKn.b, cst.b], [b6.b])
            for h in range(4):
                tr(b7[:, h * 128:(h + 1) * 128], X[:, 8 + h, :], ident, [X.b, cst.b], [b7.b])
            tt('dve', TG[:], tribd.unsqueeze(1).to_broadcast([128, 4, 128]), bc4(gg, sc), ALU.mult, [cst.b, gg.b], [TG.b])
            yield
            tt('dve', KBG[:], v4(b6), bc4(bkg, sc), ALU.mult, [b6.b, bkg.b], [KBG.b])
            tt('dve', KD[:], v4(b6), bc4(ekd, sc), ALU.mult, [b6.b, ekd.b], [KD.b])
            tt('dve', VB[:], v4(b7), bc4(beta, sc), ALU.mult, [b7.b, beta.b], [VB.b])
            for h in range(4):
                mm(b6[:, h * 128:(h + 1) * 128], QKn[:, 4 + h, :], QKn[:, 4 + h, :], [QKn.b], [b6.b])
            for h in range(4):
                mm(b7[:, h * 128:(h + 1) * 128], QKn[:, 4 + h, :], QKn[:, h, :], [QKn.b], [b7.b])
            yield
            cp('act', KKs[:], b6[:, :], [b6.b], [KKs.b])
            cp('dve', KQs[:], b7[:, :], [b7.b], [KQs.b])
            for h in range(4):
                o = b6[:, h * 128:(h + 1) * 128]
                mm(o, ones, TG[:, h, :], [cst.b, TG.b], [b6.b], start=True, stop=False)
                mm(o, TG[:, h, :], negones, [cst.b, TG.b], [b6.b], start=False, stop=False)
                mm(o, ident, negns, [cst.b], [b6.b], start=False, stop=True)
            for h in range(4):
                o = b7[:, h * 128:(h + 1) * 128]
                mm(o, TG[:, h, :], ones, [cst.b, TG.b], [b7.b], start=True, stop=False)
                mm(o, negones, TG[:, h, :], [cst.b, TG.b], [b7.b], start=False, stop=False)
                mm(o, ident, negst, [cst.b], [b7.b], start=False, stop=True)
            yield
            act(DT[:], b6[:, :], AF.Exp, [b6.b], [DT.b])
            act(DS[:], b7[:, :], AF.Exp, [b7.b], [DS.b])
            yield
            tt('dve', f2(Am), KKs[:], DS[:], ALU.mult, [KKs.b, DS.b], [Am.b])
            tt('dve', Am[:], Am[:], bc4(beta, sc), ALU.mult, [Am.b, beta.b], [Am.b])
            tt('dve', f2(ATT), KQs[:], DT[:], ALU.mult, [KQs.b, DT.b], [ATT.b])
            for h in range(4):
                tr(b6[:, h * 128:(h + 1) * 128], Am[:, h, :], ident, [Am.b, cst.b], [b6.b])
            yield
            cp('act', f2(Um), b6[:, :], [b6.b], [Um.b])
            stt(Rm[:], v4(b6), -1.0, ident4, ALU.mult, ALU.add, [b6.b, cst.b], [Rm.b])
            if n == 0:
                for name, tb_ in (("ATT0", ATT), ("Am0", Am)):
                    d = dbg(name, [128, 512])
                    if d is not None:
                        k.dma('sp', d[:, :], f2(tb_), reads=[tb_.b], writes=[Buf()])
            yield

        def gen_p2(n):
            KBG = KBG2[n % 2]; VB = VB2[n % 2]; Am = Am2[n % 2]; Um = Um2[n % 2]; Rm = Rm2[n % 2]
            Pa = Pa2[n % 2]; Qa = Qa2[n % 2]; VAL = VAL2[n % 2]; KCDT = KCDT2[n % 2]
            bA, bB, bC = PS[3], PS[4], PS[5]
            Pc, Qc = Um, Am
            Pn, Qn = Pa, Qa
            for stg in range(1, 7):
                if stg >= 2:
                    for h in range(4):
                        mm(bC[:, h * 128:(h + 1) * 128], Qc[:, h, :], Rm[:, h, :], [Qc.b, Rm.b], [bC.b])
                if stg <= 4:
                    for h in range(4):
                        mm(bA[:, h * 128:(h + 1) * 128], Qc[:, h, :], Pc[:, h, :], [Qc.b, Pc.b], [bA.b])
                if stg <= 5:
                    for h in range(4):
                        mm(bB[:, h * 128:(h + 1) * 128], Pc[:, h, :], Qc[:, h, :], [Qc.b, Pc.b], [bB.b])
                yield
                if stg >= 2:
                    tt('dve', f2(Rm), f2(Rm), bC[:, :], ALU.add, [Rm.b, bC.b], [Rm.b])
                if stg <= 4:
                    cp('act', f2(Pn), bA[:, :], [bA.b], [Pn.b])
                if stg <= 5:
                    cp('act' if stg > 4 else 'dve', f2(Qn), bB[:, :], [bB.b], [Qn.b])
                if stg == 1:
                    Pc, Qc, Pn, Qn = Pa, Qa, Um, Am
                else:
                    Pc, Qc, Pn, Qn = Pn, Qn, Pc, Qc
                yield
            for h in range(4):
                mm(bA[:, h * 128:(h + 1) * 128], Rm[:, h, :], VB[:, h, :], [Rm.b, VB.b], [bA.b])
            for h in range(4):
                mm(bB[:, h * 128:(h + 1) * 128], KBG[:, h, :], Rm[:, h, :], [Rm.b, KBG.b], [bB.b])
            yield
            cp('act', f2(VAL), bA[:, :], [bA.b], [VAL.b])
            cp('dve', f2(KCDT), bB[:, :], [bB.b], [KCDT.b])
            if n == 0:
                for name, tb_ in (("T0", Rm), ("VAL0", VAL)):
                    d = dbg(name, [128, 512])
                    if d is not None:
                        k.dma('sp', d[:, :], f2(tb_), reads=[tb_.b], writes=[Buf()])
            yield

        def gen_rec(n):
            QKn = QKn3[n % 3]; KD = KD3[n % 3]; ATT = ATT3[n % 3]; VAL = VAL2[n % 2]; KCDT = KCDT2[n % 2]
            sc = slice(n * 4, (n + 1) * 4)
            bKS, bQS, bAV = PS[0], PS[1], PS[2]
            bSU = PS[0]
            for j in range(2):
                pr = slice(64 * j, 64 * j + 64)
                for h in range(4):
                    mm(bKS[pr, h * 128:(h + 1) * 128], KCDT[:, h, pr], S[:, h, :], [KCDT.b, S.b], [bKS.b])
                for h in range(4):
                    mm(bQS[pr, h * 128:(h + 1) * 128], QKn[:, h, pr], S[:, h, :], [QKn.b, S.b], [bQS.b])
                yield
                tt('dve', VN[pr].rearrange("p h c -> p (h c)"), VAL[pr].rearrange("p h c -> p (h c)"), bKS[pr, :], ALU.subtract,
                   [VAL.b, bKS.b], [VN.b])
                yield
                for h in range(4):
                    mm(bSU[:, h * 128:(h + 1) * 128], KD[pr, h, :], VN[pr, h, :], [KD.b, VN.b], [bSU.b])
                for h in range(4):
                    mm(bAV[pr, h * 128:(h + 1) * 128], ATT[pr, h, pr], VN[pr, h, :], [ATT.b, VN.b], [bAV.b])
                tt('dve', S[:], S[:], eglb[j][:, sc].unsqueeze(2).to_broadcast([128, 4, 128]), ALU.mult, [S.b, eglb[j].b], [S.b])
                yield
                tt('dve', f2(S), f2(S), bSU[:, :], ALU.add, [S.b, bSU.b], [S.b])
                cp('act', AVs[pr].rearrange("p h c -> p (h c)"), bAV[pr, :], [bAV.b], [AVs.b])
                tt('dve', Ot[pr], bQS[pr, :].rearrange("p (h c) -> p h c", h=4), eg[pr, sc].unsqueeze(2).to_broadcast([64, 4, 128]),
                   ALU.mult, [bQS.b, eg.b], [Ot.b])
                yield
                tt('dve', Ot[pr], Ot[pr], AVs[pr], ALU.add, [Ot.b, AVs.b], [Ot.b])
            act(ON[:], Ot[:], AF.Square, [Ot.b], [ON.b])
            yield
            k.op('dve', lambda: nc.vector.tensor_reduce(out=st4[:, 0:4], in_=ON[:], axis=AX.X, op=ALU.add), [ON.b], [st4.b])
            rstd_from_ssq(st4[:, 4:8], st4[:, 0:4], 4, 1.0 / 128, [st4.b], [st4.b])
            yield
            tt('dve', ON[:], Ot[:], st4[:, 4:8].unsqueeze(2).to_broadcast([128, 4, 128]), ALU.mult, [Ot.b, st4.b], [ON.b])
            tt('dve', ON[:], ON[:], aonB[:].unsqueeze(1).to_broadcast([128, 4, 128]), ALU.mult, [ON.b, aonB.b], [ON.b])
            if n == 0:
                for name, tb_ in (("O0", Ot), ("ON0", ON)):
                    d = dbg(name, [128, 512])
                    if d is not None:
                        k.dma('sp', d[:, :], f2(tb_), reads=[tb_.b], writes=[Buf()])
            jo = n // 2
            if n % 2 == 0:
                ts('dve', OS[:], f2(ON), selt[:, 0:1], ALU.mult, [ON.b, selt.b], [OS.b])
            else:
                stt(OS[:], f2(ON), selt[:, 1:2], OS[:], ALU.mult, ALU.add, [ON.b, selt.b, OS.b], [OS.b])
                sz = szat[jo % 2]
                k.dma('sp', sz[:], sza_d[jo, :, :], writes=[sz.b])
                tt('dve', MIXb[:], OS[:], sz[:], ALU.mult, [OS.b, sz.b], [MIXb.b])
                yield
                for c in range(4):
                    tr(psb16(2)[:, c * 128:(c + 1) * 128], MIXb[:, c * 128:(c + 1) * 128], identb[:], [MIXb.b, identb.b], [PS[2].b])
                yield
                cp('act', MIXT[:, 0:4, jo * 128:(jo + 1) * 128], psb16(2)[:, 0:512].rearrange("p (c t) -> p c t", c=4),
                   [PS[2].b], [MIXT.sub(('a', jo))])
            yield

        def stepg(gen_):
            try:
                next(gen_)
                return True
            except StopIteration:
                return False

        for s_ in range(NT + 2):
            gens = []
            if 0 <= s_ - 2 < NT:
                gens.append(("rec", gen_rec(s_ - 2)))
            if 0 <= s_ - 1 < NT:
                gens.append(("p2", gen_p2(s_ - 1)))
            if s_ < NT:
                gens.append(("p1", gen_p1(s_)))
            NR = 14
            quota = {"rec": 11, "p2": 14, "p1": 8}
            r_ = 0
            alive = True
            while alive:
                alive = False
                for nm_, g_ in gens:
                    q_ = quota[nm_] if nm_ != "rec" else (13 if (s_ - 2) % 2 == 1 else 11)
                    n_now = ((r_ + 1) * q_) // NR - (r_ * q_) // NR if r_ < NR else 1
                    for _ in range(n_now):
                        alive = stepg(g_) or alive
                    if n_now == 0 and r_ < NR:
                        alive = True
                r_ += 1
        k.barrier()

    if stop == 'C':
        return done()
    with ExitStack() as sD:
        def R(name, shape, dt=F32):
            return TB(sD.enter_context(nc.sbuf_tensor("r_" + name, list(shape), dt, side="right")), name)

        ikT = R("ikT", [128, SEQ], BF16)
        bkT = R("bkT", [64, 2, SEQ], BF16)
        Vp = R("Vp", [128, NT, 2, 65], BF16)
        k.dma('sp', ikT[:], ikT_d[:, :], writes=[ikT.b])
        k.dma('sp', bkT[:], bkT_d[:, :, :], writes=[bkT.b])
        k.dma('sp', Vp[:].rearrange("p t g d -> p t (g d)"), Vp_d[:, :, :], writes=[Vp.b])
        iqj = [R("iqj0", [128, 8, 128], BF16), R("iqj1", [128, 8, 128], BF16)]
        score = [R("score%d" % i, [128, SEQ]) for i in range(2)]
        cjunk = R("cjunk", [128, SEQ], BF16)
        midT = R("midT", [128, 1]); cntD = R("cntD", [128, 1]); sgA = R("sgA", [128, 1]); tcm = R("tcm", [128, 2])
        rl = [R("rl%d" % i, [128, 512]) for i in range(4)]
        m8 = R("m8", [128, 8])
        bis = R("bis", [128, 8])
        Hh = R("Hh", [128, 32])
        pw2 = R("pw2", [128, 32])
        for i_ in range(32):
            k.op('pool', lambda i_=i_: nc.gpsimd.memset(pw2[:, i_:i_ + 1], float(2.0 ** -(i_ + 1))), [], [pw2.b])
        bigI = R("bigI", [128, 128], BF16)
        ts('dve', bigI[:], ident, 30000.0, ALU.mult, [cst.b], [bigI.b])
        tau = [R("tau0", [128, 1]), R("tau1", [128, 1])]
        msel = R("msel", [128, SEQ], BF16)
        MTs = [R("MT0", [128, NT, 128], BF16), R("MT1", [128, NT, 128], BF16)]
        Pb = [R("Pb%d" % i, [128, 4, 128], BF16) for i in range(4)]
        ob = R("ob", [128, 8, 65])
        rden = R("rden", [128, 8])
        obn = R("obn", [128, 8, 64])
        obg = R("obg", [128, 512], BF16)
        KBIS = 20

        bscore_d = [Buf() for _ in range(NOWN)]

        def gen_front(j):
            NK = 256 * (j + 1)
            qs = slice(j * 128, (j + 1) * 128)
            iqT = iqj[j % 2]
            sc_ = score[j % 2]
            groups = [(a, min(a + 512, NK)) for a in range(0, NK, 512)]
            it = 0
            for h in range(8):
                for gi, (a, b_) in enumerate(groups):
                    pb = PS[it % 4]
                    r = rl[it % 4]
                    sb_ = sc_.sub(gi)
                    mm(pb[:, 0:b_ - a], iqT[:, h, :], ikT[:, a:b_], [iqT.b, ikT.b], [pb.b])
                    act(r[:, 0:b_ - a], pb[:, 0:b_ - a], AF.Relu, [pb.b], [r.b])
                    if h == 0:
                        ts('dve', sc_[:, a:b_], r[:, 0:b_ - a], iw[:, j, 0:1], ALU.mult, [r.b, iw.sub(j)], [sb_, sc_.b])
                    else:
                        stt(sc_[:, a:b_], r[:, 0:b_ - a], iw[:, j, h:h + 1], sc_[:, a:b_], ALU.mult, ALU.add,
                            [r.b, iw.sub(j), sb_], [sb_])
                    it += 1
                    yield
            allg = [sc_.sub(gi) for gi in range(len(groups))]
            tt('dve', sc_[:, NK - 256:NK], sc_[:, NK - 256:NK], cmt[:], ALU.add, allg + [cmt.b], allg + [sc_.b])
            k.dma('sp', score_d[j, :, 0:NK], sc_[:, 0:NK], reads=[sc_.b], writes=[bscore_d[j]])
            yield

        cjunk2 = cjunk
        bisA = R("bisA", [128, 8])
        HhA = R("HhA", [128, 32])
        negHA = R("negHA", [128, 32])
        bqj4 = [R("bqj4_%d" % i, [64, 8, 128], BF16) for i in range(4)]
        szbj4 = [R("szbj4_%d" % i, [128, 512], BF16) for i in range(4)]

        def chain_init(j, on_act):
            NK = 256 * (j + 1)
            sc_ = score[j % 2]
            if j == 0:
                k.op('dve', lambda: nc.vector.memset(tau[0][:], -1.0e29), [], [tau[0].b])
                return
            H_ = HhA if on_act else Hh
            bb = bisA if on_act else bis
            k.op('dve', lambda: nc.vector.max(out=m8[:], in_=sc_[:, 0:NK]), [sc_.b], [m8.b])
            k.op('dve', lambda: nc.vector.tensor_reduce(out=bb[:, 5:6], in_=sc_[:, 0:256], axis=AX.X, op=ALU.min),
                 [sc_.b], [bb.b])
            tt('dve', bb[:, 6:7], m8[:, 0:1], bb[:, 5:6], ALU.subtract, [m8.b, bb.b], [bb.b])
            tt('dve', H_[:], pw2[:], bb[:, 6:7].to_broadcast([128, 32]), ALU.mult, [pw2.b, bb.b], [H_.b])
            if on_act:
                ts('dve', negHA[:], HhA[:], -1.0, ALU.mult, [HhA.b], [negHA.b])
                ts('dve', bisA[:, 0:1], bisA[:, 5:6], -1.0, ALU.mult, [bisA.b, HhA.b], [bisA.b], s2=HhA[:, 0:1], op1=ALU.subtract)
            else:
                tt('dve', bis[:, 2:3], bis[:, 5:6], Hh[:, 0:1], ALU.add, [bis.b, Hh.b], [bis.b])

        def gen_chain_dve(j):
            NK = 256 * (j + 1)
            sc_ = score[j % 2]
            ta = tau[j % 2]
            if j == 0:
                return
            c = max(64, (int(0.46 * NK) // 64) * 64)
            nA = NK - c
            thr = 255.5 - nA / 2.0
            cp('dve', midT[:], bis[:, 2:3], [bis.b], [midT.b])
            for it_ in range(KBIS):
                act(cjunk[:, 0:nA], sc_[:, c:NK], AF.Sign, [sc_.b, midT.b], [cjunk.b, sgA.b], scale=-1.0, bias=midT[:, 0:1],
                    accum_out=sgA[:, 0:1])
                k.op('dve', lambda: nc.vector.tensor_scalar(out=msel[:, 0:c], in0=sc_[:, 0:c], scalar1=midT[:, 0:1],
                                                            scalar2=0.0, op0=ALU.is_ge, op1=ALU.add,
                                                            accum_out=cntD[:, 0:1]), [sc_.b, midT.b], [msel.b, cntD.b])
                ts('dve', tcm[:, 0:1], sgA[:, 0:1], -0.5, ALU.mult, [sgA.b, cntD.b], [tcm.b], s2=cntD[:, 0:1], op1=ALU.add)
                ts('dve', tcm[:, 1:2], tcm[:, 0:1], float(thr), ALU.is_ge, [tcm.b, Hh.b], [tcm.b], s2=Hh[:, it_:it_ + 1], op1=ALU.mult)
                nxt = it_ + 1 if it_ < KBIS - 1 else it_
                dst = midT if it_ < KBIS - 1 else ta
                ts('dve', dst[:, 0:1], midT[:, 0:1], tcm[:, 1:2], ALU.add, [midT.b, tcm.b, Hh.b], [dst.b], s2=Hh[:, nxt:nxt + 1],
                   op1=ALU.subtract)
                yield

        def gen_chain_act(j):
            NK = 256 * (j + 1)
            sc_ = score[j % 2]
            ta = tau[j % 2]
            for it_ in range(KBIS):
                act(cjunk2[:, 0:NK], sc_[:, 0:NK], AF.Sign, [sc_.b, bisA.b], [cjunk2.b, bisA.b], bias=bisA[:, 0:1],
                    accum_out=bisA[:, 1:2])
                act(bisA[:, 2:3], bisA[:, 1:2], AF.Sign, [bisA.b], [bisA.b], bias=float(NK - 511.5))
                act(bisA[:, 0:1], bisA[:, 2:3], AF.Identity, [bisA.b, negHA.b], [bisA.b], scale=negHA[:, it_ + 1:it_ + 2],
                    bias=bisA[:, 0:1])
                yield
            ts('dve', ta[:], bisA[:, 0:1], -1.0, ALU.mult, [bisA.b, HhA.b], [ta.b], s2=HhA[:, KBIS:KBIS + 1], op1=ALU.subtract)

        def finish_select(j, part=0):
            NK = 256 * (j + 1)
            nkb = 2 * (j + 1)
            sc_ = score[j % 2]
            ta = tau[j % 2]
            MT = MTs[j % 2]
            if part in (0, 1):
                ts('dve', msel[:, 0:NK], sc_[:, 0:NK], ta[:, 0:1], ALU.is_ge, [sc_.b, ta.b], [msel.b], s2=-1.0, op1=ALU.add)
            if part == 1:
                return
            for kb0 in range(0, nkb, 8):
                nb = min(8, nkb - kb0)
                bi = 2 + ((kb0 // 8) % 2)
                for i in range(nb):
                    kb = kb0 + i
                    tr(psb16(bi)[:, i * 128:(i + 1) * 128], msel[:, kb * 128:(kb + 1) * 128], identb[:], [msel.b, identb.b], [PS[bi].b])
                cp('act', MT[:, kb0:kb0 + nb, :], psb16(bi)[:, 0:nb * 128].rearrange("p (k t) -> p k t", k=nb), [PS[bi].b], [MT.b])

        def gen_attn(j):
            nkb = 2 * (j + 1)
            qs = slice(j * 128, (j + 1) * 128)
            bqT = bqj4[j % 4]
            MT = MTs[j % 2]
            units = [(kb, g) for kb in range(nkb) for g in range(2)]

            PSX = [PS[4], PS[5], PS[0], PS[1]]
            LA = 3

            def st_part(u):
                kb, g = units[u]
                pst = PSX[u % 4]
                Pm = Pb[u % 4]
                mm(pst[:, :], bkT[0:64, g, kb * 128:(kb + 1) * 128], bqT[0:64, 4 * g:4 * g + 4, :],
                   [bkT.b, bqT.b], [pst.b], start=True, stop=False)
                for r_ in range(4):
                    mm(pst[:, r_ * 128:(r_ + 1) * 128], bigI[:], MT[:, kb, :], [bigI.b, MT.b], [pst.b], start=False, stop=(r_ == 3))
                act(Pm[:].rearrange("p h t -> p (h t)"), pst[:, :], AF.Exp, [pst.b], [Pm.b], scale=0.125)

            def pv_part(u):
                kb, g = units[u]
                Pm = Pb[u % 4]
                pso = PS[6 + g]
                for r_ in range(4):
                    mm(pso[:, r_ * 65:(r_ + 1) * 65], Pm[:, r_, :], Vp[:, kb, g, :], [Pm.b, Vp.b], [pso.b],
                       start=(kb == 0 and r_ == 0), stop=(kb == nkb - 1), sgc=True)

            for u0 in range(min(LA, len(units))):
                st_part(u0)
            for u in range(len(units)):
                if u + LA < len(units):
                    st_part(u + LA)
                pv_part(u)
                yield
            szb = szbj4[j % 4]
            for g in range(2):
                cp('act', ob[:, 4 * g:4 * g + 4, :].rearrange("p h d -> p (h d)"), PS[6 + g][:, 0:260], [PS[6 + g].b], [ob.b])
            k.op('dve', lambda: nc.vector.reciprocal(out=rden[:], in_=ob[:, :, 64]), [ob.b], [rden.b])
            tt('dve', obn[:], ob[:, :, 0:64], rden[:].unsqueeze(2).to_broadcast([128, 8, 64]), ALU.mult, [ob.b, rden.b], [obn.b])
            tt('dve', obg[:], obn[:].rearrange("p h d -> p (h d)"), szb[:], ALU.mult, [obn.b, szb.b], [obg.b])
            for c in range(4):
                tr(psb16(3)[:, c * 128:(c + 1) * 128], obg[:, c * 128:(c + 1) * 128], identb[:], [obg.b, identb.b], [PS[3].b])
            cp('act', MIXT[:, 4:8, qs], psb16(3)[:, 0:512].rearrange("p (c t) -> p c t", c=4), [PS[3].b], [MIXT.sub(('b', j))])
            yield

        def chain2(*gens):
            for g_ in gens:
                for _ in g_:
                    yield

        def step(gen_, n=1):
            for _ in range(n):
                try:
                    next(gen_)
                except StopIteration:
                    return False
            return True

        def load_iq(j):
            k.dma('sp', iqj[j % 2][:], iqT_d[:, :, j * 128:(j + 1) * 128], writes=[iqj[j % 2].b])

        load_iq(0)
        for j in range(NOWN):
            if j + 1 < NOWN:
                load_iq(j + 1)
            for _ in gen_front(j):
                pass
        def load_blk(j):
            NK_ = 256 * (j + 1)
            k.dma('sp', score[j % 2][:, 0:NK_], score_d[j, :, 0:NK_], reads=[bscore_d[j]], writes=[score[j % 2].b])
            k.dma('sp', bqj4[j % 4][:], bqT_d[:, :, j * 128:(j + 1) * 128], writes=[bqj4[j % 4].b])
            k.dma('sp', szbj4[j % 4][:], szb_d[j, :, :], writes=[szbj4[j % 4].b])

        load_blk(0)
        for j in range(NOWN + 1):
            attn_seq = gen_attn(j - 1) if j >= 1 else iter(())
            if j < NOWN:
                if j + 1 < NOWN:
                    load_blk(j + 1)
                chain_init(j, False)
                ch_ = gen_chain_dve(j)
                n_attn = (4 * j + 2) if j >= 1 else 0
                r_attn = max(1, -(-n_attn // KBIS))
                alive = True
                it_r = 0
                while alive:
                    alive = step(ch_)
                    if it_r < KBIS:
                        n_now = ((it_r + 1) * n_attn) // (KBIS + 3) - (it_r * n_attn) // (KBIS + 3)
                    else:
                        n_now = r_attn
                    it_r += 1
                    if it_r > KBIS:
                        break
                    if n_now > 0:
                        alive = step(attn_seq, n_now) or alive
                    elif j >= 1 and it_r < KBIS:
                        alive = True
                finish_select(j, part=1)
                for _ in attn_seq:
                    pass
                finish_select(j, part=2)
            else:
                for _ in attn_seq:
                    pass
        k.barrier()

    if stop == 'D':
        return done()
    bout = []
    with ExitStack() as sE:
        def R(name, shape, dt=F32):
            return TB(sE.enter_context(nc.sbuf_tensor("r_" + name, list(shape), dt, side="right")), name)

        wo = R("wo", [128, 8, 1024], BF16)
        wgt = R("wgt", [128, 8, 1024], BF16)
        wpl = R("wpl", [128, 2, 1024], BF16)
        bgB = R("bgB", [128, 1024])
        k.dma('sp', bgB[:], bgate_d[0:1, :].partition_broadcast(128), writes=[bgB.b])
        wstE = [R("wstE0", [128, 8, 512]), R("wstE1", [128, 8, 512])]
        specs = []
        for half in range(2):
            specs += [(wout_d, half, None, wo, 8), (wg_d, half, n2t, wgt, 8), (wple_d, half, None, wpl, 2)]
        for i_, (w_d_, half, gain_, dst_, kc_) in enumerate(specs):
            st_ = wstE[i_ % 2]
            cs = slice(half * 512, (half + 1) * 512)
            k.dma('sp', st_[:, 0:kc_, :], w_d_.rearrange("(c p) n -> p c n", p=128)[:, :, cs], writes=[st_.b])
            if gain_ is not None:
                for c_ in range(kc_):
                    if c_ % 2 == 0:
                        ts('dve', dst_[:, c_, cs], st_[:, c_, :], gain_[:, c_:c_ + 1], ALU.mult, [st_.b, gain_.b], [dst_.sub((half, c_))])
                    else:
                        act(dst_[:, c_, cs], st_[:, c_, :], AF.Identity, [st_.b, gain_.b], [dst_.sub((half, c_))], scale=gain_[:, c_:c_ + 1])
            else:
                hk = kc_ // 2
                cp('dve', dst_[:, 0:hk, cs], st_[:, 0:hk, :], [st_.b], [dst_.sub((half, 0))])
                cp('act', dst_[:, hk:kc_, cs], st_[:, hk:kc_, :], [st_.b], [dst_.sub((half, 1))])

        def wall(tb_):
            return [tb_.b] + list(tb_.subs.values())
        xo = [R("xo0", [128, 1024]), R("xo1", [128, 1024])]
        pin = [R("pin0", [128, 256]), R("pin1", [128, 256])]
        pbf2 = [R("pbf0", [128, 256], BF16), R("pbf1", [128, 256], BF16)]
        pT2 = [R("pT0", [128, 2, 128], BF16), R("pT1", [128, 2, 128], BF16)]
        x1 = [R("x10", [128, 1024]), R("x11", [128, 1024])]
        junk2 = R("junk2", [128, 1024])
        xn22 = [R("xn20", [128, 1024], BF16), R("xn21", [128, 1024], BF16)]
        h2T2 = [R("h2T0", [128, 8, 128], BF16), R("h2T1", [128, 8, 128], BF16)]
        gt2 = [R("gt0", [128, 1024]), R("gt1", [128, 1024])]
        yt = [R("yt0", [128, 1024]), R("yt1", [128, 1024])]
        st2 = R("st2", [128, 4])

        def e_gen(j):
            qs = slice(j * 128, (j + 1) * 128)
            xt = xo[j % 2]; pt_ = pin[j % 2]; xx = x1[j % 2]; pbf = pbf2[j % 2]; pT = pT2[j % 2]
            xn2 = xn22[j % 2]; h2T = h2T2[j % 2]; gt = gt2[j % 2]; yy = yt[j % 2]
            so = 2 * (j % 2); sb = st2.sub(j % 2)
            k.dma('sp', xt[:], xo_d[qs, :], writes=[xt.b])
            k.dma('sp', pt_[:], po_d[qs, :], writes=[pt_.b])
            for hf in range(2):
                pb = PS[hf]
                for kc in range(8):
                    mm(pb[:, :], MIXT[:, kc, qs], wo[:, kc, hf * 512:(hf + 1) * 512],
                       [MIXT.sub(('a', j)), MIXT.sub(('b', j))] + wall(wo), [pb.b], start=(kc == 0), stop=(kc == 7))
            cp('dve', pbf[:], pt_[:], [pt_.b], [pbf.b])
            yield
            for hf in range(2):
                tt('dve', xx[:, hf * 512:(hf + 1) * 512], PS[hf][:, :], xt[:, hf * 512:(hf + 1) * 512], ALU.add, [PS[hf].b, xt.b], [xx.b])
            for c in range(2):
                tr(psb16(3)[:, c * 128:(c + 1) * 128], pbf[:, c * 128:(c + 1) * 128], identb[:], [pbf.b, identb.b], [PS[3].b])
            yield
            act(junk2[:], xx[:], AF.Square, [xx.b], [junk2.b, sb], accum_out=st2[:, so:so + 1])
            cp('act', pT[:], psb16(3)[:, 0:256].rearrange("p (c t) -> p c t", c=2), [PS[3].b], [pT.b])
            yield
            rstd_from_ssq(st2[:, so + 1:so + 2], st2[:, so:so + 1], 1, 1.0 / D_MODEL, [sb], [sb])
            yield
            ts('dve', xn2[:], xx[:], st2[:, so + 1:so + 2], ALU.mult, [xx.b, sb], [xn2.b])
            yield
            for fc in range(8):
                tr(psb16(2)[:, fc * 128:(fc + 1) * 128], xn2[:, fc * 128:(fc + 1) * 128], identb[:], [xn2.b, identb.b], [PS[2].b])
            yield
            cp('act', h2T[:], psb16(2).rearrange("p (c t) -> p c t", c=8), [PS[2].b], [h2T.b])
            yield
            for hf in range(2):
                pg = PS[4 + hf]
                for fc in range(8):
                    mm(pg[:, :], h2T[:, fc, :], wgt[:, fc, hf * 512:(hf + 1) * 512], [h2T.b] + wall(wgt), [pg.b],
                       start=(fc == 0), stop=(fc == 7))
            yield
            for hf in range(2):
                tt('dve', gt[:, hf * 512:(hf + 1) * 512], PS[4 + hf][:, :], bgB[:, hf * 512:(hf + 1) * 512], ALU.add, [PS[4 + hf].b, bgB.b], [gt.b])
            yield
            act(gt[:], gt[:], AF.Sigmoid, [gt.b], [gt.b])
            for hf in range(2):
                pp_ = PS[6 + hf]
                for c in range(2):
                    mm(pp_[:, :], pT[:, c, :], wpl[:, c, hf * 512:(hf + 1) * 512], [pT.b] + wall(wpl), [pp_.b],
                       start=(c == 0), stop=(c == 1))
            yield
            for hf in range(2):
                tt('dve', yy[:, hf * 512:(hf + 1) * 512], PS[6 + hf][:, :], gt[:, hf * 512:(hf + 1) * 512], ALU.mult, [PS[6 + hf].b, gt.b], [yy.b])
            yield
            tt('pool', yy[:], yy[:], xx[:], ALU.add, [yy.b, xx.b], [yy.b])
            bo = Buf()
            k.dma('pool', y_d[qs, :], yy[:], reads=[yy.b], writes=[bo])
            bout.append(bo)
            yield
        run_pipelined((e_gen(j) for j in range(NOWN)), depth=2)
        k.barrier()
    k.finish(bout)
    return nc, dbg_out, k


def _consts():
    p = np.arange(128)
    same = (p[:, None] // 64) == (p[None, :] // 64)
    ident = np.eye(128, dtype=np.float32)
    tribd = (same & (p[:, None] <= p[None, :])).astype(np.float32)
    blk = same.astype(np.float32)
    l0 = np.zeros((128, 128), np.float32); l0[0:64, :] = 1.0
    l1 = np.zeros((128, 128), np.float32); l1[64:128, :] = 1.0
    negns = np.where(same & (p[:, None] <= p[None, :]), 0.0, NEGM).astype(np.float32)
    negst = np.where(same & (p[None, :] < p[:, None]), 0.0, NEGM).astype(np.float32)
    ones = np.ones((128, 128), np.float32)
    return np.concatenate([ident, tribd, blk, l0, l1, negns, negst, ones, -ones], axis=1)


def make_in_maps(inputs):
    f = lambda a: np.ascontiguousarray(np.asarray(a, dtype=np.float32))
    x = f(inputs["x"]); p = f(inputs["p"])
    cst = _consts()
    tril = np.where(np.arange(128)[None, :] <= np.arange(128)[:, None], 0.0, BIGNEG).astype(np.float32)
    shared = {
        "w_in": f(inputs["w_in"][0]), "w_out": f(inputs["w_out"][0]), "w_ple": f(inputs["w_ple"][0]),
        "w_gate": f(inputs["w_ple_gate"][0]),
        "n1": f(inputs["attn_norm_w"][0].reshape(8, 128).T), "n2": f(inputs["ple_gate_norm_w"][0].reshape(8, 128).T),
        "cw": f(inputs["conv_w"][0].reshape(4, 12, 128).transpose(2, 1, 0).reshape(128, 48)),
        "alog": f(inputs["a_log"][0].reshape(1, 4)), "dtb": f(inputs["dt_bias"][0].reshape(1, 4)),
        "aon": f(inputs["a_out_norm_w"][0].reshape(1, 128)), "gq": f(inputs["b_q_norm_w"][0].reshape(1, 64)),
        "gk": f(inputs["b_k_norm_w"][0].reshape(1, 64)), "bgate": f(inputs["b_ple_gate"][0].reshape(1, 1024)),
        "cst": cst,
    }
    maps = []
    for c in range(8):
        b, half = c // 2, c % 2
        xb = x[b]
        own = xb.reshape(NT, 128, D_MODEL)[half::2].reshape(NOWN * 128, D_MODEL)
        po = p[0, b].reshape(NT, 128, 256)[half::2].reshape(NOWN * 128, 256)
        if half == 0:
            cm = np.concatenate([tril, np.full((128, 128), BIGNEG, np.float32)], axis=1)
        else:
            cm = np.concatenate([np.zeros((128, 128), np.float32), tril], axis=1)
        sel = np.zeros((128, 2), np.float32); sel[:, half] = 1.0
        m = dict(shared)
        m.update({"xb": np.ascontiguousarray(xb), "xo": np.ascontiguousarray(own), "po": np.ascontiguousarray(po),
                  "cm": np.ascontiguousarray(cm), "sel": sel})
        maps.append(m)
    return maps


_CACHE = {}


def kernel(**inputs):
    if "nc" not in _CACHE:
        _CACHE["nc"] = build()[0]
    nc = _CACHE["nc"]
    maps = make_in_maps(inputs)
    res = run_bass_kernel_spmd(nc, maps, core_ids=list(range(8)))
    out = np.empty((4, NT, 128, D_MODEL), np.float32)
    for c in range(8):
        b, half = c // 2, c % 2
        out[b, half::2] = np.asarray(res.results[c]["y"], dtype=np.float32).reshape(NOWN, 128, D_MODEL)
    return out.reshape(4, SEQ, D_MODEL)
```

```python
import numpy as np
import ml_dtypes
from contextlib import ExitStack
import concourse.bass as bass
import concourse.mybir as mybir
from concourse.bass_utils import run_bass_kernel_spmd

F32 = mybir.dt.float32
BF16 = mybir.dt.bfloat16
AF = mybir.ActivationFunctionType
ALU = mybir.AluOpType
AX = mybir.AxisListType

D_MODEL = 1024
SEQ = 4096
NT = SEQ // 128
NOWN = 16
EPS = 1e-6
IN_WIDTH = 4496
NEGM = -30000.0
BIGNEG = -1.0e30

ENGS = ['pe', 'dve', 'act', 'pool', 'sp']
SAME_ENGINE_SYNC = ('dve', 'act', 'pool')


class Buf:
    __slots__ = ('name', 'last_w', 'readers', 'excl')

    def __init__(self, name=''):
        self.name = name
        self.excl = False
        self.last_w = []
        self.readers = {}


class TB:
    def __init__(self, t, name=''):
        self.t = t
        self.b = Buf(name)
        self.subs = {}

    def sub(self, key):
        if key not in self.subs:
            self.subs[key] = Buf()
        return self.subs[key]

    def __getitem__(self, idx):
        return self.t[idx]


class K:
    def __init__(self, nc, n_dma_sems=(('sp', 24), ('act', 8), ('pool', 8))):
        self.nc = nc
        self.eng = dict(pe=nc.tensor, dve=nc.vector, act=nc.scalar, pool=nc.gpsimd, sp=nc.sync)
        self.sem = {e: nc.alloc_semaphore(name='c_' + e) for e in ENGS}
        self.cnt = {e: 0 for e in ENGS}
        self.known = {e: {} for e in ENGS}
        self.dsem = {}
        self.dnext = {}
        for e, n in n_dma_sems:
            self.dsem[e] = [[nc.alloc_semaphore(name='d_%s_%d' % (e, i)), 0] for i in range(n)]
            self.dnext[e] = 0
        self.ninst = 0

    def _wait(self, e, tok):
        if tok is None:
            return
        kind, key, val = tok
        if kind == 'c' and key == e and e not in SAME_ENGINE_SYNC:
            return
        kk = (kind, key) if kind == 'c' else (kind, id(key))
        if self.known[e].get(kk, 0) >= val:
            return
        semh = self.sem[key] if kind == 'c' else key[0]
        self.eng[e].wait_ge(semh, val)
        self.known[e][kk] = val

    def _deps(self, e, reads, writes, is_dma=False):
        for b in reads:
            for t in b.last_w:
                self._wait(e, t)
        for b in writes:
            if not (is_dma and not b.readers and all(t[0] == 'd' for t in b.last_w)):
                for t in b.last_w:
                    self._wait(e, t)
            for t in list(b.readers.values()):
                self._wait(e, t)

    def _commit(self, tok, reads, writes):
        for b in writes:
            if tok[0] == 'd' and not b.readers and b.last_w and all(t[0] == 'd' for t in b.last_w):
                b.last_w = b.last_w + [tok]
            else:
                b.last_w = [tok]
            b.readers = {}
        kind, key, val = tok
        rk = (kind, key if kind == 'c' else id(key))
        for b in reads:
            if b in writes:
                continue
            b.readers[rk] = tok

    def op(self, e, fn, reads=(), writes=()):
        if any(b.excl for b in reads):
            writes = list(writes) + [b for b in reads if b.excl and b not in writes]
            reads = [b for b in reads if not b.excl]
        self._deps(e, reads, writes)
        inst = fn()
        self.cnt[e] += 1
        inst.then_inc(self.sem[e], 1)
        self._commit(('c', e, self.cnt[e]), reads, writes)
        self.ninst += 1
        return inst

    def dma(self, e, out, in_, reads=(), writes=(), **kw):
        self._deps(e, reads, writes, is_dma=True)
        pool = self.dsem[e]
        i = self.dnext[e]
        self.dnext[e] = (i + 1) % len(pool)
        ent = pool[i]
        if ent[1] > 0:
            self._wait(e, ('d', ent, ent[1]))
        inst = self.eng[e].dma_start(out=out, in_=in_, **kw)
        ent[1] += 16
        inst.then_inc(ent[0], 16)
        self._commit(('d', ent, ent[1]), reads, writes)
        self.ninst += 1
        return inst

    def barrier(self):
        for e in ENGS:
            for f in ENGS:
                if f != e and self.cnt[f] > 0:
                    self._wait(e, ('c', f, self.cnt[f]))
            for q, pool in self.dsem.items():
                for ent in pool:
                    if ent[1] > 0:
                        self._wait(e, ('d', ent, ent[1]))

    def finish(self, bufs, e='sp'):
        for b in bufs:
            for t in b.last_w:
                self._wait(e, t)
            for t in list(b.readers.values()):
                self._wait(e, t)


def run_pipelined(gens, depth=2):
    active = []
    it = iter(gens)
    more = True
    while True:
        if more and len(active) < depth:
            try:
                active.append(next(it))
            except StopIteration:
                more = False
        if not active:
            break
        for g_ in list(active):
            try:
                next(g_)
            except StopIteration:
                active.remove(g_)


def build(debug=(), stop=None):
    nc = bass.Bass("TRN2", target_bir_lowering=False)
    k = K(nc)

    def din(name, shape, dt=F32):
        return nc.dram_tensor(name, list(shape), dt, kind="ExternalInput").ap()

    xb_d = din("xb", [SEQ, D_MODEL])
    xo_d = din("xo", [NOWN * 128, D_MODEL])
    po_d = din("po", [NOWN * 128, 256])
    win_d = din("w_in", [D_MODEL, IN_WIDTH])
    wout_d = din("w_out", [1024, 1024])
    wple_d = din("w_ple", [256, 1024])
    wg_d = din("w_gate", [1024, 1024])
    n1_d = din("n1", [128, 8])
    n2_d = din("n2", [128, 8])
    cw_d = din("cw", [128, 48])
    alog_d = din("alog", [1, 4])
    dtb_d = din("dtb", [1, 4])
    aon_d = din("aon", [1, 128])
    gq_d = din("gq", [1, 64])
    gk_d = din("gk", [1, 64])
    bgate_d = din("bgate", [1, 1024])
    cst_d = din("cst", [128, 9 * 128])
    cm_d = din("cm", [128, 256])
    sel_d = din("sel", [128, 2])
    y_d = nc.dram_tensor("y", [NOWN * 128, D_MODEL], F32, kind="ExternalOutput").ap()
    qkvc_d = nc.dram_tensor("qkvc", [12, 128, SEQ], F32, kind="Internal").ap()
    bqkvc = [[Buf() for _ in range(8)] for _ in range(12)]
    sza_d = nc.dram_tensor("sza_s", [NOWN, 128, 512], BF16, kind="Internal").ap()
    szb_d = nc.dram_tensor("szb_s", [NOWN, 128, 512], BF16, kind="Internal").ap()
    bqT_d = nc.dram_tensor("bqT_s", [64, 8, NOWN * 128], BF16, kind="Internal").ap()
    iqT_d = nc.dram_tensor("iqT_s", [128, 8, NOWN * 128], BF16, kind="Internal").ap()
    ikT_d = nc.dram_tensor("ikT_s", [128, SEQ], BF16, kind="Internal").ap()
    bkT_d = nc.dram_tensor("bkT_s", [64, 2, SEQ], BF16, kind="Internal").ap()
    Vp_d = nc.dram_tensor("Vp_s", [128, NT, 130], BF16, kind="Internal").ap()
    score_d = nc.dram_tensor("score_s", [NOWN, 128, SEQ], F32, kind="Internal").ap()
    dbg_out = {}

    def done():
        k.barrier()
        return nc, dbg_out, k

    def dbg(name, shape):
        if name in debug:
            dbg_out[name] = nc.dram_tensor("dbg_" + name, list(shape), F32, kind="ExternalOutput").ap()
            return dbg_out[name]
        return None

    def P(name, shape, dt=F32):
        return TB(nc.alloc_sbuf_tensor("s_" + name, list(shape), dt), name)

    def act(out, in_, func, reads, writes, **kw):
        return k.op('act', lambda: nc.scalar.activation(out=out, in_=in_, func=func, **kw), reads, writes)

    def tt(e, out, in0, in1, op, reads, writes):
        eng = nc.vector if e == 'dve' else nc.gpsimd
        return k.op(e, lambda: eng.tensor_tensor(out=out, in0=in0, in1=in1, op=op), reads, writes)

    def ts(e, out, in0, s1, op0, reads, writes, s2=None, op1=None, **kw):
        eng = nc.vector if e == 'dve' else nc.gpsimd
        if op1 is None:
            return k.op(e, lambda: eng.tensor_scalar(out=out, in0=in0, scalar1=s1, scalar2=None, op0=op0, **kw), reads, writes)
        return k.op(e, lambda: eng.tensor_scalar(out=out, in0=in0, scalar1=s1, scalar2=s2, op0=op0, op1=op1, **kw), reads, writes)

    def stt(out, in0, scalar, in1, op0, op1, reads, writes):
        return k.op('dve', lambda: nc.vector.scalar_tensor_tensor(out=out, in0=in0, scalar=scalar, in1=in1, op0=op0, op1=op1), reads, writes)

    def mm(out, lhsT, rhs, reads, writes, start=True, stop=True, sgc=False):
        if sgc:
            return k.op('pe', lambda: nc.tensor.matmul(out, lhsT=lhsT, rhs=rhs, start=start, stop=stop,
                                                        skip_group_check=True), reads, writes)
        return k.op('pe', lambda: nc.tensor.matmul(out, lhsT=lhsT, rhs=rhs, start=start, stop=stop), reads, writes)

    def tr(out, in_, ident, reads, writes):
        return k.op('pe', lambda: nc.tensor.transpose(out=out, in_=in_, identity=ident), reads, writes)

    def cp(e, out, in_, reads, writes):
        if e == 'act':
            return k.op('act', lambda: nc.scalar.copy(out=out, in_=in_), reads, writes)
        eng = nc.vector if e == 'dve' else nc.gpsimd
        return k.op(e, lambda: eng.tensor_copy(out=out, in_=in_), reads, writes)

    PS = [TB(nc.alloc_psum_tensor("ps%d" % i, [128, 512], F32), "ps%d" % i) for i in range(8)]
    for p_ in PS:
        p_.b.excl = True

    def psb16(i):
        return PS[i].t[:].bitcast(BF16)

    cst = P("cst", [128, 9 * 128])
    k.dma('sp', cst[:], cst_d[:, :], writes=[cst.b])
    ident = cst[:, 0:128]
    tribd = cst[:, 128:256]
    blk = cst[:, 256:384]
    l0 = cst[:, 384:512]
    l1 = cst[:, 512:640]
    negns = cst[:, 640:768]
    negst = cst[:, 768:896]
    ones = cst[:, 896:1024]
    negones = cst[:, 1024:1152]
    identb = P("identb", [128, 128], BF16)
    cp('dve', identb[:], ident, [cst.b], [identb.b])
    n1t = P("n1t", [128, 8]); k.dma('sp', n1t[:], n1_d[:, :], writes=[n1t.b])
    n2t = P("n2t", [128, 8]); k.dma('sp', n2t[:], n2_d[:, :], writes=[n2t.b])
    cwt = P("cwt", [128, 48]); k.dma('sp', cwt[:], cw_d[:, :], writes=[cwt.b])
    alogB = P("alogB", [128, 4]); k.dma('sp', alogB[:], alog_d[0:1, :].partition_broadcast(128), writes=[alogB.b])
    dtbB = P("dtbB", [128, 4]); k.dma('sp', dtbB[:], dtb_d[0:1, :].partition_broadcast(128), writes=[dtbB.b])
    aonB = P("aonB", [128, 128]); k.dma('sp', aonB[:], aon_d[0:1, :].partition_broadcast(128), writes=[aonB.b])
    gqB = P("gqB", [128, 64]); k.dma('sp', gqB[:], gq_d[0:1, :].partition_broadcast(128), writes=[gqB.b])
    gkB = P("gkB", [128, 64]); k.dma('sp', gkB[:], gk_d[0:1, :].partition_broadcast(128), writes=[gkB.b])
    cmt = P("cmt", [128, 256]); k.dma('sp', cmt[:], cm_d[:, :], writes=[cmt.b])
    selt = P("selt", [128, 2]); k.dma('sp', selt[:], sel_d[:, :], writes=[selt.b])

    iw = P("iw", [128, NOWN, 8])
    BA = P("BA", [128, NT, 8])
    beta = P("beta", [128, 128])
    gg = P("gg", [128, 128])
    eg = P("eg", [128, 128])
    ekd = P("ekd", [128, 128])
    bkg = P("bkg", [128, 128])
    eglb = [P("eglb0", [128, 128]), P("eglb1", [128, 128])]
    sm = P("sm", [128, 64])

    wst = [None, None]
    wbf = [None, None]
    wctr = [0]

    def alloc_w(Rf, tag):
        w_ = Rf("wst" + tag, [128, 8, 512])
        wst[0] = wst[1] = w_
        wbf[0] = Rf("wbf0" + tag, [128, 8, 512], BF16)
        wbf[1] = Rf("wbf1" + tag, [128, 8, 512], BF16)

    def load_w(w_d, ranges, gain, kc=8):
        i = wctr[0] % 2
        wctr[0] += 1
        st, wb = wst[i], wbf[i]
        src = w_d.rearrange("(c p) n -> p c n", p=128)
        off = 0
        for (a, b_) in ranges:
            k.dma('sp', st[:, 0:kc, off:off + (b_ - a)], src[:, :, a:b_], writes=[st.b])
            off += b_ - a
        if gain is not None:
            for c_ in range(kc):
                ts('dve', wb[:, c_, 0:off], st[:, c_, 0:off], gain[:, c_:c_ + 1], ALU.mult, [st.b, gain.b], [wb.b])
        else:
            cp('dve', wb[:, 0:kc, 0:off], st[:, 0:kc, 0:off], [st.b], [wb.b])
        return wb, off

    def rstd_from_ssq(dst, src, n, scale, reads, writes):
        act(dst, src, AF.Ln, reads, writes, scale=scale, bias=EPS)
        act(dst, dst, AF.Exp, writes, writes, scale=-0.5)

    sO1 = ExitStack()
    hTo = TB(sO1.enter_context(nc.sbuf_tensor("hTo", [128, 8, NOWN * 128], BF16, side="right")), "hTo")
    with ExitStack() as sAB:
        def R(name, shape, dt=F32):
            return TB(sAB.enter_context(nc.sbuf_tensor("r_" + name, list(shape), dt, side="right")), name)

        hT = R("hT", [128, 8, SEQ], BF16)
        alloc_w(R, "ab")
        stg_ik = [R("stg_ik0", [128, 512], BF16), R("stg_ik1", [128, 512], BF16)]
        stg_bk = [R("stg_bk0", [64, 2, 128], BF16), R("stg_bk1", [64, 2, 128], BF16)]
        stg_v = [R("stg_v0", [128, 2, 65], BF16), R("stg_v1", [128, 2, 65], BF16)]
        xin = [R("xin0", [128, 1024]), R("xin1", [128, 1024])]
        junk = R("junk", [128, 1024])
        xn = [R("xn0", [128, 1024], BF16), R("xn1", [128, 1024], BF16)]
        ssq = R("ssq", [128, 4])

        def phase_a(x_d, ntiles, dst, grp):
            def tile_gen(t):
                xt = xin[t % 2]
                xnt = xn[t % 2]
                so = 2 * (t % 2)
                sb = ssq.sub(t % 2)
                k.dma('sp', xt[:], x_d[t * 128:(t + 1) * 128, :], writes=[xt.b])
                act(junk[:], xt[:], AF.Square, [xt.b], [junk.b, sb], accum_out=ssq[:, so:so + 1])
                yield
                rstd_from_ssq(ssq[:, so + 1:so + 2], ssq[:, so:so + 1], 1, 1.0 / D_MODEL, [sb], [sb])
                yield
                ts('dve', xnt[:], xt[:], ssq[:, so + 1:so + 2], ALU.mult, [xt.b, sb], [xnt.b])
                yield
                pb = PS[t % 2]
                for fc in range(8):
                    tr(psb16(t % 2)[:, fc * 128:(fc + 1) * 128], xnt[:, fc * 128:(fc + 1) * 128], identb[:],
                       [xnt.b, identb.b], [pb.b])
                yield
                cp('act' if t % 2 == 0 else 'dve', dst[:, :, t * 128:(t + 1) * 128],
                   psb16(t % 2).rearrange("p (c t) -> p c t", c=8), [pb.b], [dst.sub(t // grp)])
                yield
            run_pipelined((tile_gen(t) for t in range(ntiles)), depth=2)

        phase_a(xb_d, NT, hT, 4)
        hsel = [R("hsel0", [128, 8, 128], BF16), R("hsel1", [128, 8, 128], BF16)]
        for j in range(NOWN):
            tmp_ = hsel[j % 2]
            ts('dve', tmp_[:], hT[:, :, (2 * j) * 128:(2 * j + 1) * 128], selt[:, 0:1], ALU.mult,
               [hT.sub((2 * j) // 4), selt.b], [tmp_.b])
            stt(hTo[:, :, j * 128:(j + 1) * 128], hT[:, :, (2 * j + 1) * 128:(2 * j + 2) * 128], selt[:, 1:2], tmp_[:],
                ALU.mult, ALU.add, [hT.sub((2 * j + 1) // 4), selt.b, tmp_.b], [hTo.sub(j)])
        if stop == 'A':
            return done()

        d = dbg("hT", [128, 8 * 512])
        if d is not None:
            tmp = R("dbg_hT", [128, 8, 512])
            cp('dve', tmp[:], hT[:, :, 0:512], [hT.sub(0)], [tmp.b])
            k.dma('sp', d[:, :], tmp[:].rearrange("p c t -> p (c t)"), reads=[tmp.b], writes=[Buf()])

        pre = [R("pre%d" % i, [128, 528]) for i in range(3)]
        cacc = [R("cacc%d" % i, [128, 512]) for i in range(3)]
        cout = [R("cout%d" % i, [128, 512]) for i in range(3)]

        def b1_gen(it, wb, cc, ch, g):
            pf = pre[it % 3]
            pfn = pre[(it + 1) % 3]
            pb = PS[2 + (it % 3)]
            ca = cacc[it % 3]
            co = cout[it % 3]
            for fc in range(8):
                mm(pb[:, :], wb[:, fc, cc * 128:(cc + 1) * 128], hT[:, fc, g * 512:(g + 1) * 512],
                   [wb.b, hT.sub(g)], [pb.b], start=(fc == 0), stop=(fc == 7))
            if g == 0:
                k.op('dve', lambda pf=pf: nc.vector.memset(pf[:, 0:8], 0.0), [], [pf.b])
            yield
            cp('act', pf[:, 8:520], pb[:, :], [pb.b], [pf.b])
            act(ca[:], pb[:, :], AF.Identity, [pb.b, cwt.b], [ca.b], scale=cwt[:, ch * 4 + 3:ch * 4 + 4])
            if g < 7:
                cp('pool', pfn[:, 5:8], pf[:, 517:520], [pf.b], [pfn.b])
            yield
            rd = [pf.b, cwt.b]
            for j in range(3):
                stt(ca[:], pf[:, 5 + j:5 + j + 512], cwt[:, ch * 4 + j:ch * 4 + j + 1], ca[:],
                    ALU.mult, ALU.add, rd + [ca.b], [ca.b])
            yield
            act(co[:], ca[:], AF.Silu, [ca.b], [co.b])
            k.dma('act', qkvc_d[ch, :, g * 512:(g + 1) * 512], co[:], reads=[co.b], writes=[bqkvc[ch][g]])
            yield

        def b1_all():
            it = 0
            for wg in range(3):
                wb, _ = load_w(win_d, [(wg * 512, (wg + 1) * 512)], n1t)
                for cc in range(4):
                    for g in range(8):
                        yield b1_gen(it, wb, cc, wg * 4 + cc, g)
                        it += 1
        run_pipelined(b1_all(), depth=3)

        if stop == 'B1':
            return done()
        wb, _ = load_w(win_d, [(4360, 4488)], n1t)
        def b2_gen(g, wb):
            pb = PS[2 + (g % 2)]
            for fc in range(8):
                mm(pb[:, :], wb[:, fc, 0:128], hT[:, fc, g * 512:(g + 1) * 512], [wb.b, hT.sub(g)], [pb.b],
                   start=(fc == 0), stop=(fc == 7))
            yield
            sg = stg_ik[g % 2]
            cp('act', sg[:], pb[:, :], [pb.b], [sg.b])
            k.dma('act', ikT_d[:, g * 512:(g + 1) * 512], sg[:], reads=[sg.b], writes=[Buf()])
            yield
        run_pipelined((b2_gen(g, wb) for g in range(8)), depth=2)

        wb, ncol = load_w(win_d, [(2048, 2056), (2568, 2824)], n1t)
        knt = [R("knt0", [128, 128], BF16), R("knt1", [128, 128], BF16)]
        for sv in stg_v:
            k.op('pool', lambda sv=sv: nc.gpsimd.memset(sv[:, :, 64:65], 1.0), [], [sv.b])
        def b3_gen(t, wb):
            pb = PS[2 + (t % 2)]
            for fc in range(8):
                mm(pb[:, 0:264], hT[:, fc, t * 128:(t + 1) * 128], wb[:, fc, 0:264], [wb.b, hT.sub(t // 4)], [pb.b],
                   start=(fc == 0), stop=(fc == 7))
            yield
            cp('act', BA[:, t, :], pb[:, 0:8], [pb.b], [BA.sub(t)])
            smb = sm.sub(('b3', t % 2))
            so = (t % 2) * 8
            for g in range(2):
                act(junk[:, 0:64], pb[:, 8 + g * 64:8 + (g + 1) * 64], AF.Square, [pb.b], [junk.b, smb],
                    accum_out=sm[:, so + g:so + g + 1])
            sv = stg_v[t % 2]
            cp('act', sv[:, :, 0:64], pb[:, 136:264].rearrange("p (g d) -> p g d", g=2), [pb.b], [sv.b])
            k.dma('act', Vp_d[:, t, :], sv[:].rearrange("p g d -> p (g d)"), reads=[sv.b], writes=[Buf()])
            yield
            rstd_from_ssq(sm[:, so + 2:so + 4], sm[:, so:so + 2], 2, 1.0 / 64, [smb], [smb])
            yield
            kt = knt[t % 2]
            for g in range(2):
                stt(kt[:, g * 64:(g + 1) * 64], pb[:, 8 + g * 64:8 + (g + 1) * 64], sm[:, so + 2 + g:so + 3 + g], gkB[:],
                    ALU.mult, ALU.mult, [pb.b, smb, gkB.b], [kt.b])
            yield
            pt = PS[4 + (t % 2)]
            for g in range(2):
                tr(psb16(4 + (t % 2))[0:64, g * 128:(g + 1) * 128], kt[:, g * 64:(g + 1) * 64], identb[:],
                   [kt.b, identb.b], [pt.b])
            yield
            sk = stg_bk[t % 2]
            cp('dve', sk[:], psb16(4 + (t % 2))[0:64, 0:256].rearrange("p (g t) -> p g t", g=2), [pt.b], [sk.b])
            k.dma('pool', bkT_d[:, :, t * 128:(t + 1) * 128], sk[:], reads=[sk.b], writes=[Buf()])
            yield
        run_pipelined((b3_gen(t, wb) for t in range(NT)), depth=2)

        if stop == 'B3':
            return done()
        BAv = BA[:].rearrange("p t (a h) -> p t a h", a=2)
        bv3 = beta[:].rearrange("p (t h) -> p t h", h=4)
        g3 = gg[:].rearrange("p (t h) -> p t h", h=4)
        tA = R("tA", [128, 128]); tA3 = tA[:].rearrange("p (t h) -> p t h", h=4)
        tB_ = R("tB", [128, 128]); tB3 = tB_[:].rearrange("p (t h) -> p t h", h=4)
        nA = R("nA", [128, 4])
        act(beta[:].rearrange("p (t h) -> p t h", h=4), BAv[:, :, 0, :], AF.Sigmoid, [BA.sub(t_) for t_ in range(NT)], [beta.b])
        tt('dve', tA3, BAv[:, :, 1, :], dtbB[:].unsqueeze(1).to_broadcast([128, NT, 4]), ALU.add, [BA.sub(t_) for t_ in range(NT)] + [dtbB.b], [tA.b])
        act(tB_[:], tA[:], AF.Abs, [tA.b], [tB_.b])
        act(tB_[:], tB_[:], AF.Exp, [tB_.b], [tB_.b], scale=-1.0)
        act(tB_[:], tB_[:], AF.Ln, [tB_.b], [tB_.b], bias=1.0)
        stt(tA[:], tA[:], 0.0, tB_[:], ALU.max, ALU.add, [tA.b, tB_.b], [tA.b])
        act(nA[:], alogB[:], AF.Exp, [alogB.b], [nA.b])
        ts('dve', nA[:], nA[:], -1.0, ALU.mult, [nA.b], [nA.b])
        tt('dve', g3, tA3, nA[:].unsqueeze(1).to_broadcast([128, NT, 4]), ALU.mult, [tA.b, nA.b], [gg.b])
        pb = PS[6]
        mm(pb[:, 0:128], tribd, gg[:], [cst.b, gg.b], [pb.b])
        mm(pb[:, 128:256], blk, gg[:], [cst.b, gg.b], [pb.b])
        mm(pb[:, 256:384], l0, gg[:], [cst.b, gg.b], [pb.b])
        mm(pb[:, 384:512], l1, gg[:], [cst.b, gg.b], [pb.b])
        act(eg[:], pb[:, 0:128], AF.Exp, [pb.b], [eg.b])
        cp('dve', tA[:], pb[:, 0:128], [pb.b], [tA.b])
        tt('dve', tB_[:], pb[:, 128:256], tA[:], ALU.subtract, [pb.b, tA.b], [tB_.b])
        act(ekd[:], tB_[:], AF.Exp, [tB_.b], [ekd.b])
        act(eglb[0][:], pb[:, 256:384], AF.Exp, [pb.b], [eglb[0].b])
        act(eglb[1][:], pb[:, 384:512], AF.Exp, [pb.b], [eglb[1].b])
        tt('dve', bkg[:], beta[:], eg[:], ALU.mult, [beta.b, eg.b], [bkg.b])

        for name, tb_ in (("beta", beta), ("gg", gg), ("eg", eg), ("ekd", ekd), ("eglb1", eglb[1])):
            d = dbg(name, [128, 128])
            if d is not None:
                k.dma('sp', d[:, :], tb_[:], reads=[tb_.b], writes=[Buf()])

        k.barrier()
    if stop == 'AB':
        return done()

    with ExitStack() as sB5:
        def R(name, shape, dt=F32):
            return TB(sB5.enter_context(nc.sbuf_tensor("r_" + name, list(shape), dt, side="right")), name)

        alloc_w(R, "b5")
        stg_z = [R("stg_z0", [128, 512], BF16), R("stg_z1", [128, 512], BF16)]
        stg_q = [R("stg_q0", [64, 8, 128], BF16), R("stg_q1", [64, 8, 128], BF16)]
        sqq = R("sqq", [128, 512])
        qn1 = R("qn1", [128, 512])
        qnb = [R("qnb0", [128, 512], BF16), R("qnb1", [128, 512], BF16)]
        stq = R("stq", [128, 16])
        sqq2 = [sqq, R("sqq1", [128, 512])]
        qn12 = [qn1, R("qn11", [128, 512])]

        def z_gen(j, wb, dst_d):
            pb = PS[j % 2]
            for fc in range(8):
                mm(pb[:, :], hTo[:, fc, j * 128:(j + 1) * 128], wb[:, fc, 0:512], [wb.b, hTo.sub(j)], [pb.b],
                   start=(fc == 0), stop=(fc == 7))
            yield
            sg = stg_z[j % 2]
            act(sg[:], pb[:, :], AF.Silu, [pb.b], [sg.b])
            k.dma('act', dst_d[j, :, :], sg[:], reads=[sg.b], writes=[Buf()])
            yield
        for (c0, dst_d) in ((1536, sza_d), (2824, szb_d)):
            wb, _ = load_w(win_d, [(c0, c0 + 512)], n1t)
            run_pipelined((z_gen(j, wb, dst_d) for j in range(NOWN)), depth=2)

        def bq_gen(j, wb):
            pb = PS[2 + (j % 2)]
            sq_s = sqq2[j % 2]; qn_s = qn12[j % 2]
            so = 16 * 0
            stb = stq.sub(j % 2)
            c0 = (j % 2) * 8
            for fc in range(8):
                mm(pb[:, :], hTo[:, fc, j * 128:(j + 1) * 128], wb[:, fc, 0:512], [wb.b, hTo.sub(j)], [pb.b],
                   start=(fc == 0), stop=(fc == 7))
            yield
            act(sq_s[:], pb[:, :], AF.Square, [pb.b], [sq_s.b])
            yield
            k.op('dve', lambda: nc.vector.tensor_reduce(out=stq2[:, c0:c0 + 8], in_=sq_s[:].rearrange("p (h d) -> p h d", h=8),
                                                        axis=AX.X, op=ALU.add), [sq_s.b], [stb])
            yield
            rstd_from_ssq(stq2[:, 16 + c0:16 + c0 + 8], stq2[:, c0:c0 + 8], 8, 1.0 / 64, [stb], [stb])
            yield
            tt('dve', qn_s[:].rearrange("p (h d) -> p h d", h=8), pb[:, :].rearrange("p (h d) -> p h d", h=8),
               stq2[:, 16 + c0:16 + c0 + 8].unsqueeze(2).to_broadcast([128, 8, 64]), ALU.mult, [pb.b, stb], [qn_s.b])
            qb_ = qnb[j % 2]
            tt('dve', qb_[:].rearrange("p (h d) -> p h d", h=8), qn_s[:].rearrange("p (h d) -> p h d", h=8),
               gqB[:].unsqueeze(1).to_broadcast([128, 8, 64]), ALU.mult, [qn_s.b, gqB.b], [qb_.b])
            yield
            pt = PS[4 + (j % 2)]
            for h in range(8):
                tr(psb16(4 + (j % 2))[0:64, h * 128:(h + 1) * 128], qb_[:, h * 64:(h + 1) * 64], identb[:], [qb_.b, identb.b], [pt.b])
            yield
            sq_ = stg_q[j % 2]
            cp('act', sq_[:], psb16(4 + (j % 2))[0:64, :].rearrange("p (h t) -> p h t", h=8), [pt.b], [sq_.b])
            k.dma('act', bqT_d[:, :, j * 128:(j + 1) * 128], sq_[:], reads=[sq_.b], writes=[Buf()])
            yield
        stq2 = R("stq2", [128, 32])
        wb, _ = load_w(win_d, [(2056, 2568)], n1t)
        run_pipelined((bq_gen(j, wb) for j in range(NOWN)), depth=2)
        wb, _ = load_w(win_d, [(4488, 4496)], n1t)
        for j in range(NOWN):
            pb = PS[j % 2]
            for fc in range(8):
                mm(pb[:, 0:8], hTo[:, fc, j * 128:(j + 1) * 128], wb[:, fc, 0:8], [wb.b, hTo.sub(j)], [pb.b],
                   start=(fc == 0), stop=(fc == 7))
            ts('dve', iw[:, j, :], pb[:, 0:8], float(8 ** -0.5 * 128 ** -0.5), ALU.mult, [pb.b], [iw.sub(j)])

        def iq_gen(it, wb, wg, hh, g):
            pb = PS[2 + (it % 2)]
            for fc in range(8):
                mm(pb[:, :], wb[:, fc, hh * 128:(hh + 1) * 128], hTo[:, fc, g * 512:(g + 1) * 512],
                   [wb.b] + [hTo.sub(4 * g + i) for i in range(4)], [pb.b], start=(fc == 0), stop=(fc == 7))
            yield
            sg = stg_z[it % 2]
            cp('act', sg[:], pb[:, :], [pb.b], [sg.b])
            k.dma('act', iqT_d[:, wg * 4 + hh, g * 512:(g + 1) * 512], sg[:], reads=[sg.b], writes=[Buf()])
            yield

        def iq_all():
            it = 0
            for wg in range(2):
                wb, _ = load_w(win_d, [(3336 + wg * 512, 3336 + (wg + 1) * 512)], n1t)
                for hh in range(4):
                    for g in range(4):
                        yield iq_gen(it, wb, wg, hh, g)
                        it += 1
        run_pipelined(iq_all(), depth=2)
        k.barrier()
    sO1.close()
    MIXT = P("MIXT", [128, 8, NOWN * 128], BF16)
    if stop == 'B5':
        return done()

    with ExitStack() as sC:
        def R(name, shape, dt=F32):
            return TB(sC.enter_context(nc.sbuf_tensor("r_" + name, list(shape), dt, side="right")), name)

        def RN_(name, shape, n, dt=F32):
            return [R(name + str(i), shape, dt) for i in range(n)]

        Xin = RN_("Xin", [128, 12, 128], 2)
        SQ = R("SQ", [128, 1024])
        RNt = R("RN", [128, 1024])
        QKn3 = RN_("QKn", [128, 8, 128], 3)
        KD3 = RN_("KD", [128, 4, 128], 3)
        ATT3 = RN_("ATT", [128, 4, 128], 3)
        KBG2 = RN_("KBG", [128, 4, 128], 2); VB2 = RN_("VB", [128, 4, 128], 2)
        TG = R("TG", [128, 4, 128])
        DT = R("DT", [128, 512]); DS = R("DS", [128, 512])
        KKs = R("KKs", [128, 512]); KQs = R("KQs", [128, 512])
        Am2 = RN_("Am", [128, 4, 128], 2); Um2 = RN_("Um", [128, 4, 128], 2)
        Pa2 = RN_("Pa", [128, 4, 128], 2); Qa2 = RN_("Qa", [128, 4, 128], 2)
        Rm2 = RN_("Rm", [128, 4, 128], 2)
        VAL2 = RN_("VAL", [128, 4, 128], 2); KCDT2 = RN_("KCDT", [128, 4, 128], 2)
        VN = R("VN", [128, 4, 128]); AVs = R("AVs", [128, 4, 128])
        Ot = R("Ot", [128, 4, 128]); ON = R("ON", [128, 4, 128]); OS = R("OS", [128, 512])
        MIXb = R("MIXb", [128, 512], BF16)
        szat = [R("szat0", [128, 512], BF16), R("szat1", [128, 512], BF16)]
        S = R("S", [128, 4, 128])
        st4 = R("st4", [128, 8])
        k.op('pool', lambda: nc.gpsimd.memset(S[:], 0.0), [], [S.b])
        ident4 = ident.unsqueeze(1).to_broadcast([128, 4, 128])
        b6, b7 = PS[6], PS[7]

        def v4(ps):
            return ps[:, :].rearrange("p (h c) -> p h c", h=4)

        def f2(tb_):
            return tb_[:].rearrange("p h c -> p (h c)")

        def bc4(t_, sc):
            return t_[:, sc].unsqueeze(2).to_broadcast([128, 4, 128])

        def gen_p1(n):
            X = Xin[n % 2]
            QKn = QKn3[n % 3]; KD = KD3[n % 3]; ATT = ATT3[n % 3]
            KBG = KBG2[n % 2]; VB = VB2[n % 2]; Am = Am2[n % 2]; Um = Um2[n % 2]; Rm = Rm2[n % 2]
            for c3 in range(3):
                k.dma('sp', X[:, c3 * 4:(c3 + 1) * 4, :],
                      qkvc_d[c3 * 4:(c3 + 1) * 4, :, n * 128:(n + 1) * 128].rearrange("c p t -> p c t"),
                      reads=[bqkvc[c3 * 4 + i][n // 4] for i in range(4)], writes=[X.b])
            sc = slice(n * 4, (n + 1) * 4)
            act(SQ[:], X[:, 0:8, :].rearrange("p c t -> p (c t)"), AF.Square, [X.b], [SQ.b])
            mm(b6[:, :], ones, SQ[:, 0:512], [cst.b, SQ.b], [b6.b])
            mm(b7[:, :], ones, SQ[:, 512:1024], [cst.b, SQ.b], [b7.b])
            yield
            act(RNt[:, 0:512], b6[:, :], AF.Ln, [b6.b], [RNt.b], bias=EPS)
            act(RNt[:, 512:1024], b7[:, :], AF.Ln, [b7.b], [RNt.b], bias=EPS)
            act(RNt[:], RNt[:], AF.Exp, [RNt.b], [RNt.b], scale=-0.5)
            yield
            stt(QKn[:, 0:4, :].rearrange("p c t -> p (c t)"), X[:, 0:4, :].rearrange("p c t -> p (c t)"), 128.0 ** -0.5,
                RNt[:, 0:512], ALU.mult, ALU.mult, [X.b, RNt.b], [QKn.b])
            tt('dve', QKn[:, 4:8, :].rearrange("p c t -> p (c t)"), X[:, 4:8, :].rearrange("p c t -> p (c t)"),
               RNt[:, 512:1024], ALU.mult, [X.b, RNt.b], [QKn.b])
            for h in range(4):
                tr(b6[:, h * 128:(h + 1) * 128], QKn[:, 4 + h, :], ident, [QKn.b, cst.b], [b6.b])
            for h in range(4):
                tr(b7[:, h * 128:(h + 1) * 128], X[:, 8 + h, :], ident, [X.b, cst.b], [b7.b])
            tt('dve', TG[:], tribd.unsqueeze(1).to_broadcast([128, 4, 128]), bc4(gg, sc), ALU.mult, [cst.b, gg.b], [TG.b])
            yield
            tt('dve', KBG[:], v4(b6), bc4(bkg, sc), ALU.mult, [b6.b, bkg.b], [KBG.b])
            tt('dve', KD[:], v4(b6), bc4(ekd, sc), ALU.mult, [b6.b, ekd.b], [KD.b])
            tt('dve', VB[:], v4(b7), bc4(beta, sc), ALU.mult, [b7.b, beta.b], [VB.b])
            for h in range(4):
                mm(b6[:, h * 128:(h + 1) * 128], QKn[:, 4 + h, :], QKn[:, 4 + h, :], [QKn.b], [b6.b])
            for h in range(4):
                mm(b7[:, h * 128:(h + 1) * 128], QKn[:, 4 + h, :], QKn[:, h, :], [QKn.b], [b7.b])
            yield
            cp('act', KKs[:], b6[:, :], [b6.b], [KKs.b])
            cp('dve', KQs[:], b7[:, :], [b7.b], [KQs.b])
            for h in range(4):
                o = b6[:, h * 128:(h + 1) * 128]
                mm(o, ones, TG[:, h, :], [cst.b, TG.b], [b6.b], start=True, stop=False)
                mm(o, TG[:, h, :], negones, [cst.b, TG.b], [b6.b], start=False, stop=False)
                mm(o, ident, negns, [cst.b], [b6.b], start=False, stop=True)
            for h in range(4):
                o = b7[:, h * 128:(h + 1) * 128]
                mm(o, TG[:, h, :], ones, [cst.b, TG.b], [b7.b], start=True, stop=False)
                mm(o, negones, TG[:, h, :], [cst.b, TG.b], [b7.b], start=False, stop=False)
                mm(o, ident, negst, [cst.b], [b7.b], start=False, stop=True)
            yield
            act(DT[:], b6[:, :], AF.Exp, [b6.b], [DT.b])
            act(DS[:], b7[:, :], AF.Exp, [b7.b], [DS.b])
            yield
            tt('dve', f2(Am), KKs[:], DS[:], ALU.mult, [KKs.b, DS.b], [Am.b])
            tt('dve', Am[:], Am[:], bc4(beta, sc), ALU.mult, [Am.b, beta.b], [Am.b])
            tt('dve', f2(ATT), KQs[:], DT[:], ALU.mult, [KQs.b, DT.b], [ATT.b])
            for h in range(4):
                tr(b6[:, h * 128:(h + 1) * 128], Am[:, h, :], ident, [Am.b, cst.b], [b6.b])
            yield
            cp('act', f2(Um), b6[:, :], [b6.b], [Um.b])
            stt(Rm[:], v4(b6), -1.0, ident4, ALU.mult, ALU.add, [b6.b, cst.b], [Rm.b])
            if n == 0:
                for name, tb_ in (("ATT0", ATT), ("Am0", Am)):
                    d = dbg(name, [128, 512])
                    if d is not None:
                        k.dma('sp', d[:, :], f2(tb_), reads=[tb_.b], writes=[Buf()])
            yield

        def gen_p2(n):
            KBG = KBG2[n % 2]; VB = VB2[n % 2]; Am = Am2[n % 2]; Um = Um2[n % 2]; Rm = Rm2[n % 2]
            Pa = Pa2[n % 2]; Qa = Qa2[n % 2]; VAL = VAL2[n % 2]; KCDT = KCDT2[n % 2]
            bA, bB, bC = PS[3], PS[4], PS[5]
            Pc, Qc = Um, Am
            Pn, Qn = Pa, Qa
            for stg in range(1, 7):
                if stg >= 2:
                    for h in range(4):
                        mm(bC[:, h * 128:(h + 1) * 128], Qc[:, h, :], Rm[:, h, :], [Qc.b, Rm.b], [bC.b])
                if stg <= 4:
                    for h in range(4):
                        mm(bA[:, h * 128:(h + 1) * 128], Qc[:, h, :], Pc[:, h, :], [Qc.b, Pc.b], [bA.b])
                if stg <= 5:
                    for h in range(4):
                        mm(bB[:, h * 128:(h + 1) * 128], Pc[:, h, :], Qc[:, h, :], [Qc.b, Pc.b], [bB.b])
                yield
                if stg >= 2:
                    tt('dve', f2(Rm), f2(Rm), bC[:, :], ALU.add, [Rm.b, bC.b], [Rm.b])
                if stg <= 4:
                    cp('act', f2(Pn), bA[:, :], [bA.b], [Pn.b])
                if stg <= 5:
                    cp('act' if stg > 4 else 'dve', f2(Qn), bB[:, :], [bB.b], [Qn.b])
                if stg == 1:
                    Pc, Qc, Pn, Qn = Pa, Qa, Um, Am
                else:
                    Pc, Qc, Pn, Qn = Pn, Qn, Pc, Qc
                yield
            for h in range(4):
                mm(bA[:, h * 128:(h + 1) * 128], Rm[:, h, :], VB[:, h, :], [Rm.b, VB.b], [bA.b])
            for h in range(4):
                mm(bB[:, h * 128:(h + 1) * 128], KBG[:, h, :], Rm[:, h, :], [Rm.b, KBG.b], [bB.b])
            cp('act', f2(VAL), bA[:, :], [bA.b], [VAL.b])
            cp('dve', f2(KCDT), bB[:, :], [bB.b], [KCDT.b])
            if n == 0:
                for name, tb_ in (("T0", Rm), ("VAL0", VAL)):
                    d = dbg(name, [128, 512])
                    if d is not None:
                        k.dma('sp', d[:, :], f2(tb_), reads=[tb_.b], writes=[Buf()])
            yield

        def gen_rec(n):
            QKn = QKn3[n % 3]; KD = KD3[n % 3]; ATT = ATT3[n % 3]; VAL = VAL2[n % 2]; KCDT = KCDT2[n % 2]
            sc = slice(n * 4, (n + 1) * 4)
            bKS, bQS, bAV = PS[0], PS[1], PS[2]
            bSU = PS[0]
            for j in range(2):
                pr = slice(64 * j, 64 * j + 64)
                for h in range(4):
                    mm(bKS[pr, h * 128:(h + 1) * 128], KCDT[:, h, pr], S[:, h, :], [KCDT.b, S.b], [bKS.b])
                for h in range(4):
                    mm(bQS[pr, h * 128:(h + 1) * 128], QKn[:, h, pr], S[:, h, :], [QKn.b, S.b], [bQS.b])
                yield
                tt('dve', VN[pr].rearrange("p h c -> p (h c)"), VAL[pr].rearrange("p h c -> p (h c)"), bKS[pr, :], ALU.subtract,
                   [VAL.b, bKS.b], [VN.b])
                yield
                for h in range(4):
                    mm(bSU[:, h * 128:(h + 1) * 128], KD[pr, h, :], VN[pr, h, :], [KD.b, VN.b], [bSU.b])
                for h in range(4):
                    mm(bAV[pr, h * 128:(h + 1) * 128], ATT[pr, h, pr], VN[pr, h, :], [ATT.b, VN.b], [bAV.b])
                tt('dve', S[:], S[:], eglb[j][:, sc].unsqueeze(2).to_broadcast([128, 4, 128]), ALU.mult, [S.b, eglb[j].b], [S.b])
                yield
                tt('dve', f2(S), f2(S), bSU[:, :], ALU.add, [S.b, bSU.b], [S.b])
                cp('act', AVs[pr].rearrange("p h c -> p (h c)"), bAV[pr, :], [bAV.b], [AVs.b])
                tt('dve', Ot[pr], bQS[pr, :].rearrange("p (h c) -> p h c", h=4), eg[pr, sc].unsqueeze(2).to_broadcast([64, 4, 128]),
                   ALU.mult, [bQS.b, eg.b], [Ot.b])
                yield
                tt('dve', Ot[pr], Ot[pr], AVs[pr], ALU.add, [Ot.b, AVs.b], [Ot.b])
            act(ON[:], Ot[:], AF.Square, [Ot.b], [ON.b])
            yield
            k.op('dve', lambda: nc.vector.tensor_reduce(out=st4[:, 0:4], in_=ON[:], axis=AX.X, op=ALU.add), [ON.b], [st4.b])
            rstd_from_ssq(st4[:, 4:8], st4[:, 0:4], 4, 1.0 / 128, [st4.b], [st4.b])
            yield
            tt('dve', ON[:], Ot[:], st4[:, 4:8].unsqueeze(2).to_broadcast([128, 4, 128]), ALU.mult, [Ot.b, st4.b], [ON.b])
            tt('dve', ON[:], ON[:], aonB[:].unsqueeze(1).to_broadcast([128, 4, 128]), ALU.mult, [ON.b, aonB.b], [ON.b])
            if n == 0:
                for name, tb_ in (("O0", Ot), ("ON0", ON)):
                    d = dbg(name, [128, 512])
                    if d is not None:
                        k.dma('sp', d[:, :], f2(tb_), reads=[tb_.b], writes=[Buf()])
            jo = n // 2
            if n % 2 == 0:
                ts('dve', OS[:], f2(ON), selt[:, 0:1], ALU.mult, [ON.b, selt.b], [OS.b])
            else:
                stt(OS[:], f2(ON), selt[:, 1:2], OS[:], ALU.mult, ALU.add, [ON.b, selt.b, OS.b], [OS.b])
                sz = szat[jo % 2]
                k.dma('sp', sz[:], sza_d[jo, :, :], writes=[sz.b])
                tt('dve', MIXb[:], OS[:], sz[:], ALU.mult, [OS.b, sz.b], [MIXb.b])
                yield
                for c in range(4):
                    tr(psb16(2)[:, c * 128:(c + 1) * 128], MIXb[:, c * 128:(c + 1) * 128], identb[:], [MIXb.b, identb.b], [PS[2].b])
                yield
                cp('act', MIXT[:, 0:4, jo * 128:(jo + 1) * 128], psb16(2)[:, 0:512].rearrange("p (c t) -> p c t", c=4),
                   [PS[2].b], [MIXT.sub(('a', jo))])
            yield

        def stepg(gen_):
            try:
                next(gen_)
                return True
            except StopIteration:
                return False

        for s_ in range(NT + 2):
            gens = []
            if 0 <= s_ - 2 < NT:
                gens.append(("rec", gen_rec(s_ - 2)))
            if 0 <= s_ - 1 < NT:
                gens.append(("p2", gen_p2(s_ - 1)))
            if s_ < NT:
                gens.append(("p1", gen_p1(s_)))
            NR = 13
            quota = {"rec": 11, "p2": 13, "p1": 8}
            r_ = 0
            alive = True
            while alive:
                alive = False
                for nm_, g_ in gens:
                    q_ = quota[nm_] if nm_ != "rec" else (13 if (s_ - 2) % 2 == 1 else 11)
                    n_now = ((r_ + 1) * q_) // NR - (r_ * q_) // NR if r_ < NR else 1
                    for _ in range(n_now):
                        alive = stepg(g_) or alive
                    if n_now == 0 and r_ < NR:
                        alive = True
                r_ += 1
        k.barrier()

    if stop == 'C':
        return done()
    with ExitStack() as sD:
        def R(name, shape, dt=F32):
            return TB(sD.enter_context(nc.sbuf_tensor("r_" + name, list(shape), dt, side="right")), name)

        ikT = R("ikT", [128, SEQ], BF16)
        bkT = R("bkT", [64, 2, SEQ], BF16)
        Vp = R("Vp", [128, NT, 2, 65], BF16)
        k.dma('sp', ikT[:], ikT_d[:, :], writes=[ikT.b])
        k.dma('sp', bkT[:], bkT_d[:, :, :], writes=[bkT.b])
        k.dma('sp', Vp[:].rearrange("p t g d -> p t (g d)"), Vp_d[:, :, :], writes=[Vp.b])
        iqj = [R("iqj0", [128, 8, 128], BF16), R("iqj1", [128, 8, 128], BF16)]
        score = [R("score%d" % i, [128, SEQ]) for i in range(2)]
        cjunk = R("cjunk", [128, SEQ], BF16)
        midT = R("midT", [128, 1]); cntD = R("cntD", [128, 1]); sgA = R("sgA", [128, 1]); tcm = R("tcm", [128, 2])
        rl = [R("rl%d" % i, [128, 512]) for i in range(4)]
        m8 = R("m8", [128, 8])
        bis = R("bis", [128, 8])
        Hh = R("Hh", [128, 32])
        pw2 = R("pw2", [128, 32])
        for i_ in range(32):
            k.op('pool', lambda i_=i_: nc.gpsimd.memset(pw2[:, i_:i_ + 1], float(2.0 ** -(i_ + 1))), [], [pw2.b])
        bigI = R("bigI", [128, 128], BF16)
        ts('dve', bigI[:], ident, 30000.0, ALU.mult, [cst.b], [bigI.b])
        tau = [R("tau0", [128, 1]), R("tau1", [128, 1])]
        msel = R("msel", [128, SEQ], BF16)
        MTs = [R("MT0", [128, NT, 128], BF16), R("MT1", [128, NT, 128], BF16)]
        Pb = [R("Pb%d" % i, [128, 4, 128], BF16) for i in range(4)]
        ob = R("ob", [128, 8, 65])
        rden = R("rden", [128, 8])
        obn = R("obn", [128, 8, 64])
        obg = R("obg", [128, 512], BF16)
        KBIS = 20

        bscore_d = [Buf() for _ in range(NOWN)]

        def gen_front(j):
            NK = 256 * (j + 1)
            qs = slice(j * 128, (j + 1) * 128)
            iqT = iqj[j % 2]
            sc_ = score[j % 2]
            groups = [(a, min(a + 512, NK)) for a in range(0, NK, 512)]
            it = 0
            for h in range(8):
                for gi, (a, b_) in enumerate(groups):
                    pb = PS[it % 4]
                    r = rl[it % 4]
                    sb_ = sc_.sub(gi)
                    mm(pb[:, 0:b_ - a], iqT[:, h, :], ikT[:, a:b_], [iqT.b, ikT.b], [pb.b])
                    act(r[:, 0:b_ - a], pb[:, 0:b_ - a], AF.Relu, [pb.b], [r.b])
                    if h == 0:
                        ts('dve', sc_[:, a:b_], r[:, 0:b_ - a], iw[:, j, 0:1], ALU.mult, [r.b, iw.sub(j)], [sb_, sc_.b])
                    else:
                        stt(sc_[:, a:b_], r[:, 0:b_ - a], iw[:, j, h:h + 1], sc_[:, a:b_], ALU.mult, ALU.add,
                            [r.b, iw.sub(j), sb_], [sb_])
                    it += 1
                    yield
            allg = [sc_.sub(gi) for gi in range(len(groups))]
            tt('dve', sc_[:, NK - 256:NK], sc_[:, NK - 256:NK], cmt[:], ALU.add, allg + [cmt.b], allg + [sc_.b])
            k.dma('sp', score_d[j, :, 0:NK], sc_[:, 0:NK], reads=[sc_.b], writes=[bscore_d[j]])
            yield

        cjunk2 = cjunk
        bisA = R("bisA", [128, 8])
        HhA = R("HhA", [128, 32])
        negHA = R("negHA", [128, 32])
        bqj4 = [R("bqj4_%d" % i, [64, 8, 128], BF16) for i in range(4)]
        szbj4 = [R("szbj4_%d" % i, [128, 512], BF16) for i in range(4)]

        def chain_init(j, on_act):
            NK = 256 * (j + 1)
            sc_ = score[j % 2]
            if j == 0:
                k.op('dve', lambda: nc.vector.memset(tau[0][:], -1.0e29), [], [tau[0].b])
                return
            H_ = HhA if on_act else Hh
            bb = bisA if on_act else bis
            k.op('dve', lambda: nc.vector.max(out=m8[:], in_=sc_[:, 0:NK]), [sc_.b], [m8.b])
            k.op('dve', lambda: nc.vector.tensor_reduce(out=bb[:, 5:6], in_=sc_[:, 0:256], axis=AX.X, op=ALU.min),
                 [sc_.b], [bb.b])
            tt('dve', bb[:, 6:7], m8[:, 0:1], bb[:, 5:6], ALU.subtract, [m8.b, bb.b], [bb.b])
            tt('dve', H_[:], pw2[:], bb[:, 6:7].to_broadcast([128, 32]), ALU.mult, [pw2.b, bb.b], [H_.b])
            if on_act:
                ts('dve', negHA[:], HhA[:], -1.0, ALU.mult, [HhA.b], [negHA.b])
                ts('dve', bisA[:, 0:1], bisA[:, 5:6], -1.0, ALU.mult, [bisA.b, HhA.b], [bisA.b], s2=HhA[:, 0:1], op1=ALU.subtract)
            else:
                tt('dve', bis[:, 2:3], bis[:, 5:6], Hh[:, 0:1], ALU.add, [bis.b, Hh.b], [bis.b])

        def gen_chain_dve(j):
            NK = 256 * (j + 1)
            sc_ = score[j % 2]
            ta = tau[j % 2]
            if j == 0:
                return
            c = max(64, (int(0.46 * NK) // 64) * 64)
            nA = NK - c
            thr = 255.5 - nA / 2.0
            cp('dve', midT[:], bis[:, 2:3], [bis.b], [midT.b])
            for it_ in range(KBIS):
                act(cjunk[:, 0:nA], sc_[:, c:NK], AF.Sign, [sc_.b, midT.b], [cjunk.b, sgA.b], scale=-1.0, bias=midT[:, 0:1],
                    accum_out=sgA[:, 0:1])
                k.op('dve', lambda: nc.vector.tensor_scalar(out=msel[:, 0:c], in0=sc_[:, 0:c], scalar1=midT[:, 0:1],
                                                            scalar2=0.0, op0=ALU.is_ge, op1=ALU.add,
                                                            accum_out=cntD[:, 0:1]), [sc_.b, midT.b], [msel.b, cntD.b])
                ts('dve', tcm[:, 0:1], sgA[:, 0:1], -0.5, ALU.mult, [sgA.b, cntD.b], [tcm.b], s2=cntD[:, 0:1], op1=ALU.add)
                ts('dve', tcm[:, 1:2], tcm[:, 0:1], float(thr), ALU.is_ge, [tcm.b, Hh.b], [tcm.b], s2=Hh[:, it_:it_ + 1], op1=ALU.mult)
                nxt = it_ + 1 if it_ < KBIS - 1 else it_
                dst = midT if it_ < KBIS - 1 else ta
                ts('dve', dst[:, 0:1], midT[:, 0:1], tcm[:, 1:2], ALU.add, [midT.b, tcm.b, Hh.b], [dst.b], s2=Hh[:, nxt:nxt + 1],
                   op1=ALU.subtract)
                yield

        def gen_chain_act(j):
            NK = 256 * (j + 1)
            sc_ = score[j % 2]
            ta = tau[j % 2]
            for it_ in range(KBIS):
                act(cjunk2[:, 0:NK], sc_[:, 0:NK], AF.Sign, [sc_.b, bisA.b], [cjunk2.b, bisA.b], bias=bisA[:, 0:1],
                    accum_out=bisA[:, 1:2])
                act(bisA[:, 2:3], bisA[:, 1:2], AF.Sign, [bisA.b], [bisA.b], bias=float(NK - 511.5))
                act(bisA[:, 0:1], bisA[:, 2:3], AF.Identity, [bisA.b, negHA.b], [bisA.b], scale=negHA[:, it_ + 1:it_ + 2],
                    bias=bisA[:, 0:1])
                yield
            ts('dve', ta[:], bisA[:, 0:1], -1.0, ALU.mult, [bisA.b, HhA.b], [ta.b], s2=HhA[:, KBIS:KBIS + 1], op1=ALU.subtract)

        def finish_select(j, part=0):
            NK = 256 * (j + 1)
            nkb = 2 * (j + 1)
            sc_ = score[j % 2]
            ta = tau[j % 2]
            MT = MTs[j % 2]
            if part in (0, 1):
                ts('dve', msel[:, 0:NK], sc_[:, 0:NK], ta[:, 0:1], ALU.is_ge, [sc_.b, ta.b], [msel.b], s2=-1.0, op1=ALU.add)
            if part == 1:
                return
            for kb0 in range(0, nkb, 8):
                nb = min(8, nkb - kb0)
                bi = 2 + ((kb0 // 8) % 2)
                for i in range(nb):
                    kb = kb0 + i
                    tr(psb16(bi)[:, i * 128:(i + 1) * 128], msel[:, kb * 128:(kb + 1) * 128], identb[:], [msel.b, identb.b], [PS[bi].b])
                cp('act', MT[:, kb0:kb0 + nb, :], psb16(bi)[:, 0:nb * 128].rearrange("p (k t) -> p k t", k=nb), [PS[bi].b], [MT.b])

        def gen_attn(j):
            nkb = 2 * (j + 1)
            qs = slice(j * 128, (j + 1) * 128)
            bqT = bqj4[j % 4]
            MT = MTs[j % 2]
            units = [(kb, g) for kb in range(nkb) for g in range(2)]

            PSX = [PS[4], PS[5], PS[0], PS[1]]
            LA = 3

            def st_part(u):
                kb, g = units[u]
                pst = PSX[u % 4]
                Pm = Pb[u % 4]
                mm(pst[:, :], bkT[0:64, g, kb * 128:(kb + 1) * 128], bqT[0:64, 4 * g:4 * g + 4, :],
                   [bkT.b, bqT.b], [pst.b], start=True, stop=False)
                for r_ in range(4):
                    mm(pst[:, r_ * 128:(r_ + 1) * 128], bigI[:], MT[:, kb, :], [bigI.b, MT.b], [pst.b], start=False, stop=(r_ == 3))
                act(Pm[:].rearrange("p h t -> p (h t)"), pst[:, :], AF.Exp, [pst.b], [Pm.b], scale=0.125)

            def pv_part(u):
                kb, g = units[u]
                Pm = Pb[u % 4]
                pso = PS[6 + g]
                for r_ in range(4):
                    mm(pso[:, r_ * 65:(r_ + 1) * 65], Pm[:, r_, :], Vp[:, kb, g, :], [Pm.b, Vp.b], [pso.b],
                       start=(kb == 0 and r_ == 0), stop=(kb == nkb - 1), sgc=True)

            for u0 in range(min(LA, len(units))):
                st_part(u0)
            for u in range(len(units)):
                if u + LA < len(units):
                    st_part(u + LA)
                pv_part(u)
                yield
            szb = szbj4[j % 4]
            for g in range(2):
                cp('act', ob[:, 4 * g:4 * g + 4, :].rearrange("p h d -> p (h d)"), PS[6 + g][:, 0:260], [PS[6 + g].b], [ob.b])
            k.op('dve', lambda: nc.vector.reciprocal(out=rden[:], in_=ob[:, :, 64]), [ob.b], [rden.b])
            tt('dve', obn[:], ob[:, :, 0:64], rden[:].unsqueeze(2).to_broadcast([128, 8, 64]), ALU.mult, [ob.b, rden.b], [obn.b])
            tt('dve', obg[:], obn[:].rearrange("p h d -> p (h d)"), szb[:], ALU.mult, [obn.b, szb.b], [obg.b])
            for c in range(4):
                tr(psb16(3)[:, c * 128:(c + 1) * 128], obg[:, c * 128:(c + 1) * 128], identb[:], [obg.b, identb.b], [PS[3].b])
            cp('act', MIXT[:, 4:8, qs], psb16(3)[:, 0:512].rearrange("p (c t) -> p c t", c=4), [PS[3].b], [MIXT.sub(('b', j))])
            yield

        def chain2(*gens):
            for g_ in gens:
                for _ in g_:
                    yield

        def step(gen_, n=1):
            for _ in range(n):
                try:
                    next(gen_)
                except StopIteration:
                    return False
            return True

        def load_iq(j):
            k.dma('sp', iqj[j % 2][:], iqT_d[:, :, j * 128:(j + 1) * 128], writes=[iqj[j % 2].b])

        load_iq(0)
        for j in range(NOWN):
            if j + 1 < NOWN:
                load_iq(j + 1)
            for _ in gen_front(j):
                pass
        def load_blk(j):
            NK_ = 256 * (j + 1)
            k.dma('sp', score[j % 2][:, 0:NK_], score_d[j, :, 0:NK_], reads=[bscore_d[j]], writes=[score[j % 2].b])
            k.dma('sp', bqj4[j % 4][:], bqT_d[:, :, j * 128:(j + 1) * 128], writes=[bqj4[j % 4].b])
            k.dma('sp', szbj4[j % 4][:], szb_d[j, :, :], writes=[szbj4[j % 4].b])

        load_blk(0)
        for j in range(NOWN + 1):
            attn_seq = gen_attn(j - 1) if j >= 1 else iter(())
            if j < NOWN:
                if j + 1 < NOWN:
                    load_blk(j + 1)
                chain_init(j, False)
                ch_ = gen_chain_dve(j)
                n_attn = (4 * j + 2) if j >= 1 else 0
                r_attn = max(1, -(-n_attn // KBIS))
                alive = True
                it_r = 0
                while alive:
                    alive = step(ch_)
                    if it_r < KBIS:
                        n_now = ((it_r + 1) * n_attn) // (KBIS + 3) - (it_r * n_attn) // (KBIS + 3)
                    else:
                        n_now = r_attn
                    it_r += 1
                    if it_r > KBIS:
                        break
                    if n_now > 0:
                        alive = step(attn_seq, n_now) or alive
                    elif j >= 1 and it_r < KBIS:
                        alive = True
                finish_select(j, part=1)
                for _ in attn_seq:
                    pass
                finish_select(j, part=2)
            else:
                for _ in attn_seq:
                    pass
        k.barrier()

    if stop == 'D':
        return done()
    bout = []
    with ExitStack() as sE:
        def R(name, shape, dt=F32):
            return TB(sE.enter_context(nc.sbuf_tensor("r_" + name, list(shape), dt, side="right")), name)

        wo = R("wo", [128, 8, 1024], BF16)
        wgt = R("wgt", [128, 8, 1024], BF16)
        wpl = R("wpl", [128, 2, 1024], BF16)
        bgB = R("bgB", [128, 1024])
        k.dma('sp', bgB[:], bgate_d[0:1, :].partition_broadcast(128), writes=[bgB.b])
        wstE = [R("wstE0", [128, 8, 512]), R("wstE1", [128, 8, 512])]
        specs = []
        for half in range(2):
            specs += [(wout_d, half, None, wo, 8), (wg_d, half, n2t, wgt, 8), (wple_d, half, None, wpl, 2)]
        for i_, (w_d_, half, gain_, dst_, kc_) in enumerate(specs):
            st_ = wstE[i_ % 2]
            cs = slice(half * 512, (half + 1) * 512)
            k.dma('sp', st_[:, 0:kc_, :], w_d_.rearrange("(c p) n -> p c n", p=128)[:, :, cs], writes=[st_.b])
            if gain_ is not None:
                for c_ in range(kc_):
                    if c_ % 2 == 0:
                        ts('dve', dst_[:, c_, cs], st_[:, c_, :], gain_[:, c_:c_ + 1], ALU.mult, [st_.b, gain_.b], [dst_.sub((half, c_))])
                    else:
                        act(dst_[:, c_, cs], st_[:, c_, :], AF.Identity, [st_.b, gain_.b], [dst_.sub((half, c_))], scale=gain_[:, c_:c_ + 1])
            else:
                hk = kc_ // 2
                cp('dve', dst_[:, 0:hk, cs], st_[:, 0:hk, :], [st_.b], [dst_.sub((half, 0))])
                cp('act', dst_[:, hk:kc_, cs], st_[:, hk:kc_, :], [st_.b], [dst_.sub((half, 1))])

        def wall(tb_):
            return [tb_.b] + list(tb_.subs.values())
        xo = [R("xo0", [128, 1024]), R("xo1", [128, 1024])]
        pin = [R("pin0", [128, 256]), R("pin1", [128, 256])]
        pbf2 = [R("pbf0", [128, 256], BF16), R("pbf1", [128, 256], BF16)]
        pT2 = [R("pT0", [128, 2, 128], BF16), R("pT1", [128, 2, 128], BF16)]
        x1 = [R("x10", [128, 1024]), R("x11", [128, 1024])]
        junk2 = R("junk2", [128, 1024])
        xn22 = [R("xn20", [128, 1024], BF16), R("xn21", [128, 1024], BF16)]
        h2T2 = [R("h2T0", [128, 8, 128], BF16), R("h2T1", [128, 8, 128], BF16)]
        gt2 = [R("gt0", [128, 1024]), R("gt1", [128, 1024])]
        yt = [R("yt0", [128, 1024]), R("yt1", [128, 1024])]
        st2 = R("st2", [128, 4])

        def e_gen(j):
            qs = slice(j * 128, (j + 1) * 128)
            xt = xo[j % 2]; pt_ = pin[j % 2]; xx = x1[j % 2]; pbf = pbf2[j % 2]; pT = pT2[j % 2]
            xn2 = xn22[j % 2]; h2T = h2T2[j % 2]; gt = gt2[j % 2]; yy = yt[j % 2]
            so = 2 * (j % 2); sb = st2.sub(j % 2)
            k.dma('sp', xt[:], xo_d[qs, :], writes=[xt.b])
            k.dma('sp', pt_[:], po_d[qs, :], writes=[pt_.b])
            for hf in range(2):
                pb = PS[hf]
                for kc in range(8):
                    mm(pb[:, :], MIXT[:, kc, qs], wo[:, kc, hf * 512:(hf + 1) * 512],
                       [MIXT.sub(('a', j)), MIXT.sub(('b', j))] + wall(wo), [pb.b], start=(kc == 0), stop=(kc == 7))
            cp('dve', pbf[:], pt_[:], [pt_.b], [pbf.b])
            yield
            for hf in range(2):
                tt('dve', xx[:, hf * 512:(hf + 1) * 512], PS[hf][:, :], xt[:, hf * 512:(hf + 1) * 512], ALU.add, [PS[hf].b, xt.b], [xx.b])
            for c in range(2):
                tr(psb16(3)[:, c * 128:(c + 1) * 128], pbf[:, c * 128:(c + 1) * 128], identb[:], [pbf.b, identb.b], [PS[3].b])
            yield
            act(junk2[:], xx[:], AF.Square, [xx.b], [junk2.b, sb], accum_out=st2[:, so:so + 1])
            cp('act', pT[:], psb16(3)[:, 0:256].rearrange("p (c t) -> p c t", c=2), [PS[3].b], [pT.b])
            yield
            rstd_from_ssq(st2[:, so + 1:so + 2], st2[:, so:so + 1], 1, 1.0 / D_MODEL, [sb], [sb])
            yield
            ts('dve', xn2[:], xx[:], st2[:, so + 1:so + 2], ALU.mult, [xx.b, sb], [xn2.b])
            yield
            for fc in range(8):
                tr(psb16(2)[:, fc * 128:(fc + 1) * 128], xn2[:, fc * 128:(fc + 1) * 128], identb[:], [xn2.b, identb.b], [PS[2].b])
            yield
            cp('act', h2T[:], psb16(2).rearrange("p (c t) -> p c t", c=8), [PS[2].b], [h2T.b])
            yield
            for hf in range(2):
                pg = PS[4 + hf]
                for fc in range(8):
                    mm(pg[:, :], h2T[:, fc, :], wgt[:, fc, hf * 512:(hf + 1) * 512], [h2T.b] + wall(wgt), [pg.b],
                       start=(fc == 0), stop=(fc == 7))
            yield
            for hf in range(2):
                tt('dve', gt[:, hf * 512:(hf + 1) * 512], PS[4 + hf][:, :], bgB[:, hf * 512:(hf + 1) * 512], ALU.add, [PS[4 + hf].b, bgB.b], [gt.b])
            yield
            act(gt[:], gt[:], AF.Sigmoid, [gt.b], [gt.b])
            for hf in range(2):
                pp_ = PS[6 + hf]
                for c in range(2):
                    mm(pp_[:, :], pT[:, c, :], wpl[:, c, hf * 512:(hf + 1) * 512], [pT.b] + wall(wpl), [pp_.b],
                       start=(c == 0), stop=(c == 1))
            yield
            for hf in range(2):
                tt('dve', yy[:, hf * 512:(hf + 1) * 512], PS[6 + hf][:, :], gt[:, hf * 512:(hf + 1) * 512], ALU.mult, [PS[6 + hf].b, gt.b], [yy.b])
            yield
            tt('pool', yy[:], yy[:], xx[:], ALU.add, [yy.b, xx.b], [yy.b])
            bo = Buf()
            k.dma('pool', y_d[qs, :], yy[:], reads=[yy.b], writes=[bo])
            bout.append(bo)
            yield
        run_pipelined((e_gen(j) for j in range(NOWN)), depth=2)
        k.barrier()
    k.finish(bout)
    return nc, dbg_out, k


def _consts():
    p = np.arange(128)
    same = (p[:, None] // 64) == (p[None, :] // 64)
    ident = np.eye(128, dtype=np.float32)
    tribd = (same & (p[:, None] <= p[None, :])).astype(np.float32)
    blk = same.astype(np.float32)
    l0 = np.zeros((128, 128), np.float32); l0[0:64, :] = 1.0
    l1 = np.zeros((128, 128), np.float32); l1[64:128, :] = 1.0
    negns = np.where(same & (p[:, None] <= p[None, :]), 0.0, NEGM).astype(np.float32)
    negst = np.where(same & (p[None, :] < p[:, None]), 0.0, NEGM).astype(np.float32)
    ones = np.ones((128, 128), np.float32)
    return np.concatenate([ident, tribd, blk, l0, l1, negns, negst, ones, -ones], axis=1)


def make_in_maps(inputs):
    f = lambda a: np.ascontiguousarray(np.asarray(a, dtype=np.float32))
    x = f(inputs["x"]); p = f(inputs["p"])
    cst = _consts()
    tril = np.where(np.arange(128)[None, :] <= np.arange(128)[:, None], 0.0, BIGNEG).astype(np.float32)
    shared = {
        "w_in": f(inputs["w_in"][0]), "w_out": f(inputs["w_out"][0]), "w_ple": f(inputs["w_ple"][0]),
        "w_gate": f(inputs["w_ple_gate"][0]),
        "n1": f(inputs["attn_norm_w"][0].reshape(8, 128).T), "n2": f(inputs["ple_gate_norm_w"][0].reshape(8, 128).T),
        "cw": f(inputs["conv_w"][0].reshape(4, 12, 128).transpose(2, 1, 0).reshape(128, 48)),
        "alog": f(inputs["a_log"][0].reshape(1, 4)), "dtb": f(inputs["dt_bias"][0].reshape(1, 4)),
        "aon": f(inputs["a_out_norm_w"][0].reshape(1, 128)), "gq": f(inputs["b_q_norm_w"][0].reshape(1, 64)),
        "gk": f(inputs["b_k_norm_w"][0].reshape(1, 64)), "bgate": f(inputs["b_ple_gate"][0].reshape(1, 1024)),
        "cst": cst,
    }
    maps = []
    for c in range(8):
        b, half = c // 2, c % 2
        xb = x[b]
        own = xb.reshape(NT, 128, D_MODEL)[half::2].reshape(NOWN * 128, D_MODEL)
        po = p[0, b].reshape(NT, 128, 256)[half::2].reshape(NOWN * 128, 256)
        if half == 0:
            cm = np.concatenate([tril, np.full((128, 128), BIGNEG, np.float32)], axis=1)
        else:
            cm = np.concatenate([np.zeros((128, 128), np.float32), tril], axis=1)
        sel = np.zeros((128, 2), np.float32); sel[:, half] = 1.0
        m = dict(shared)
        m.update({"xb": np.ascontiguousarray(xb), "xo": np.ascontiguousarray(own), "po": np.ascontiguousarray(po),
                  "cm": np.ascontiguousarray(cm), "sel": sel})
        maps.append(m)
    return maps


_CACHE = {}


def kernel(**inputs):
    if "nc" not in _CACHE:
        _CACHE["nc"] = build()[0]
    nc = _CACHE["nc"]
    maps = make_in_maps(inputs)
    res = run_bass_kernel_spmd(nc, maps, core_ids=list(range(8)))
    out = np.empty((4, NT, 128, D_MODEL), np.float32)
    for c in range(8):
        b, half = c // 2, c % 2
        out[b, half::2] = np.asarray(res.results[c]["y"], dtype=np.float32).reshape(NOWN, 128, D_MODEL)
    return out.reshape(4, SEQ, D_MODEL)
```
